# Optimizing a Trainium2 kernel written in Bass

```python
import functools
import jax, jax.numpy as jnp
from jax import lax
import numpy as np

D_MODEL = 2048
BATCH = 16
SEQ = 2048
DEPTH = 4

GRID_W = 64
CTX_LEN = 256
N_MIXERS = 3
EPS = 1e-6

F_GROUPS = 4
M_HEADS = 8
M_DQK = D_MODEL // (2 * M_HEADS)
M_DV = D_MODEL // M_HEADS
M_QK = M_HEADS * M_DQK
M_V = M_HEADS * M_DV
M_CHUNK = 64
M_IN = 2 * M_QK + 3 * M_V + 4 * M_HEADS
A_HEADS = 16
A_KV = 4
A_HD = 128
A_GROUP = A_HEADS // A_KV
A_Q = A_HEADS * A_HD
A_KVW = A_KV * A_HD
A_IN = 2 * A_Q + 2 * A_KVW
A_BLOCK = 128
A_ROT_AXIS = A_HD // 2
ROPE_THETA = 10000.0

N_FNET = (DEPTH + 2) // 3
N_MLSTM = (DEPTH + 1) // 3
N_ATTN = DEPTH // 3

kernel_name = 'hybrid_fnet_mlstm_gqa_prefix_dit'


def _rmsnorm(x, g):
    x32 = x.astype(jnp.float32)
    y = x32 * lax.rsqrt(jnp.mean(x32 * x32, axis=-1, keepdims=True) + EPS)
    return (y * g.astype(jnp.float32)).astype(x.dtype)


def _fourier_mix(h):
    B, T, D = h.shape
    hg = h.astype(jnp.float32).reshape(B, T, F_GROUPS, D // F_GROUPS)
    y = jnp.fft.fft2(hg, axes=(1, 3), norm='ortho').real
    return y.reshape(B, T, D).astype(h.dtype)


def _fnet_branch(h, w_gate, w_out):
    return (_fourier_mix(h) * jax.nn.silu(h @ w_gate)) @ w_out


def _mlstm_chunk(carry, xs, emit):
    C, n, m = carry
    q, k, v, ig, lf = xs
    L = lf.shape[-1]
    b = jnp.cumsum(lf, axis=-1)
    b_end = b[..., -1]
    g = b_end[..., None] - b + ig
    m_new = jnp.maximum(b_end + m, jnp.max(g, axis=-1))
    w = jnp.exp(g - m_new[..., None])
    decay = jnp.exp(b_end + m - m_new)
    C_new = decay[..., None, None] * C + jnp.einsum('bhsd,bhse->bhde', k * w[..., None], v)
    n_new = decay[..., None] * n + jnp.einsum('bhs,bhsd->bhd', w, k)
    if not emit:
        return (C_new, n_new, m_new), None
    order = jnp.tril(jnp.ones((L, L), dtype=bool))
    dmat = jnp.where(order, b[..., :, None] - b[..., None, :] + ig[..., None, :], -jnp.inf)
    inter = b + m[..., None]
    m_t = jnp.maximum(inter, jnp.max(dmat, axis=-1))
    s = jnp.einsum('bhtd,bhsd->bhts', q, k) * jnp.exp(dmat - m_t[..., None])
    a = jnp.exp(inter - m_t)
    num = a[..., None] * jnp.einsum('bhtd,bhde->bhte', q, C) + jnp.einsum('bhts,bhse->bhte', s, v)
    den = a * jnp.einsum('bhtd,bhd->bht', q, n) + jnp.sum(s, axis=-1)
    h = num / jnp.maximum(jnp.abs(den), jnp.exp(-m_t))[..., None]
    return (C_new, n_new, m_new), h


def _mlstm_scan(q, k, v, ig, lf, state, emit):
    B, NH, T, _ = q.shape
    nc = T // M_CHUNK

    def to_chunks(a):
        a = a.reshape(a.shape[:2] + (nc, M_CHUNK) + a.shape[3:])
        return jnp.moveaxis(a, 2, 0)

    xs = (to_chunks(q), to_chunks(k), to_chunks(v), to_chunks(ig), to_chunks(lf))
    state, h = lax.scan(functools.partial(_mlstm_chunk, emit=emit), state, xs)
    if emit:
        h = jnp.moveaxis(h, 0, 2).reshape(B, NH, T, M_DV)
    return state, h


def _mlstm_bidir(q, k, v, ig_f, lf_f, ig_b, lf_b, st_f, st_b, emit):
    st_f, h_f = _mlstm_scan(q, k, v, ig_f, lf_f, st_f, emit)
    flip = lambda a: jnp.flip(a, axis=2)
    st_b, h_b = _mlstm_scan(flip(q), flip(k), flip(v), flip(ig_b), flip(lf_b), st_b, emit)
    h = (h_f + flip(h_b)) if emit else None
    return st_f, st_b, h


def _mlstm_branch(h_lat, h_ctx, w_in, b_gate, hn, w_out, ctx_out):
    f32 = jnp.float32
    idx = [M_QK, 2 * M_QK, 2 * M_QK + M_V, 2 * M_QK + 2 * M_V, 2 * M_QK + 2 * M_V + 4 * M_HEADS]

    def project(h):
        B, T, _ = h.shape
        q, k, v, o, g, z = jnp.split(h @ w_in, idx, axis=-1)
        heads = lambda a, d: a.reshape(B, T, M_HEADS, d).transpose(0, 2, 1, 3).astype(f32)
        q = heads(q, M_DQK) * (M_DQK ** -0.5)
        k = heads(k, M_DQK)
        v = heads(v, M_DV)
        g = (g.astype(f32) + b_gate.astype(f32)).reshape(B, T, 4, M_HEADS).transpose(2, 0, 3, 1)
        ig_f, fg_f, ig_b, fg_b = g[0], g[1], g[2], g[3]
        scan_in = (q, k, v, ig_f, jax.nn.log_sigmoid(fg_f), ig_b, jax.nn.log_sigmoid(fg_b))
        return scan_in, o, z

    def finish(h, o, z):
        B, _, T, _ = h.shape
        h = h.transpose(0, 2, 1, 3)
        y = jax.nn.sigmoid(o.astype(f32)).reshape(B, T, M_HEADS, M_DV) * h
        y = _rmsnorm(y, hn.reshape(M_HEADS, M_DV)).reshape(B, T, M_V).astype(z.dtype)
        return (y * jax.nn.silu(z)) @ w_out

    B = h_ctx.shape[0]
    zero = (jnp.zeros((B, M_HEADS, M_DQK, M_DV), f32), jnp.zeros((B, M_HEADS, M_DQK), f32),
            jnp.zeros((B, M_HEADS), f32))
    ctx_in, o_c, z_c = project(h_ctx)
    st_f, st_b, h_c = _mlstm_bidir(*ctx_in, zero, zero, emit=ctx_out)
    lat_in, o_x, z_x = project(h_lat)
    _, _, h_x = _mlstm_bidir(*lat_in, st_f, st_b, emit=True)
    y_lat = finish(h_x, o_x, z_x)
    y_ctx = finish(h_c, o_c, z_c) if ctx_out else None
    return y_lat, y_ctx


def _rope_tables(T):
    rows = T // GRID_W
    r = jnp.repeat(jnp.arange(rows), GRID_W).astype(jnp.float32)
    col = jnp.tile(jnp.arange(GRID_W), rows).astype(jnp.float32)
    freqs = ROPE_THETA ** (-jnp.arange(0, A_ROT_AXIS, 2, dtype=jnp.float32) / A_ROT_AXIS)
    ang = jnp.concatenate([r[:, None] * freqs, col[:, None] * freqs], axis=-1)
    return jnp.cos(ang), jnp.sin(ang)


def _rope(x, cos, sin):
    x32 = x.astype(jnp.float32).reshape(x.shape[:-1] + (A_HD // 2, 2))
    x0, x1 = x32[..., 0], x32[..., 1]
    c = cos[None, :, None, :]
    s = sin[None, :, None, :]
    out = jnp.stack([x0 * c - x1 * s, x0 * s + x1 * c], axis=-1).reshape(x.shape)
    return out.astype(x.dtype)


def _attend(q, k, v):
    B, Tq = q.shape[:2]
    nb = Tq // A_BLOCK
    qb = jnp.moveaxis(q.reshape((B, nb, A_BLOCK) + q.shape[2:]), 1, 0)

    def one(qblk):
        s = jnp.einsum('bqkgd,bskd->bkgqs', qblk, k).astype(jnp.float32) * (A_HD ** -0.5)
        p = jax.nn.softmax(s, axis=-1).astype(v.dtype)
        return jnp.einsum('bkgqs,bskd->bqkgd', p, v)

    o = lax.map(one, qb)
    return jnp.moveaxis(o, 0, 1).reshape(B, Tq, A_Q)


def _attn_branch(h_lat, h_ctx, w_in, qn, kn, w_out, ctx_out):
    def project(h):
        B, T, _ = h.shape
        q, k, v, z = jnp.split(h @ w_in, [A_Q, A_Q + A_KVW, A_Q + 2 * A_KVW], axis=-1)
        q = _rmsnorm(q.reshape(B, T, A_HEADS, A_HD), qn)
        k = _rmsnorm(k.reshape(B, T, A_KV, A_HD), kn)
        v = v.reshape(B, T, A_KV, A_HD)
        return q, k, v, z

    group = lambda q: q.reshape(q.shape[:2] + (A_KV, A_GROUP, A_HD))
    qc, kc, vc, zc = project(h_ctx)
    qx, kx, vx, zx = project(h_lat)
    cos, sin = _rope_tables(h_lat.shape[1])
    qx = _rope(qx, cos, sin)
    kx = _rope(kx, cos, sin)
    k_all = jnp.concatenate([kx, kc], axis=1)
    v_all = jnp.concatenate([vx, vc], axis=1)
    y_lat = (_attend(group(qx), k_all, v_all) * jax.nn.silu(zx)) @ w_out
    y_ctx = ((_attend(group(qc), kc, vc) * jax.nn.silu(zc)) @ w_out) if ctx_out else None
    return y_lat, y_ctx


def setup_inputs(seed: int = 0) -> dict:
    key = jax.random.key(seed)
    ks = jax.random.split(key, 20)
    D = D_MODEL
    nrm = lambda k, shape, s: jax.random.normal(k, shape, jnp.float32) * s
    f_bias = jnp.linspace(3.0, 6.0, M_HEADS, dtype=jnp.float32)
    base = jnp.stack([jnp.zeros_like(f_bias), f_bias, jnp.zeros_like(f_bias), f_bias])
    b_gate = (base[None] + nrm(ks[11], (N_MLSTM, 4, M_HEADS), 0.1)).reshape(N_MLSTM, 4 * M_HEADS)
    return {
        'x': nrm(ks[0], (BATCH, SEQ, D), 1.0),
        'c': nrm(ks[1], (BATCH, D), 1.0),
        'ctx': nrm(ks[2], (BATCH, CTX_LEN, D), 1.0),
        'c_ctx': nrm(ks[3], (D,), 1.0),
        'ada_w': nrm(ks[4], (DEPTH, D, 3 * D), 0.5 * D ** -0.5),
        'ada_b': nrm(ks[5], (DEPTH, 3 * D), 0.01),
        'norm_g': 1.0 + nrm(ks[6], (DEPTH, D), 0.02),
        'fnet_w_gate': nrm(ks[7], (N_FNET, D, D), D ** -0.5),
        'fnet_w_out': nrm(ks[8], (N_FNET, D, D), D ** -0.5),
        'mlstm_w_in': nrm(ks[9], (N_MLSTM, D, M_IN), D ** -0.5),
        'mlstm_b_gate': b_gate,
        'mlstm_hn': 1.0 + nrm(ks[12], (N_MLSTM, M_V), 0.02),
        'mlstm_w_out': nrm(ks[13], (N_MLSTM, M_V, D), M_V ** -0.5),
        'attn_w_in': nrm(ks[14], (N_ATTN, D, A_IN), D ** -0.5),
        'attn_qn': 1.0 + nrm(ks[15], (N_ATTN, A_HD), 0.02),
        'attn_kn': 1.0 + nrm(ks[16], (N_ATTN, A_HD), 0.02),
        'attn_w_out': nrm(ks[17], (N_ATTN, A_Q, D), A_Q ** -0.5),
        'final_g': 1.0 + nrm(ks[18], (D,), 0.02),
    }


def reference(x, c, ctx, c_ctx, ada_w, ada_b, norm_g, fnet_w_gate, fnet_w_out, mlstm_w_in,
              mlstm_b_gate, mlstm_hn, mlstm_w_out, attn_w_in, attn_qn, attn_kn, attn_w_out, final_g):
    sc = jax.nn.silu(c)
    sc_ctx = jax.nn.silu(c_ctx)
    for i in range(DEPTH):
        kind = i % N_MIXERS
        j = i // N_MIXERS
        ctx_out = i != DEPTH - 1
        shift, scale, gate = jnp.split(sc @ ada_w[i] + ada_b[i], 3, axis=-1)
        hx = _rmsnorm(x, norm_g[i]) * (1.0 + scale[:, None]) + shift[:, None]
        hc = None
        if ctx_out or kind != 0:
            shift_c, scale_c, gate_c = jnp.split(sc_ctx @ ada_w[i] + ada_b[i], 3, axis=-1)
            hc = _rmsnorm(ctx, norm_g[i]) * (1.0 + scale_c) + shift_c
        if kind == 0:
            y_x = _fnet_branch(hx, fnet_w_gate[j], fnet_w_out[j])
            y_c = _fnet_branch(hc, fnet_w_gate[j], fnet_w_out[j]) if ctx_out else None
        elif kind == 1:
            y_x, y_c = _mlstm_branch(hx, hc, mlstm_w_in[j], mlstm_b_gate[j], mlstm_hn[j],
                                     mlstm_w_out[j], ctx_out)
        else:
            y_x, y_c = _attn_branch(hx, hc, attn_w_in[j], attn_qn[j], attn_kn[j],
                                    attn_w_out[j], ctx_out)
        x = x + gate[:, None] * y_x
        if ctx_out:
            ctx = ctx + gate_c * y_c
    return _rmsnorm(x, final_g)
```

```python
import math
import numpy as np
import ml_dtypes
from contextlib import ExitStack
import concourse.bass as bass
import concourse.mybir as mybir
from concourse.bass_utils import run_bass_kernel_spmd

F32 = mybir.dt.float32
BF16 = mybir.dt.bfloat16
AF = mybir.ActivationFunctionType
ALU = mybir.AluOpType
AX = mybir.AxisListType
NPBF = ml_dtypes.bfloat16

D = 2048
TL = 2048
TC = 256
TA = TL + TC
NB = 2
EPS = 1e-6
ENGS = ("sp", "act", "dve", "pool", "pe")


class Slot:
    __slots__ = ("sem", "cnt")

    def __init__(self):
        self.sem = None
        self.cnt = 0


class DT:
    __slots__ = ("name", "writers", "readers", "multi", "slot", "last_dma")

    def __init__(self, name, multi=False, fence=()):
        self.name = name
        self.writers = list(fence)
        self.readers = []
        self.multi = multi
        self.slot = None
        self.last_dma = None


class Op:
    __slots__ = ("eng", "fn", "deps", "is_dma", "sig", "need")

    def __init__(self, eng, fn, is_dma):
        self.eng = eng
        self.fn = fn
        self.deps = []
        self.is_dma = is_dma
        self.sig = None
        self.need = False


class Prog:
    def __init__(self, nc):
        self.nc = nc
        self.ops = {e: [] for e in ENGS}
        self.slots = []
        self.free_slots = []
        self.fence = []
        self.live = []

    def tile(self, name, multi=False, phase=True):
        t = DT(name, multi, self.fence)
        if phase:
            self.live.append(t)
        return t

    def _track(self, op, reads, writes):
        deps = op.deps
        for t in reads:
            deps.extend(t.writers)
            t.readers.append(op)
        for t in writes:
            if t.multi:
                if t.readers:
                    deps.extend(t.readers)
                    t.readers = []
                    t.writers = [op]
                else:
                    t.writers.append(op)
            else:
                deps.extend(t.readers)
                deps.extend(t.writers)
                t.readers = []
                t.writers = [op]

    def op(self, eng, fn, reads=(), writes=()):
        o = Op(eng, fn, False)
        self._track(o, reads, writes)
        o.deps = [d for d in o.deps if d is not o]
        self.ops[eng].append(o)
        return o

    def dma(self, eng, fn, n, st, reads=(), writes=()):
        o = Op(eng, fn, True)
        self._track(o, reads, writes)
        o.deps = [d for d in o.deps if d is not o]
        if st.last_dma is not None:
            o.deps.append(st.last_dma)
        st.last_dma = o
        if st.slot is None:
            if self.free_slots:
                st.slot = self.free_slots.pop()
            else:
                st.slot = Slot()
                self.slots.append(st.slot)
        st.slot.cnt += 16 * n
        o.sig = (st.slot, st.slot.cnt)
        self.ops[eng].append(o)
        return o

    def barrier(self, fn):
        o = self.op("dve", fn, writes=self.live)
        for t in self.live:
            if t.slot is not None:
                self.free_slots.append(t.slot)
        self.live = []
        self.fence = [o]
        return o

    def emit(self, stack):
        nc = self.nc
        for e in ENGS:
            for o in self.ops[e]:
                for d in o.deps:
                    if not d.is_dma:
                        if d.eng == "pe" and o.eng == "pe" and not o.is_dma:
                            continue
                        d.need = True
        engsem = {}
        for e in ENGS:
            if e != "sp":
                engsem[e] = stack.enter_context(nc.semaphore("eng_" + e))
        for i, t in enumerate(self.slots):
            t.sem = stack.enter_context(nc.semaphore("dsl%d" % i))
        for e in ENGS:
            c = 0
            for o in self.ops[e]:
                if o.is_dma:
                    continue
                if o.need:
                    c += 1
                    o.sig = (e, c)
        block = stack.enter_context(nc.Block())
        prog = self

        def run(e, eng):
            waited = {}
            for o in prog.ops[e]:
                req = {}
                for d in o.deps:
                    if d.sig is None:
                        continue
                    if (not d.is_dma) and d.eng == "pe" and e == "pe" and not o.is_dma:
                        continue
                    k, v = d.sig
                    if req.get(k, 0) < v:
                        req[k] = v
                for k, v in req.items():
                    if waited.get(k, 0) >= v:
                        continue
                    waited[k] = v
                    sem = engsem[k] if isinstance(k, str) else k.sem
                    eng.wait_ge(sem, v)
                if o.is_dma:
                    o.fn(eng, o.sig[0].sem)
                else:
                    ins = o.fn(eng)
                    if o.need:
                        ins.then_inc(engsem[e], 1)

        @block.sync
        def _(eng):
            run("sp", eng)

        @block.scalar
        def _(eng):
            run("act", eng)

        @block.vector
        def _(eng):
            run("dve", eng)

        @block.gpsimd
        def _(eng):
            run("pool", eng)

        @block.tensor
        def _(eng):
            run("pe", eng)


class Arena:
    def __init__(self, ap, P, base=0):
        self.ap = ap
        self.P = P
        self.off = base
        self.base = base
        self.cap = ap.shape[1]

    def reset(self):
        self.off = self.base

    def alloc(self, name, shape, dtype, phase=True):
        n = int(np.prod(shape))
        words = n if dtype == F32 else (n + 1) // 2
        assert self.off + words <= self.cap, (name, self.off, words, self.cap)
        v = self.ap[:, self.off:self.off + words]
        if dtype != F32:
            v = v.bitcast(dtype)
            if n % 2:
                v = v[:, 0:n]
        self.off += words
        if len(shape) == 2:
            v = v.rearrange("p (a b) -> p a b", a=shape[0])
        elif len(shape) == 3:
            v = v.rearrange("p (a b c) -> p a b c", a=shape[0], b=shape[1])
        elif len(shape) == 4:
            v = v.rearrange("p (a b c d) -> p a b c d", a=shape[0], b=shape[1], c=shape[2])
        return v, self.P.tile(name, phase=phase)


class Rot:
    def __init__(self, items):
        self.items = items
        self.i = 0

    def next(self):
        r = self.items[self.i % len(self.items)]
        self.i += 1
        return r


def make_consts():
    c = {}
    p = np.arange(128)
    d = (np.arange(4)[None, :] * 128 + p[:, None]).astype(np.float64)
    e = np.arange(512, dtype=np.float64)
    ang = 2 * np.pi * d[:, :, None] * e[None, None, :] / 512.0
    c["CD"] = np.cos(ang).astype(NPBF)
    c["SD"] = np.sin(ang).astype(NPBF)
    t = (256 * np.arange(8)[None, None, :] + 2 * p[:, None, None] + np.arange(2)[None, :, None]).astype(np.float64)
    tt = np.arange(1024, dtype=np.float64)
    tm = np.mod(t[:, :, :, None] * tt[None, None, None, :], 2048.0)
    ang = 2 * np.pi * tm / 2048.0
    c["CT"] = np.cos(ang).astype(NPBF)
    c["STn"] = (-np.sin(ang)).astype(NPBF)
    t = (128 * np.arange(2)[None, :] + p[:, None]).astype(np.float64)
    tt = np.arange(256, dtype=np.float64)
    ang = 2 * np.pi * np.mod(t[:, :, None] * tt[None, None, :], 256.0) / 256.0
    c["C256"] = np.cos(ang).astype(NPBF)
    c["S256n"] = (-np.sin(ang)).astype(NPBF)
    rows = TL // 64
    r = np.repeat(np.arange(rows), 64).astype(np.float32)
    col = np.tile(np.arange(64), rows).astype(np.float32)
    freqs = (np.float32(10000.0) ** (-np.arange(0, 64, 2, dtype=np.float32) / np.float32(64))).astype(np.float32)
    angt = np.concatenate([r[:, None] * freqs, col[:, None] * freqs], axis=-1).astype(np.float32)
    cosT = np.repeat(np.cos(angt).T, 2, axis=0)
    sinT = np.repeat(np.sin(angt).T, 2, axis=0)
    c["ropec"] = np.ascontiguousarray(cosT).astype(np.float32)
    c["ropes"] = np.ascontiguousarray(sinT).astype(np.float32)
    ident = np.eye(128, dtype=np.float32)
    Rm = np.zeros((128, 128), np.float32)
    for i in range(64):
        Rm[2 * i + 1, 2 * i] = -1.0
        Rm[2 * i, 2 * i + 1] = 1.0
    s = np.arange(128)[:, None]
    tq = np.arange(128)[None, :]
    mf = (s <= tq).astype(np.float32)
    mb = (s >= tq).astype(np.float32)
    c["m128"] = np.stack([ident, Rm, mf, mb], axis=1).astype(NPBF)
    c["f128"] = np.stack([(s > tq).astype(np.float32), (s < tq).astype(np.float32), np.ones((128, 128), np.float32)], axis=1)
    return c


CONST_SPECS = [("CD", [128, 4, 512], BF16), ("SD", [128, 4, 512], BF16), ("CT", [128, 2, 8, 1024], BF16),
               ("STn", [128, 2, 8, 1024], BF16), ("C256", [128, 2, 256], BF16), ("S256n", [128, 2, 256], BF16),
               ("ropec", [128, 2048], F32), ("ropes", [128, 2048], F32), ("m128", [128, 4, 128], BF16),
               ("f128", [128, 3, 128], F32)]


def build(nlayers=4, dump=()):
    nc = bass.Bass("TRN2", target_bir_lowering=False)

    def din(name, shape, dt=F32):
        return nc.dram_tensor(name, shape, dt, kind="ExternalInput").ap()

    kinds = set(l % 3 for l in range(nlayers))
    need = {"xr": nlayers > 0, "hT": nlayers > 0, "uT": nlayers > 0, "sgT": bool(kinds & {0, 2}),
            "Pd": 0 in kinds, "Qd": 0 in kinds, "qTd": 2 in kinds, "kTd": 2 in kinds, "vvd": 2 in kinds}

    def dsc(name, shape, dt):
        if not need.get(name, 1 in kinds):
            shape = [1] * (len(shape) - 1) + [16]
        if name in dump:
            return nc.dram_tensor(name, shape, dt, kind="ExternalOutput").ap()
        return nc.dram_tensor(name, shape, dt).ap()

    xT = din("xT", [NB, 16, 128, TA])
    cT = din("cT", [128, 16, 3])
    ada_w = din("ada_w", [4, 2048, 6144])
    ada_b = din("ada_b", [4, 6144])
    gT = din("gT", [128, 5, 16])
    fwg = din("fnet_w_gate", [2, 2048, 2048])
    fwo = din("fnet_w_out", [2, 2048, 2048])
    mwi = din("mlstm_w_in", [2048, 8224])
    mbg = din("mlstm_b_gate", [1, 32])
    mhn = din("mlstm_hn", [1, 2048])
    mwo = din("mlstm_w_out", [2048, 2048])
    awi = din("attn_w_in", [2048, 5120])
    aqk = din("attn_qk", [128, 2])
    awo = din("attn_w_out", [2048, 2048])
    CN = {n: din("c_" + n, s, dt) for n, s, dt in CONST_SPECS}
    outT = nc.dram_tensor("outT", [NB, 16, 128, TL], F32, kind="ExternalOutput").ap()

    xr = dsc("xr", [NB, 16, 128, TA], F32)
    hT = dsc("hT", [NB, 16, 128, TA], BF16)
    uT = dsc("uT", [NB, 16, 128, TA], BF16)
    sgT = dsc("sgT", [NB, 16, 128, TA], BF16)
    Pd = dsc("Pd", [NB, 16, TA, 128], BF16)
    Qd = dsc("Qd", [NB, 16, TA, 128], BF16)
    qTd = dsc("qTd", [NB, 16, 128, TA], BF16)
    kTd = dsc("kTd", [NB, 4, 128, TA], BF16)
    vvd = dsc("vvd", [NB, TA, 512], BF16)
    mqT = dsc("mqT", [NB, 8, 128, TA], BF16)
    mkT = dsc("mkT", [NB, 8, 128, TA], BF16)
    mkd = dsc("mkd", [NB, TA, 1024], BF16)
    mvd = dsc("mvd", [NB, TA, 2048], BF16)
    mso = dsc("mso", [NB, TA, 2048], BF16)
    msz = dsc("msz", [NB, TA, 2048], BF16)
    mgd = dsc("mgd", [NB, TA, 32], F32)
    mhd = dsc("mhd", [NB, 2, TA, 2048], F32)

    st = ExitStack()
    with st:
        P = Prog(nc)
        sb = st.enter_context(nc.sbuf_tensor("arena", [128, 51200], F32))
        PA = Arena(sb[:], P)
        banks = []
        for i in range(8):
            pt = st.enter_context(nc.psum_tensor("ps%d" % i, [128, 512], F32))
            banks.append((pt[:], P.tile("psb%d" % i, phase=False)))

        def dtile(name):
            return [P.tile(name + str(b), multi=True, phase=False) for b in range(NB)]
        d_x, d_h, d_u, d_sg, d_P, d_Q = dtile("x"), dtile("h"), dtile("u"), dtile("sg"), dtile("P"), dtile("Q")
        d_q, d_k, d_v = dtile("q"), dtile("k"), dtile("v")
        d_mq, d_mkT, d_mk, d_mv, d_mso, d_msz, d_mg, d_mh = (dtile("mq"), dtile("mkT"), dtile("mk"), dtile("mv"),
                                                            dtile("mso"), dtile("msz"), dtile("mg"), dtile("mh"))
        d_out = P.tile("out", multi=True, phase=False)

        m128, t_m128 = PA.alloc("m128", [4, 128], BF16, phase=False)
        f128, t_f128 = PA.alloc("f128", [3, 128], F32, phase=False)
        onesb, t_onesb = PA.alloc("onesb", [128], BF16, phase=False)
        scT, t_scT = PA.alloc("scT", [16, 3], BF16, phase=False)
        cTs, t_cTs = PA.alloc("cTs", [16, 3], F32, phase=False)
        gTs, t_gTs = PA.alloc("gTs", [5, 16], F32, phase=False)
        mod, t_mod = PA.alloc("mod", [4, 48, 3], F32, phase=False)
        amod, t_amod = PA.alloc("amod", [4, 16, 3], F32, phase=False)
        qks, t_qks = PA.alloc("qks", [2], F32, phase=False)
        scr, t_scr = PA.alloc("scr", [4], F32, phase=False)
        epsc, t_eps = PA.alloc("epsc", [1], F32, phase=False)
        onec, t_onec = PA.alloc("onec", [1], F32, phase=False)
        PA.base = PA.off
        A = PA
        ident = m128[:, 0, :]
        Rm = m128[:, 1, :]
        maskd = [m128[:, 2, :], m128[:, 3, :]]
        trid = [f128[:, 0, :], f128[:, 1, :]]
        onesf = f128[:, 2, :]

        def ld(eng, dst, src, tl, reads=(), n=1):
            def f(e, s):
                e.dma_start(out=dst, in_=src).then_inc(s, 16)
            P.dma(eng, f, 1, tl, reads=reads, writes=[tl])

        def stq(eng, dst, src, tl, dtiles):
            def f(e, s):
                e.dma_start(out=dst, in_=src).then_inc(s, 16)
            P.dma(eng, f, 1, tl, reads=[tl], writes=dtiles)

        ld("sp", m128, CN["m128"], t_m128)
        ld("sp", f128, CN["f128"], t_f128)
        ld("sp", cTs, cT, t_cTs)
        ld("sp", gTs, gT, t_gTs)
        ld("sp", qks, aqk, t_qks)
        P.op("dve", lambda e: e.memset(onesb, 1.0), writes=[t_onesb])
        P.op("dve", lambda e: e.memset(epsc, EPS), writes=[t_eps])
        P.op("dve", lambda e: e.memset(onec, 1.0), writes=[t_onec])
        P.op("act", lambda e: e.activation(out=scT, in_=cTs, func=AF.Silu), reads=[t_cTs], writes=[t_scT])

        def phase_end():
            P.barrier(lambda e: e.memset(scr, 0.0))
            A.reset()

        def mm16(out, lhs_fn, rhs_fn, nk=16):
            def f(e):
                for k in range(nk):
                    ins = e.matmul(out, lhsT=lhs_fn(k), rhs=rhs_fn(k), start=(k == 0), stop=(k == nk - 1))
                return ins
            return f

        def mod_phase():
            wbs = [A.alloc("mw%d" % i, [16, 512], BF16) for i in range(2)]
            adb, t_adb = A.alloc("adb", [6144], BF16)
            psm, t_psm = banks[0]
            for l in range(nlayers):
                def f(e, s, l=l):
                    e.dma_start(out=adb[0:1, :], in_=ada_b[l:l + 1, :]).then_inc(s, 16)
                P.dma("pool", f, 1, t_adb, writes=[t_adb])
                for nb in range(12):
                    wv, wt = wbs[nb % 2]
                    ld("pool", wv, ada_w[l, :, nb * 512:(nb + 1) * 512].rearrange("(k p) n -> p k n", p=128), wt)

                    def f(e, nb=nb, wv=wv):
                        for j in range(4):
                            nt_ = nb * 4 + j
                            o = psm[:, nt_ * 3:nt_ * 3 + 3]
                            for k in range(16):
                                e.matmul(o, lhsT=wv[:, k, j * 128:(j + 1) * 128], rhs=scT[:, k, :], start=(k == 0), stop=False)
                            ins = e.matmul(o, lhsT=adb[0:1, nt_ * 128:(nt_ + 1) * 128], rhs=onesb[0:1, 0:3], start=False, stop=True)
                        return ins
                    P.op("pe", f, reads=[wt, t_scT, t_adb, t_onesb], writes=[t_psm])
                P.op("act", lambda e, l=l: e.activation(out=mod[:, l, :, :], in_=psm[:, 0:144].rearrange("p (a b) -> p a b", a=48), func=AF.Identity),
                     reads=[t_psm], writes=[t_mod])
                P.op("dve", lambda e, l=l: e.tensor_scalar_add(out=amod[:, l, :, :], in0=mod[:, l, 16:32, :], scalar1=1.0),
                     reads=[t_mod], writes=[t_amod])
                P.op("dve", lambda e, l=l: e.tensor_tensor(out=amod[:, l, :, :], in0=amod[:, l, :, :],
                                                          in1=gTs[:, l, :].unsqueeze(2).to_broadcast([128, 16, 3]), op=ALU.mult),
                     reads=[t_gTs, t_amod], writes=[t_amod])
            phase_end()

        def norm_phase(l, blocks, xsrc, final=False):
            xs = [A.alloc("nx%d" % i, [16, 512], F32) for i in range(2)]
            sq, t_sq = A.alloc("nsq", [16, 512], BF16)
            tt, t_tt = A.alloc("ntt", [16, 512], F32)
            hb = [A.alloc("nh%d" % i, [16, 512], F32 if final else BF16) for i in range(1 if final else 2)]
            rs, t_rs = A.alloc("nrs", [512], F32)
            psn, t_psn = banks[1]

            def load(i):
                b, t0, nt = blocks[i]
                xv, xt = xs[i % 2]
                ld("sp", xv[:, :, :nt], xsrc[b, :, :, t0:t0 + nt].rearrange("i p t -> p i t"), xt, reads=[d_x[b]])
            load(0)
            for i, (b, t0, nt) in enumerate(blocks):
                if i + 1 < len(blocks):
                    load(i + 1)
                c = 2 if t0 >= TL else b
                xv, xt = xs[i % 2]
                hv, ht = hb[i % len(hb)]
                P.op("act", lambda e, xv=xv, nt=nt: e.activation(out=sq[:, :, :nt], in_=xv[:, :, :nt], func=AF.Square),
                     reads=[xt], writes=[t_sq])
                P.op("pe", mm16(psn[:, :nt], lambda k: onesb, lambda k, nt=nt: sq[:, k, :nt]), reads=[t_sq, t_onesb], writes=[t_psn])
                P.op("act", lambda e, nt=nt: e.activation(out=rs[:, :nt], in_=psn[:, :nt], func=AF.Ln, scale=1.0 / D, bias=epsc),
                     reads=[t_psn, t_eps], writes=[t_rs])
                P.op("act", lambda e, nt=nt: e.activation(out=rs[:, :nt], in_=rs[:, :nt], func=AF.Exp, scale=-0.5),
                     reads=[t_rs], writes=[t_rs])
                P.op("dve", lambda e, xv=xv, nt=nt: e.tensor_tensor(out=tt[:, :, :nt], in0=xv[:, :, :nt],
                                                                  in1=rs[:, :nt].unsqueeze(1).to_broadcast([128, 16, nt]), op=ALU.mult),
                     reads=[xt, t_rs], writes=[t_tt])

                def f(e, hv=hv, nt=nt, c=c):
                    for i_ in range(16):
                        if final:
                            ins = e.activation(out=hv[:, i_, :nt], in_=tt[:, i_, :nt], func=AF.Identity, scale=gTs[:, 4, i_:i_ + 1])
                        else:
                            ins = e.activation(out=hv[:, i_, :nt], in_=tt[:, i_, :nt], func=AF.Identity,
                                               bias=mod[:, l, i_, c:c + 1], scale=amod[:, l, i_, c:c + 1])
                    return ins
                P.op("act", f, reads=[t_tt, t_mod, t_amod, t_gTs], writes=[ht])
                if final:
                    stq("sp", outT[b, :, :, t0:t0 + nt].rearrange("i p t -> p i t"), hv[:, :, :nt], ht, [d_out])
                else:
                    stq("sp", hT[b, :, :, t0:t0 + nt].rearrange("i p t -> p i t"), hv[:, :, :nt], ht, [d_h[b]])
            phase_end()

        def linear(src, d_src, wsrc, ncols, mode, blocks, evac, psb, pre=None, gsz=1024):
            wb = [A.alloc("lw%d" % i, [16, gsz], BF16) for i in range(2)]
            hb = [A.alloc("lh%d" % i, [16, 512], BF16) for i in range(2)]
            ng = (ncols + gsz - 1) // gsz
            items = [(g, b, t0, nt) for g in range(ng) for (b, t0, nt) in blocks]

            def load(i):
                g, b, t0, nt = items[i]
                hv, ht = hb[i % 2]
                ld("sp", hv[:, :, :nt], src[b, :, :, t0:t0 + nt].rearrange("i p t -> p i t"), ht, reads=[d_src[b]])
                if pre is not None:
                    pre(i, g, b, t0, nt)
            load(0)
            lastg = -1
            for i, (g, b, t0, nt) in enumerate(items):
                c0 = g * gsz
                gc = min(gsz, ncols - c0)
                wv, wt = wb[g % 2]
                if g != lastg:
                    ld("pool", wv[:, :, :gc], wsrc[:, c0:c0 + gc].rearrange("(k p) n -> p k n", p=128), wt)
                    lastg = g
                if i + 1 < len(items):
                    load(i + 1)
                hv, ht = hb[i % 2]
                if mode == "ws":
                    for j in range(gc // 128):
                        ps, pst = psb.next()
                        P.op("pe", mm16(ps[:, :nt], lambda k, j=j, wv=wv: wv[:, k, j * 128:(j + 1) * 128],
                                        lambda k, hv=hv, nt=nt: hv[:, k, :nt]), reads=[wt, ht], writes=[pst])
                        evac(i, b, t0, nt, c0 // 128 + j, j, gc // 128, ps[:, :nt], pst)
                else:
                    for tq in range(nt // 128):
                        for n0 in range(0, gc, 512):
                            n1 = min(gc, n0 + 512)
                            ps, pst = psb.next()
                            P.op("pe", mm16(ps[:, :n1 - n0], lambda k, hv=hv, tq=tq: hv[:, k, tq * 128:(tq + 1) * 128],
                                            lambda k, wv=wv, n0=n0, n1=n1: wv[:, k, n0:n1]), reads=[wt, ht], writes=[pst])
                            evac(i, b, t0 + tq * 128, c0 + n0, n1 - n0, n0, gc, ps[:, :n1 - n0], pst)

        def ws_simple(dst, d_dst, func, scale=1.0, hbase=0):
            stg = Rot([A.alloc("wss%d" % i, [8, 512], BF16) for i in range(2)])
            cur = {}

            def evac(i, b, t0, nt, jn, j, nj, ps, pst):
                if j == 0:
                    cur["s"] = stg.next()
                sv, stl = cur["s"]
                P.op("act", lambda e: e.activation(out=sv[:, j, :nt], in_=ps, func=func, scale=scale), reads=[pst], writes=[stl])
                if j == nj - 1:
                    h0 = jn - j - hbase
                    stq("sp", dst[b, h0:h0 + nj, :, t0:t0 + nt].rearrange("j p t -> p j t"), sv[:, :nj, :nt], stl, [d_dst[b]])
            return evac

        def as_simple(dst, d_dst, func, cbase, width=1024, dt=BF16, addt=None):
            stg = Rot([A.alloc("ass%d" % i, [width], dt) for i in range(2)])
            cur = {}

            def evac(i, b, tok0, col0, ncol, n0, gc, ps, pst):
                if n0 == 0:
                    cur["s"] = stg.next()
                sv, stl = cur["s"]
                if addt is not None:
                    av, at = addt
                    P.op("dve", lambda e: e.tensor_tensor(out=sv[:, n0:n0 + ncol], in0=ps, in1=av[:, n0:n0 + ncol], op=ALU.add),
                         reads=[pst, at], writes=[stl])
                else:
                    P.op("act", lambda e: e.activation(out=sv[:, n0:n0 + ncol], in_=ps, func=func), reads=[pst], writes=[stl])
                if n0 + ncol >= gc:
                    cs = col0 - n0
                    stq("sp", dst[b, tok0:tok0 + 128, cs:cs + gc], sv[:, :gc], stl, [d_dst[b]])
            return evac

        def out_phase(l, wout, blocks, xsrc):
            xb = [A.alloc("ox%d" % i, [8, 512], F32) for i in range(2)]

            def pre(i, g, b, t0, nt):
                xv, xt = xb[i % 2]
                ld("sp", xv[:, :, :nt], xsrc[b, g * 8:(g + 1) * 8, :, t0:t0 + nt].rearrange("i p t -> p i t"), xt, reads=[d_x[b]])

            def evac(i, b, t0, nt, jn, j, nj, ps, pst):
                xv, xt = xb[i % 2]
                c = 2 if t0 >= TL else b
                P.op("dve", lambda e: e.scalar_tensor_tensor(out=xv[:, j, :nt], in0=ps, scalar=mod[:, l, 32 + jn, c:c + 1],
                                                            in1=xv[:, j, :nt], op0=ALU.mult, op1=ALU.add),
                     reads=[pst, xt, t_mod], writes=[xt])
                if j == nj - 1:
                    g0 = jn - j
                    stq("sp", xr[b, g0:g0 + 8, :, t0:t0 + nt].rearrange("i p t -> p i t"), xv[:, :, :nt], xt, [d_x[b]])
            linear(uT, d_u, wout, 2048, "ws", blocks, evac, Rot(banks[0:4]), pre=pre)
            phase_end()

        def fnet_layer(l, j, blocks, with_ctx, xsrc):
            norm_phase(l, blocks, xsrc)
            linear(hT, d_h, fwg[j], 2048, "ws", blocks, ws_simple(sgT, d_sg, AF.Silu), Rot(banks[0:4]))
            phase_end()
            CDs, t_CD = A.alloc("CD", [4, 512], BF16)
            SDs, t_SD = A.alloc("SD", [4, 512], BF16)
            ld("sp", CDs, CN["CD"], t_CD)
            ld("sp", SDs, CN["SD"], t_SD)
            hb = [A.alloc("fh%d" % i, [16, 512], BF16) for i in range(2)]
            stP = Rot([A.alloc("fsp%d" % i, [2048], BF16) for i in range(2)])
            stQ = Rot([A.alloc("fsq%d" % i, [2048], BF16) for i in range(2)])
            psb = Rot(banks[0:6])

            def load(i):
                b, t0, nt = blocks[i]
                hv, ht = hb[i % 2]
                ld("sp", hv[:, :, :nt], hT[b, :, :, t0:t0 + nt].rearrange("i p t -> p i t"), ht, reads=[d_h[b]])
            load(0)
            for i, (b, t0, nt) in enumerate(blocks):
                if i + 1 < len(blocks):
                    load(i + 1)
                hv, ht = hb[i % 2]
                nrm = (1.0 / 1024.0) if t0 < TL else 1.0 / math.sqrt(256.0 * 512.0)
                for tq in range(nt // 128):
                    pv, pt = stP.next()
                    qv, qt = stQ.next()
                    for g in range(4):
                        psP, tP = psb.next()
                        P.op("pe", mm16(psP, lambda k, g=g, tq=tq, hv=hv: hv[:, 4 * g + k, tq * 128:(tq + 1) * 128],
                                        lambda k: CDs[:, k, :], nk=4), reads=[ht, t_CD], writes=[tP])
                        P.op("act", lambda e, g=g, psP=psP, pv=pv, nrm=nrm: e.activation(out=pv[:, g * 512:(g + 1) * 512], in_=psP, func=AF.Identity, scale=nrm),
                             reads=[tP], writes=[pt])
                        psQ, tQ = psb.next()
                        P.op("pe", mm16(psQ, lambda k, g=g, tq=tq, hv=hv: hv[:, 4 * g + k, tq * 128:(tq + 1) * 128],
                                        lambda k: SDs[:, k, :], nk=4), reads=[ht, t_SD], writes=[tQ])
                        P.op("dve", lambda e, g=g, psQ=psQ, qv=qv, nrm=nrm: e.tensor_scalar(out=qv[:, g * 512:(g + 1) * 512], in0=psQ, scalar1=nrm,
                                                                                 scalar2=None, op0=ALU.mult), reads=[tQ], writes=[qt])
                    tok = t0 + tq * 128
                    stq("sp", Pd[b, :, tok:tok + 128, :].rearrange("j t e -> t j e"), pv.rearrange("p (j e) -> p j e", j=16), pt, [d_P[b]])
                    stq("sp", Qd[b, :, tok:tok + 128, :].rearrange("j t e -> t j e"), qv.rearrange("p (j e) -> p j e", j=16), qt, [d_Q[b]])
            phase_end()
            CTs, t_CT = A.alloc("CT", [2, 8, 1024], BF16)
            STs, t_ST = A.alloc("ST", [2, 8, 1024], BF16)
            ld("sp", CTs, CN["CT"], t_CT)
            ld("sp", STs, CN["STn"], t_ST)
            Pe = [A.alloc("Pe%d" % i, [2, 8, 128], BF16) for i in range(2)]
            Qe = [A.alloc("Qe%d" % i, [2, 8, 128], BF16) for i in range(2)]
            sgs = [A.alloc("sgs%d" % i, [2048], BF16) for i in range(2)]
            ust = [A.alloc("ust%d" % i, [2048], BF16) for i in range(2)]
            tB, t_tB = A.alloc("tB", [512], F32)
            y1, t_y1 = A.alloc("y1", [512], F32)
            y2, t_y2 = A.alloc("y2", [512], F32)
            psA = Rot(banks[0:2])
            psBk = Rot(banks[2:4])
            nb_list = sorted(set(b for (b, _, _) in blocks))
            items = [(b, jt) for b in nb_list for jt in range(16)]

            def load2(i):
                b, jt = items[i]
                pv, pt = Pe[i % 2]
                qv, qt = Qe[i % 2]
                sv, stl = sgs[i % 2]
                for r in range(2):
                    ld("sp", pv[:, r, :, :], Pd[b, jt, 0:TL, :].rearrange("(c p r) e -> r p c e", p=128, r=2)[r], pt, reads=[d_P[b]])
                    ld("sp", qv[:, r, :, :], Qd[b, jt, 0:TL, :].rearrange("(c p r) e -> r p c e", p=128, r=2)[r], qt, reads=[d_Q[b]])
                ld("sp", sv, sgT[b, jt, :, 0:TL], stl, reads=[d_sg[b]])
            load2(0)
            for i, (b, jt) in enumerate(items):
                if i + 1 < len(items):
                    load2(i + 1)
                pv, pt = Pe[i % 2]
                qv, qt = Qe[i % 2]
                sv, stl = sgs[i % 2]
                uv, ut = ust[i % 2]
                for jb in range(2):
                    cs = slice(jb * 512, (jb + 1) * 512)
                    pa, ta = psA.next()
                    pb, tb = psBk.next()
                    for r, (pp, tp) in enumerate(((pa, ta), (pb, tb))):
                        def f(e, r=r, pp=pp, pv=pv, qv=qv, cs=cs):
                            for c in range(8):
                                e.matmul(pp, lhsT=pv[:, r, c, :], rhs=CTs[:, r, c, cs], start=(c == 0), stop=False)
                                ins = e.matmul(pp, lhsT=qv[:, r, c, :], rhs=STs[:, r, c, cs], start=False, stop=(c == 7))
                            return ins
                        P.op("pe", f, reads=[pt, qt, t_CT, t_ST], writes=[tp])
                    P.op("act", lambda e, pb=pb: e.activation(out=tB, in_=pb, func=AF.Identity), reads=[tb], writes=[t_tB])
                    P.op("dve", lambda e, pa=pa: e.tensor_tensor(out=y1, in0=pa, in1=tB, op=ALU.add), reads=[ta, t_tB], writes=[t_y1])
                    P.op("dve", lambda e, pa=pa: e.tensor_tensor(out=y2, in0=pa, in1=tB, op=ALU.subtract), reads=[ta, t_tB], writes=[t_y2])
                    P.op("pool", lambda e, uv=uv, sv=sv, cs=cs: e.tensor_tensor(out=uv[:, cs], in0=y1, in1=sv[:, cs], op=ALU.mult),
                         reads=[t_y1, stl], writes=[ut])
                    c2 = slice(1024 + jb * 512, 1024 + (jb + 1) * 512)
                    P.op("dve", lambda e, uv=uv, sv=sv, c2=c2: e.tensor_tensor(out=uv[:, c2], in0=y2, in1=sv[:, c2], op=ALU.mult),
                         reads=[t_y2, stl], writes=[ut])
                stq("sp", uT[b, jt, :, 0:TL], uv, ut, [d_u[b]])
            if with_ctx:
                C2, t_C2 = A.alloc("C2", [2, 256], BF16)
                S2, t_S2 = A.alloc("S2", [2, 256], BF16)
                ld("sp", C2, CN["C256"], t_C2)
                ld("sp", S2, CN["S256n"], t_S2)
                Pc, t_Pc = A.alloc("Pc", [16, 2, 128], BF16)
                Qc, t_Qc = A.alloc("Qc", [16, 2, 128], BF16)
                sgc, t_sgc = A.alloc("sgc", [16, 256], BF16)
                uc, t_uc = A.alloc("uc", [16, 256], BF16)
                for b in nb_list:
                    for c in range(2):
                        ld("sp", Pc[:, :, c, :], Pd[b, :, TL + c * 128:TL + (c + 1) * 128, :].rearrange("j p e -> p j e"), t_Pc, reads=[d_P[b]])
                        ld("sp", Qc[:, :, c, :], Qd[b, :, TL + c * 128:TL + (c + 1) * 128, :].rearrange("j p e -> p j e"), t_Qc, reads=[d_Q[b]])
                    ld("sp", sgc, sgT[b, :, :, TL:TA].rearrange("j p t -> p j t"), t_sgc, reads=[d_sg[b]])
                    for jt in range(16):
                        pa, ta = psA.next()

                        def f(e, pa=pa, jt=jt):
                            for c in range(2):
                                e.matmul(pa[:, 0:256], lhsT=Pc[:, jt, c, :], rhs=C2[:, c, :], start=(c == 0), stop=False)
                                ins = e.matmul(pa[:, 0:256], lhsT=Qc[:, jt, c, :], rhs=S2[:, c, :], start=False, stop=(c == 1))
                            return ins
                        P.op("pe", f, reads=[t_Pc, t_Qc, t_C2, t_S2], writes=[ta])
                        P.op("dve", lambda e, pa=pa, jt=jt: e.tensor_tensor(out=uc[:, jt, :], in0=pa[:, 0:256], in1=sgc[:, jt, :], op=ALU.mult),
                             reads=[ta, t_sgc], writes=[t_uc])
                    stq("sp", uT[b, :, :, TL:TA].rearrange("j p t -> p j t"), uc, t_uc, [d_u[b]])
            phase_end()
            out_phase(l, fwo[j], blocks, xsrc)

        def attn_layer(l, blocks, xsrc):
            norm_phase(l, blocks, xsrc)
            rc, t_rc = A.alloc("rc", [2048], F32)
            rsn, t_rsn = A.alloc("rsn", [2048], F32)
            ld("sp", rc, CN["ropec"], t_rc)
            ld("sp", rsn, CN["ropes"], t_rsn)
            stg = Rot([A.alloc("aqs%d" % i, [8, 512], BF16) for i in range(2)])
            sqh = Rot([A.alloc("asq%d" % i, [512], BF16) for i in range(2)])
            rsv = Rot([A.alloc("ars%d" % i, [512], F32) for i in range(2)])
            qnb = Rot([A.alloc("aqn%d" % i, [512], BF16) for i in range(2)])
            t1v = Rot([A.alloc("at1%d" % i, [512], F32) for i in range(2)])
            t2v = Rot([A.alloc("at2%d" % i, [512], F32) for i in range(2)])
            ps2 = Rot(banks[4:6])
            ps3 = Rot(banks[6:8])
            cur = {}

            def evac(i, b, t0, nt, jn, j, nj, ps, pst):
                if j == 0:
                    cur["s"] = stg.next()
                sv, stl = cur["s"]
                gcol = qks[:, 0:1] if jn < 16 else qks[:, 1:2]
                sqv, sqt = sqh.next()
                P.op("act", lambda e: e.activation(out=sqv[:, :nt], in_=ps, func=AF.Square), reads=[pst], writes=[sqt])
                p2, tp2 = ps2.next()
                P.op("pe", lambda e: e.matmul(p2[:, :nt], lhsT=onesb, rhs=sqv[:, :nt], start=True, stop=True), reads=[sqt, t_onesb], writes=[tp2])
                rv, rt = rsv.next()
                P.op("act", lambda e: e.activation(out=rv[:, :nt], in_=p2[:, :nt], func=AF.Ln, scale=1.0 / 128, bias=epsc),
                     reads=[tp2, t_eps], writes=[rt])
                P.op("act", lambda e: e.activation(out=rv[:, :nt], in_=rv[:, :nt], func=AF.Exp, scale=-0.5), reads=[rt], writes=[rt])
                if t0 >= TL:
                    P.op("dve", lambda e: e.scalar_tensor_tensor(out=sv[:, j, :nt], in0=ps, scalar=gcol, in1=rv[:, :nt], op0=ALU.mult, op1=ALU.mult),
                         reads=[pst, rt, t_qks], writes=[stl])
                else:
                    qv, qt = qnb.next()
                    P.op("dve", lambda e: e.scalar_tensor_tensor(out=qv[:, :nt], in0=ps, scalar=gcol, in1=rv[:, :nt], op0=ALU.mult, op1=ALU.mult),
                         reads=[pst, rt, t_qks], writes=[qt])
                    p3, tp3 = ps3.next()
                    P.op("pe", lambda e: e.matmul(p3[:, :nt], lhsT=Rm, rhs=qv[:, :nt], start=True, stop=True), reads=[qt, t_m128], writes=[tp3])
                    a1, ta1 = t1v.next()
                    a2, ta2 = t2v.next()
                    P.op("pool", lambda e: e.tensor_tensor(out=a1[:, :nt], in0=qv[:, :nt], in1=rc[:, t0:t0 + nt], op=ALU.mult),
                         reads=[qt, t_rc], writes=[ta1])
                    P.op("dve", lambda e: e.tensor_tensor(out=a2[:, :nt], in0=p3[:, :nt], in1=rsn[:, t0:t0 + nt], op=ALU.mult),
                         reads=[tp3, t_rsn], writes=[ta2])
                    P.op("pool", lambda e: e.tensor_tensor(out=sv[:, j, :nt], in0=a1[:, :nt], in1=a2[:, :nt], op=ALU.add),
                         reads=[ta1, ta2], writes=[stl])
                if j == nj - 1:
                    h0 = jn - j
                    if h0 < 16:
                        stq("sp", qTd[b, h0:h0 + nj, :, t0:t0 + nt].rearrange("j p t -> p j t"), sv[:, :nj, :nt], stl, [d_q[b]])
                    else:
                        stq("sp", kTd[b, h0 - 16:h0 - 16 + nj, :, t0:t0 + nt].rearrange("j p t -> p j t"), sv[:, :nj, :nt], stl, [d_k[b]])
            linear(hT, d_h, awi[:, 0:2560], 2560, "ws", blocks, evac, Rot(banks[0:4]))
            phase_end()
            linear(hT, d_h, awi[:, 3072:5120], 2048, "ws", blocks, ws_simple(sgT, d_sg, AF.Silu), Rot(banks[0:4]))
            phase_end()
            linear(hT, d_h, awi[:, 2560:3072], 512, "as", blocks, as_simple(vvd, d_v, AF.Identity, 0, width=512), Rot(banks[0:4]))
            phase_end()
            kTs = [A.alloc("kTs%d" % i, [TA], BF16) for i in range(2)]
            vs = [A.alloc("vs%d" % i, [18, 128], BF16) for i in range(2)]
            qs = Rot([A.alloc("qs%d" % i, [512], BF16) for i in range(2)])
            szs = Rot([A.alloc("szs%d" % i, [512], BF16) for i in range(2)])
            pTs = Rot([A.alloc("pT%d" % i, [512], BF16) for i in range(4)])
            rden, t_rden = A.alloc("rden", [512], F32)
            ot, t_ot = A.alloc("ot", [512], F32)
            usts = Rot([A.alloc("aus%d" % i, [512], BF16) for i in range(2)])
            psS = Rot(banks[0:3])
            psO = Rot(banks[3:5])
            psD = Rot(banks[5:7])
            sc = 128.0 ** -0.5

            def attn_block(b, h, t0, nt, kv_, kt_, vv_, vt_):
                chunks = list(range(18)) if t0 < TL else [16, 17]
                qv, qt = qs.next()
                zv, zt = szs.next()
                ld("sp", qv[:, :nt], qTd[b, h, :, t0:t0 + nt], qt, reads=[d_q[b]])
                ld("sp", zv[:, :nt], sgT[b, h, :, t0:t0 + nt], zt, reads=[d_sg[b]])
                po, tpo = psO.next()
                pd_, tpd = psD.next()
                n = len(chunks)
                pend = []

                def s_step(c):
                    pss, tps = psS.next()
                    pv, ptl = pTs.next()
                    P.op("pe", lambda e: e.matmul(pss[:, :nt], lhsT=kv_[:, c * 128:(c + 1) * 128], rhs=qv[:, :nt],
                                                  start=True, stop=True), reads=[kt_, qt], writes=[tps])
                    P.op("act", lambda e: e.activation(out=pv[:, :nt], in_=pss[:, :nt], func=AF.Exp, scale=sc),
                         reads=[tps], writes=[ptl])
                    pend.append((c, pv, ptl))

                def pv_step(c, pv, ptl, first, last):
                    def f(e):
                        e.matmul(po[:, :nt], lhsT=vv_[:, c, :], rhs=pv[:, :nt], start=first, stop=last)
                        return e.matmul(pd_[:, :nt], lhsT=onesb, rhs=pv[:, :nt], start=first, stop=last)
                    P.op("pe", f, reads=[vt_, ptl, t_onesb], writes=[tpo, tpd])
                for idx in range(n + 2):
                    if idx < n:
                        s_step(chunks[idx])
                    if idx >= 2:
                        c, pv, ptl = pend[idx - 2]
                        pv_step(c, pv, ptl, idx == 2, idx == n + 1)
                uv, ut = usts.next()
                P.op("dve", lambda e: e.reciprocal(out=rden[:, :nt], in_=pd_[:, :nt]), reads=[tpd], writes=[t_rden])
                P.op("dve", lambda e: e.tensor_tensor(out=ot[:, :nt], in0=po[:, :nt], in1=rden[:, :nt], op=ALU.mult),
                     reads=[tpo, t_rden], writes=[t_ot])
                P.op("dve", lambda e: e.tensor_tensor(out=uv[:, :nt], in0=ot[:, :nt], in1=zv[:, :nt], op=ALU.mult),
                     reads=[t_ot, zt], writes=[ut])
                stq("sp", uT[b, h, :, t0:t0 + nt], uv[:, :nt], ut, [d_u[b]])

            kvi = 0
            for b in sorted(set(b for (b, _, _) in blocks)):
                for kv in range(4):
                    kv_, kt_ = kTs[kvi % 2]
                    vv_, vt_ = vs[kvi % 2]
                    kvi += 1
                    ld("sp", kv_, kTd[b, kv, :, :], kt_, reads=[d_k[b]])
                    ld("sp", vv_, vvd[b, :, kv * 128:(kv + 1) * 128].rearrange("(c p) e -> p c e", p=128), vt_, reads=[d_v[b]])
                    for hh in range(4):
                        h = kv * 4 + hh
                        for (bb, t0, nt) in blocks:
                            if bb != b:
                                continue
                            attn_block(b, h, t0, nt, kv_, kt_, vv_, vt_)
            phase_end()
            out_phase(l, awo, blocks, xsrc)

        def mlstm_layer(l, blocks, xsrc):
            norm_phase(l, blocks, xsrc)
            evq = ws_simple(mqT, d_mq, AF.Identity, scale=128.0 ** -0.5)
            linear(hT, d_h, mwi[:, 0:1024], 1024, "ws", blocks, evq, Rot(banks[0:4]))
            phase_end()
            linear(hT, d_h, mwi[:, 1024:2048], 1024, "ws", blocks, ws_simple(mkT, d_mkT, AF.Identity), Rot(banks[0:4]))
            phase_end()
            linear(hT, d_h, mwi[:, 1024:2048], 1024, "as", blocks, as_simple(mkd, d_mk, AF.Identity, 1024), Rot(banks[0:4]))
            phase_end()
            linear(hT, d_h, mwi[:, 2048:4096], 2048, "as", blocks, as_simple(mvd, d_mv, AF.Identity, 2048), Rot(banks[0:4]))
            phase_end()
            linear(hT, d_h, mwi[:, 4096:6144], 2048, "as", blocks, as_simple(mso, d_mso, AF.Sigmoid, 4096), Rot(banks[0:4]))
            phase_end()
            bgb, t_bgb = A.alloc("bgb", [32], F32)
            ld("sp", bgb, mbg[0:1, :].to_broadcast([128, 32]), t_bgb)
            linear(hT, d_h, mwi[:, 6144:6176], 32, "as", blocks, as_simple(mgd, d_mg, None, 6144, width=32, dt=F32, addt=(bgb, t_bgb)),
                   Rot(banks[0:4]), gsz=32)
            phase_end()
            linear(hT, d_h, mwi[:, 6176:8224], 2048, "as", blocks, as_simple(msz, d_msz, AF.Silu, 6176), Rot(banks[0:4]))
            phase_end()
            for b in sorted(set(b for (b, _, _) in blocks)):
                mlstm_scan(b)
                mlstm_finish(b)
            out_phase(l, mwo, blocks, xsrc)

        def mlstm_scan(b):
            G, t_G = A.alloc("G", [18, 32], F32)
            ld("sp", G, mgd[b].rearrange("(c p) n -> p c n", p=128), t_G, reads=[d_mg[b]])
            E, t_E = A.alloc("E", [2, 144], F32)
            L, t_L = A.alloc("L", [2, 144], F32)
            Wt, t_W = A.alloc("Wt", [2, 18, 8], F32)
            RI, t_RI = A.alloc("RI", [2, 18, 8], F32)
            DC, t_DC = A.alloc("DC", [2, 18, 8], F32)
            pS, t_pS = banks[0]
            pT_, t_pT = banks[1]
            for d in range(2):
                P.op("act", lambda e, d=d: e.activation(out=E[:, d, :].rearrange("p (c h) -> p c h", c=18), in_=G[:, :, 16 * d + 8:16 * d + 16], func=AF.Exp, scale=-1.0),
                     reads=[t_G], writes=[t_E])
            P.op("act", lambda e: e.activation(out=L, in_=E, func=AF.Ln, bias=onec), reads=[t_E, t_onec], writes=[t_L])
            for d in range(2):
                P.op("pe", lambda e, d=d: e.matmul(pS[:, d * 144:(d + 1) * 144], lhsT=trid[d], rhs=L[:, d, :], start=True, stop=True),
                     reads=[t_L, t_f128], writes=[t_pS])
                P.op("pe", lambda e, d=d: e.matmul(pT_[:, d * 144:(d + 1) * 144], lhsT=onesf, rhs=L[:, d, :], start=True, stop=True),
                     reads=[t_L, t_f128], writes=[t_pT])
            for d in range(2):
                P.op("dve", lambda e, d=d: e.tensor_tensor(out=Wt[:, d, :, :], in0=G[:, :, 16 * d:16 * d + 8],
                                                          in1=pS[:, d * 144:(d + 1) * 144].rearrange("p (c h) -> p c h", c=18), op=ALU.subtract),
                     reads=[t_G, t_pS], writes=[t_W])
            P.op("act", lambda e: e.activation(out=Wt, in_=Wt, func=AF.Exp), reads=[t_W], writes=[t_W])
            P.op("act", lambda e: e.activation(out=RI, in_=pS[:, 0:288].rearrange("p (d c h) -> p d c h", d=2, c=18), func=AF.Exp, scale=-1.0),
                 reads=[t_pS], writes=[t_RI])
            P.op("act", lambda e: e.activation(out=DC, in_=pT_[:, 0:288].rearrange("p (d c h) -> p d c h", d=2, c=18), func=AF.Exp, scale=-1.0),
                 reads=[t_pT], writes=[t_DC])
            Cst = [[A.alloc("Cs%d%d" % (d, h), [257], F32) for h in range(8)] for d in range(2)]
            Cb = [[A.alloc("Cb%d%d" % (d, h), [258], BF16) for h in range(8)] for d in range(2)]
            qc = [[A.alloc("qc%d%d" % (d, i), [8, 128], BF16) for i in range(2)] for d in range(2)]
            kc = [[A.alloc("kc%d%d" % (d, i), [8, 128], BF16) for i in range(2)] for d in range(2)]
            kt = [[A.alloc("kt%d%d" % (d, i), [8, 128], BF16) for i in range(2)] for d in range(2)]
            ve = [[A.alloc("ve%d%d" % (d, i), [8, 258], BF16) for i in range(2)] for d in range(2)]
            kw = [A.alloc("kw%d" % d, [8, 128], BF16) for d in range(2)]
            stt = Rot([A.alloc("stt%d" % i, [128], BF16) for i in range(4)])
            fc = Rot([A.alloc("fc%d" % i, [8], F32) for i in range(4)])
            hst = [[A.alloc("hst%d%d" % (d, i), [8, 257], F32) for i in range(2)] for d in range(2)]
            hout = [A.alloc("hout%d" % d, [8, 256], F32) for d in range(2)]
            for d in range(2):
                for i in range(2):
                    vv_, vt_ = ve[d][i]
                    P.op("pool", lambda e, vv_=vv_: e.memset(vv_[:, :, 256:257], 1.0), writes=[vt_])
            order = [[16, 17] + list(range(16)), [17, 16] + list(range(15, -1, -1))]
            psSb = Rot([(banks[2][0][:, i * 128:(i + 1) * 128], None) for i in range(4)])
            psSt = banks[2][1]
            psS2 = Rot([(banks[3][0][:, i * 128:(i + 1) * 128], None) for i in range(4)])
            psS2t = banks[3][1]
            psAr = Rot(banks[4:6])
            psCr = Rot(banks[6:8])

            def loads(d, si):
                c = order[d][si]
                tk = slice(c * 128, (c + 1) * 128)
                i = si % 2
                ld("sp", qc[d][i][0], mqT[b, :, :, tk].rearrange("h p t -> p h t"), qc[d][i][1], reads=[d_mq[b]])
                ld("sp", kc[d][i][0], mkT[b, :, :, tk].rearrange("h p t -> p h t"), kc[d][i][1], reads=[d_mkT[b]])
                ld("sp", kt[d][i][0], mkd[b, tk, :].rearrange("t (h e) -> t h e", h=8), kt[d][i][1], reads=[d_mk[b]])
                ld("sp", ve[d][i][0][:, :, 0:256], mvd[b, tk, :].rearrange("t (h e) -> t h e", h=8), ve[d][i][1], reads=[d_mv[b]])
            for d in range(2):
                loads(d, 0)
            for si in range(18):
                for d in range(2):
                    if si + 1 < 18:
                        loads(d, si + 1)
                    c = order[d][si]
                    i = si % 2
                    first = si == 0
                    last = si == 17
                    cn = order[d][si + 1] if not last else None
                    qv, qt = qc[d][i]
                    kv_, kt_ = kc[d][i]
                    ktv, ktt = kt[d][i]
                    vv_, vt_ = ve[d][i]
                    kwv, kwt = kw[d]
                    hv, ht = hst[d][i]
                    P.op("dve", lambda e, d=d, c=c, kwv=kwv, ktv=ktv: e.tensor_tensor(
                        out=kwv, in0=ktv, in1=Wt[:, d, c, :].unsqueeze(2).to_broadcast([128, 8, 128]), op=ALU.mult),
                        reads=[ktt, t_W], writes=[kwt])
                    for h in range(8):
                        if h < 4:
                            pss, tps = psSb.next()[0], psSt
                        else:
                            pss, tps = psS2.next()[0], psS2t
                        P.op("pe", lambda e, pss=pss, kv_=kv_, qv=qv, h=h: e.matmul(pss, lhsT=kv_[:, h, :], rhs=qv[:, h, :], start=True, stop=True),
                             reads=[kt_, qt], writes=[tps])
                        sv, stl = stt.next()
                        P.op("dve", lambda e, sv=sv, pss=pss, d=d, c=c, h=h: e.scalar_tensor_tensor(
                            out=sv, in0=pss, scalar=Wt[:, d, c, h:h + 1], in1=maskd[d], op0=ALU.mult, op1=ALU.mult),
                            reads=[tps, t_W, t_m128], writes=[stl])
                        pa, tpa = psAr.next()
                        cbv, cbt = Cb[d][h]

                        def f(e, pa=pa, qv=qv, h=h, cbv=cbv, sv=sv, vv_=vv_, first=first):
                            if not first:
                                e.matmul(pa[:, 0:257], lhsT=qv[:, h, :], rhs=cbv[:, 0:257], start=True, stop=False)
                            return e.matmul(pa[:, 0:257], lhsT=sv, rhs=vv_[:, h, 0:257], start=first, stop=True)
                        P.op("pe", f, reads=[qt, cbt, stl, vt_], writes=[tpa])
                        pc, tpc = psCr.next()
                        P.op("pe", lambda e, pc=pc, kwv=kwv, vv_=vv_, h=h: e.matmul(pc[:, 0:257], lhsT=kwv[:, h, :], rhs=vv_[:, h, 0:257], start=True, stop=True),
                             reads=[kwt, vt_], writes=[tpc])
                        csv, cst = Cst[d][h]
                        if first:
                            P.op("dve", lambda e, csv=csv, pc=pc: e.tensor_copy(out=csv, in_=pc[:, 0:257]), reads=[tpc], writes=[cst])
                        else:
                            P.op("dve", lambda e, csv=csv, pc=pc, d=d, c=c, h=h: e.scalar_tensor_tensor(
                                out=csv, in0=csv, scalar=DC[:, d, c, h:h + 1], in1=pc[:, 0:257], op0=ALU.mult, op1=ALU.add),
                                reads=[tpc, cst, t_DC], writes=[cst])
                        if not last:
                            P.op("act", lambda e, cbv=cbv, csv=csv, d=d, cn=cn, h=h: e.activation(
                                out=cbv[:, 0:257], in_=csv, func=AF.Identity, scale=DC[:, d, cn, h:h + 1]), reads=[cst, t_DC], writes=[cbt])
                        P.op("act", lambda e, hv=hv, pa=pa, h=h: e.activation(out=hv[:, h, :], in_=pa[:, 0:257], func=AF.Identity),
                             reads=[tpa], writes=[ht])
                    fv, ft = fc.next()
                    f2, ft2 = fc.next()
                    P.op("dve", lambda e, fv=fv, hv=hv: e.tensor_scalar(out=fv, in0=hv[:, :, 256], scalar1=-1.0, scalar2=None, op0=ALU.mult),
                         reads=[ht], writes=[ft])
                    P.op("dve", lambda e, fv=fv, hv=hv: e.tensor_tensor(out=fv, in0=fv, in1=hv[:, :, 256], op=ALU.max), reads=[ht, ft], writes=[ft])
                    P.op("dve", lambda e, fv=fv, d=d, c=c: e.tensor_tensor(out=fv, in0=fv, in1=RI[:, d, c, :], op=ALU.max), reads=[ft, t_RI], writes=[ft])
                    P.op("dve", lambda e, fv=fv, f2=f2: e.reciprocal(out=f2, in_=fv), reads=[ft], writes=[ft2])
                    ho, hot = hout[d]
                    P.op("dve", lambda e, ho=ho, hv=hv, f2=f2: e.tensor_tensor(out=ho, in0=hv[:, :, 0:256],
                                                                              in1=f2.unsqueeze(2).to_broadcast([128, 8, 256]), op=ALU.mult),
                         reads=[ht, ft2], writes=[hot])
                    stq("sp", mhd[b, d, c * 128:(c + 1) * 128, :], ho.rearrange("p h e -> p (h e)"), hot, [d_mh[b]])
            phase_end()

        def mlstm_finish(b):
            hnb, t_hnb = A.alloc("hnb", [2048], F32)
            ld("sp", hnb, mhn[0:1, :].to_broadcast([128, 2048]), t_hnb)
            hf = [A.alloc("hf%d" % i, [2048], F32) for i in range(2)]
            hbk = [A.alloc("hbk%d" % i, [2048], F32) for i in range(2)]
            so = [A.alloc("so%d" % i, [2048], BF16) for i in range(2)]
            sz = [A.alloc("sz%d" % i, [2048], BF16) for i in range(2)]
            y, t_y = A.alloc("fy", [8, 256], F32)
            sq, t_sq = A.alloc("fsq", [8, 256], F32)
            ss, t_ss = A.alloc("fss", [8], F32)
            ub, t_ub = A.alloc("fub", [2048], BF16)
            uts = [A.alloc("uts%d" % i, [16, 128], BF16) for i in range(2)]
            psT = [(banks[0][0].bitcast(BF16), banks[0][1]), (banks[1][0].bitcast(BF16), banks[1][1])]

            def loads(c):
                tk = slice(c * 128, (c + 1) * 128)
                i = c % 2
                ld("sp", hf[i][0], mhd[b, 0, tk, :], hf[i][1], reads=[d_mh[b]])
                ld("sp", hbk[i][0], mhd[b, 1, tk, :], hbk[i][1], reads=[d_mh[b]])
                ld("sp", so[i][0], mso[b, tk, :], so[i][1], reads=[d_mso[b]])
                ld("sp", sz[i][0], msz[b, tk, :], sz[i][1], reads=[d_msz[b]])
            loads(0)
            for c in range(18):
                if c + 1 < 18:
                    loads(c + 1)
                i = c % 2
                hfv, hft = hf[i]
                hbv, hbt = hbk[i]
                sov, sot = so[i]
                szv, szt = sz[i]
                yf = y.rearrange("p h e -> p (h e)")
                P.op("pool", lambda e, hfv=hfv, hbv=hbv: e.tensor_tensor(out=hfv, in0=hfv, in1=hbv, op=ALU.add), reads=[hbt, hft], writes=[hft])
                P.op("dve", lambda e, hfv=hfv, sov=sov: e.tensor_tensor(out=yf, in0=hfv, in1=sov, op=ALU.mult), reads=[hft, sot], writes=[t_y])
                P.op("pool", lambda e: e.tensor_tensor(out=sq, in0=y, in1=y, op=ALU.mult), reads=[t_y], writes=[t_sq])
                P.op("dve", lambda e: e.tensor_reduce(out=ss, in_=sq, axis=AX.X, op=ALU.add), reads=[t_sq], writes=[t_ss])
                P.op("act", lambda e: e.activation(out=ss, in_=ss, func=AF.Ln, scale=1.0 / 256, bias=epsc), reads=[t_ss, t_eps], writes=[t_ss])
                P.op("act", lambda e: e.activation(out=ss, in_=ss, func=AF.Exp, scale=-0.5), reads=[t_ss], writes=[t_ss])
                P.op("dve", lambda e: e.tensor_tensor(out=y, in0=y, in1=ss.unsqueeze(2).to_broadcast([128, 8, 256]), op=ALU.mult),
                     reads=[t_y, t_ss], writes=[t_y])
                P.op("pool", lambda e: e.tensor_tensor(out=yf, in0=yf, in1=hnb, op=ALU.mult), reads=[t_y, t_hnb], writes=[t_y])
                P.op("dve", lambda e, szv=szv: e.tensor_tensor(out=ub, in0=yf, in1=szv, op=ALU.mult), reads=[t_y, szt], writes=[t_ub])
                uv, ut = uts[i]
                for half in range(2):
                    pt_, tpt = psT[half]

                    def f(e, pt_=pt_, half=half):
                        for jj in range(8):
                            j = half * 8 + jj
                            ins = e.transpose(out=pt_[:, jj * 128:(jj + 1) * 128], in_=ub[:, j * 128:(j + 1) * 128], identity=ident)
                        return ins
                    P.op("pe", f, reads=[t_ub, t_m128], writes=[tpt])
                    P.op("act", lambda e, uv=uv, pt_=pt_, half=half: e.activation(
                        out=uv[:, half * 8:(half + 1) * 8, :], in_=pt_[:, 0:1024].rearrange("p (j t) -> p j t", j=8), func=AF.Identity),
                        reads=[tpt], writes=[ut])
                stq("sp", uT[b, :, :, c * 128:(c + 1) * 128].rearrange("j p t -> p j t"), uv, ut, [d_u[b]])
            phase_end()

        blocks_all = [(b, t0, 512) for b in range(NB) for t0 in range(0, TL, 512)] + [(b, TL, 256) for b in range(NB)]
        blocks_all.sort(key=lambda x: (x[0], x[1]))
        blocks_lat = [(b, t0, 512) for b in range(NB) for t0 in range(0, TL, 512)]
        mod_phase()
        xsrc = xT
        for l in range(nlayers):
            kind = l % 3
            if kind == 0:
                last_no_ctx = (l == 3)
                fnet_layer(l, l // 3, blocks_lat if last_no_ctx else blocks_all, not last_no_ctx, xsrc)
            elif kind == 1:
                mlstm_layer(l, blocks_all, xsrc)
            else:
                attn_layer(l, blocks_all, xsrc)
            xsrc = xr
        norm_phase(0, blocks_lat, xsrc, final=True)
        P.op("sp", lambda e: None, reads=[d_out])
        print("ops:", {e: len(P.ops[e]) for e in ENGS}, "dma slots:", len(P.slots))
        P.emit(st)
    return nc


_CACHE = {}


def prep_inputs(inp, core):
    b0 = 2 * core
    f32 = np.float32
    x = inp["x"]
    ctx = inp["ctx"]
    xt = np.empty((NB, D, TA), f32)
    for i in range(NB):
        xt[i, :, :TL] = x[b0 + i].T
        xt[i, :, TL:] = ctx[b0 + i].T
    m = {"xT": xt.reshape(NB, 16, 128, TA)}
    cc = np.stack([inp["c"][b0], inp["c"][b0 + 1], inp["c_ctx"]], axis=1).astype(f32)
    m["cT"] = np.ascontiguousarray(cc.reshape(16, 128, 3).transpose(1, 0, 2))
    return m


def shared_inputs(inp):
    f32 = np.float32
    m = {}
    m["ada_w"] = np.ascontiguousarray(inp["ada_w"], dtype=f32)
    m["ada_b"] = np.ascontiguousarray(inp["ada_b"], dtype=f32)
    g = np.concatenate([inp["norm_g"], inp["final_g"][None]], axis=0).astype(f32)
    m["gT"] = np.ascontiguousarray(g.reshape(5, 16, 128).transpose(2, 0, 1))
    m["fnet_w_gate"] = np.ascontiguousarray(inp["fnet_w_gate"], dtype=f32)
    m["fnet_w_out"] = np.ascontiguousarray(inp["fnet_w_out"], dtype=f32)
    m["mlstm_w_in"] = np.ascontiguousarray(inp["mlstm_w_in"][0], dtype=f32)
    m["mlstm_b_gate"] = np.ascontiguousarray(inp["mlstm_b_gate"], dtype=f32)
    m["mlstm_hn"] = np.ascontiguousarray(inp["mlstm_hn"], dtype=f32)
    m["mlstm_w_out"] = np.ascontiguousarray(inp["mlstm_w_out"][0], dtype=f32)
    m["attn_w_in"] = np.ascontiguousarray(inp["attn_w_in"][0], dtype=f32)
    m["attn_qk"] = np.ascontiguousarray(np.stack([inp["attn_qn"][0], inp["attn_kn"][0]], axis=1), dtype=f32)
    m["attn_w_out"] = np.ascontiguousarray(inp["attn_w_out"][0], dtype=f32)
    for k, v in make_consts().items():
        m["c_" + k] = v
    return m


def kernel(**inputs):
    inp = {k: np.asarray(v) for k, v in inputs.items()}
    ncores = 8
    if "nc" not in _CACHE:
        _CACHE["nc"] = build()
    nc = _CACHE["nc"]
    sh = shared_inputs(inp)
    in_maps = []
    for c in range(ncores):
        m = dict(sh)
        m.update(prep_inputs(inp, c))
        in_maps.append(m)
    res = run_bass_kernel_spmd(nc, in_maps, core_ids=list(range(ncores)))
    out = np.empty((16, TL, D), np.float32)
    for c in range(ncores):
        o = np.asarray(res.results[c]["outT"]).reshape(NB, D, TL)
        for i in range(NB):
            out[2 * c + i] = o[i].T
    return out
```

```python
import math
import numpy as np
import ml_dtypes
from contextlib import ExitStack
import concourse.bass as bass
import concourse.mybir as mybir
from concourse.bass_utils import run_bass_kernel_spmd

F32 = mybir.dt.float32
BF16 = mybir.dt.bfloat16
AF = mybir.ActivationFunctionType
ALU = mybir.AluOpType
AX = mybir.AxisListType
NPBF = ml_dtypes.bfloat16

D = 2048
TL = 2048
TC = 256
TA = TL + TC
NB = 2
EPS = 1e-6
ENGS = ("sp", "act", "dve", "pool", "pe")


class Slot:
    __slots__ = ("sem", "cnt")

    def __init__(self):
        self.sem = None
        self.cnt = 0


class DT:
    __slots__ = ("name", "writers", "readers", "multi", "slot", "last_dma")

    def __init__(self, name, multi=False, fence=()):
        self.name = name
        self.writers = list(fence)
        self.readers = []
        self.multi = multi
        self.slot = None
        self.last_dma = None


class Op:
    __slots__ = ("eng", "fn", "deps", "is_dma", "sig", "need")

    def __init__(self, eng, fn, is_dma):
        self.eng = eng
        self.fn = fn
        self.deps = []
        self.is_dma = is_dma
        self.sig = None
        self.need = False


class Prog:
    def __init__(self, nc):
        self.nc = nc
        self.ops = {e: [] for e in ENGS}
        self.slots = []
        self.free_slots = []
        self.fence = []
        self.live = []

    def tile(self, name, multi=False, phase=True):
        t = DT(name, multi, self.fence)
        if phase:
            self.live.append(t)
        return t

    def _track(self, op, reads, writes):
        deps = op.deps
        for t in reads:
            deps.extend(t.writers)
            t.readers.append(op)
        for t in writes:
            if t.multi:
                if t.readers:
                    deps.extend(t.readers)
                    t.readers = []
                    t.writers = [op]
                else:
                    t.writers.append(op)
            else:
                deps.extend(t.readers)
                deps.extend(t.writers)
                t.readers = []
                t.writers = [op]

    def op(self, eng, fn, reads=(), writes=()):
        o = Op(eng, fn, False)
        self._track(o, reads, writes)
        o.deps = [d for d in o.deps if d is not o]
        self.ops[eng].append(o)
        return o

    def dma(self, eng, fn, n, st, reads=(), writes=()):
        o = Op(eng, fn, True)
        self._track(o, reads, writes)
        o.deps = [d for d in o.deps if d is not o]
        if st.last_dma is not None:
            o.deps.append(st.last_dma)
        st.last_dma = o
        if st.slot is None:
            if self.free_slots:
                st.slot = self.free_slots.pop()
            else:
                st.slot = Slot()
                self.slots.append(st.slot)
        st.slot.cnt += 16 * n
        o.sig = (st.slot, st.slot.cnt)
        self.ops[eng].append(o)
        return o

    def barrier(self, fn):
        o = self.op("dve", fn, writes=self.live)
        for t in self.live:
            if t.slot is not None:
                self.free_slots.append(t.slot)
        self.live = []
        self.fence = [o]
        return o

    def emit(self, stack):
        nc = self.nc
        for e in ENGS:
            for o in self.ops[e]:
                for d in o.deps:
                    if not d.is_dma:
                        if d.eng == "pe" and o.eng == "pe" and not o.is_dma:
                            continue
                        d.need = True
        engsem = {}
        for e in ENGS:
            if e != "sp":
                engsem[e] = stack.enter_context(nc.semaphore("eng_" + e))
        for i, t in enumerate(self.slots):
            t.sem = stack.enter_context(nc.semaphore("dsl%d" % i))
        for e in ENGS:
            c = 0
            for o in self.ops[e]:
                if o.is_dma:
                    continue
                if o.need:
                    c += 1
                    o.sig = (e, c)
        block = stack.enter_context(nc.Block())
        prog = self

        def run(e, eng):
            waited = {}
            for o in prog.ops[e]:
                req = {}
                for d in o.deps:
                    if d.sig is None:
                        continue
                    if (not d.is_dma) and d.eng == "pe" and e == "pe" and not o.is_dma:
                        continue
                    k, v = d.sig
                    if req.get(k, 0) < v:
                        req[k] = v
                for k, v in req.items():
                    if waited.get(k, 0) >= v:
                        continue
                    waited[k] = v
                    sem = engsem[k] if isinstance(k, str) else k.sem
                    eng.wait_ge(sem, v)
                if o.is_dma:
                    o.fn(eng, o.sig[0].sem)
                else:
                    ins = o.fn(eng)
                    if o.need:
                        ins.then_inc(engsem[e], 1)

        @block.sync
        def _(eng):
            run("sp", eng)

        @block.scalar
        def _(eng):
            run("act", eng)

        @block.vector
        def _(eng):
            run("dve", eng)

        @block.gpsimd
        def _(eng):
            run("pool", eng)

        @block.tensor
        def _(eng):
            run("pe", eng)


class Arena:
    def __init__(self, ap, P, base=0):
        self.ap = ap
        self.P = P
        self.off = base
        self.base = base
        self.cap = ap.shape[1]

    def reset(self):
        self.off = self.base

    def alloc(self, name, shape, dtype, phase=True):
        n = int(np.prod(shape))
        words = n if dtype == F32 else (n + 1) // 2
        assert self.off + words <= self.cap, (name, self.off, words, self.cap)
        v = self.ap[:, self.off:self.off + words]
        if dtype != F32:
            v = v.bitcast(dtype)
            if n % 2:
                v = v[:, 0:n]
        self.off += words
        if len(shape) == 2:
            v = v.rearrange("p (a b) -> p a b", a=shape[0])
        elif len(shape) == 3:
            v = v.rearrange("p (a b c) -> p a b c", a=shape[0], b=shape[1])
        elif len(shape) == 4:
            v = v.rearrange("p (a b c d) -> p a b c d", a=shape[0], b=shape[1], c=shape[2])
        return v, self.P.tile(name, phase=phase)


class Rot:
    def __init__(self, items):
        self.items = items
        self.i = 0

    def next(self):
        r = self.items[self.i % len(self.items)]
        self.i += 1
        return r


def make_consts():
    c = {}
    p = np.arange(128)
    d = (np.arange(4)[None, :] * 128 + p[:, None]).astype(np.float64)
    e = np.arange(512, dtype=np.float64)
    ang = 2 * np.pi * d[:, :, None] * e[None, None, :] / 512.0
    c["CD"] = np.cos(ang).astype(NPBF)
    c["SD"] = np.sin(ang).astype(NPBF)
    t = (256 * np.arange(8)[None, None, :] + 2 * p[:, None, None] + np.arange(2)[None, :, None]).astype(np.float64)
    tt = np.arange(1024, dtype=np.float64)
    tm = np.mod(t[:, :, :, None] * tt[None, None, None, :], 2048.0)
    ang = 2 * np.pi * tm / 2048.0
    c["CT"] = np.cos(ang).astype(NPBF)
    c["STn"] = (-np.sin(ang)).astype(NPBF)
    t = (128 * np.arange(2)[None, :] + p[:, None]).astype(np.float64)
    tt = np.arange(256, dtype=np.float64)
    ang = 2 * np.pi * np.mod(t[:, :, None] * tt[None, None, :], 256.0) / 256.0
    c["C256"] = np.cos(ang).astype(NPBF)
    c["S256n"] = (-np.sin(ang)).astype(NPBF)
    rows = TL // 64
    r = np.repeat(np.arange(rows), 64).astype(np.float32)
    col = np.tile(np.arange(64), rows).astype(np.float32)
    freqs = (np.float32(10000.0) ** (-np.arange(0, 64, 2, dtype=np.float32) / np.float32(64))).astype(np.float32)
    angt = np.concatenate([r[:, None] * freqs, col[:, None] * freqs], axis=-1).astype(np.float32)
    cosT = np.repeat(np.cos(angt).T, 2, axis=0)
    sinT = np.repeat(np.sin(angt).T, 2, axis=0)
    c["ropec"] = np.ascontiguousarray(cosT).astype(np.float32)
    c["ropes"] = np.ascontiguousarray(sinT).astype(np.float32)
    ident = np.eye(128, dtype=np.float32)
    Rm = np.zeros((128, 128), np.float32)
    for i in range(64):
        Rm[2 * i + 1, 2 * i] = -1.0
        Rm[2 * i, 2 * i + 1] = 1.0
    s = np.arange(128)[:, None]
    tq = np.arange(128)[None, :]
    mf = (s <= tq).astype(np.float32)
    mb = (s >= tq).astype(np.float32)
    c["m128"] = np.stack([ident, Rm, mf, mb], axis=1).astype(NPBF)
    c["f128"] = np.stack([(s > tq).astype(np.float32), (s < tq).astype(np.float32), np.ones((128, 128), np.float32)], axis=1)
    return c


CONST_SPECS = [("CD", [128, 4, 512], BF16), ("SD", [128, 4, 512], BF16), ("CT", [128, 2, 8, 1024], BF16),
               ("STn", [128, 2, 8, 1024], BF16), ("C256", [128, 2, 256], BF16), ("S256n", [128, 2, 256], BF16),
               ("ropec", [128, 2048], F32), ("ropes", [128, 2048], F32), ("m128", [128, 4, 128], BF16),
               ("f128", [128, 3, 128], F32)]


def build(nlayers=4, dump=()):
    nc = bass.Bass("TRN2", target_bir_lowering=False)

    def din(name, shape, dt=F32):
        return nc.dram_tensor(name, shape, dt, kind="ExternalInput").ap()

    kinds = set(l % 3 for l in range(nlayers))
    need = {"xr": nlayers > 0, "hT": nlayers > 0, "uT": nlayers > 0, "sgT": bool(kinds & {0, 2}),
            "Pd": 0 in kinds, "Qd": 0 in kinds, "qTd": 2 in kinds, "kTd": 2 in kinds, "vvd": 2 in kinds}

    def dsc(name, shape, dt):
        if not need.get(name, 1 in kinds):
            shape = [1] * (len(shape) - 1) + [16]
        if name in dump:
            return nc.dram_tensor(name, shape, dt, kind="ExternalOutput").ap()
        return nc.dram_tensor(name, shape, dt).ap()

    xT = din("xT", [NB, 16, 128, TA])
    cT = din("cT", [128, 16, 3])
    ada_w = din("ada_w", [4, 2048, 6144])
    ada_b = din("ada_b", [4, 6144])
    gT = din("gT", [128, 5, 16])
    fwg = din("fnet_w_gate", [2, 2048, 2048])
    fwo = din("fnet_w_out", [2, 2048, 2048])
    mwi = din("mlstm_w_in", [2048, 8224])
    mbg = din("mlstm_b_gate", [1, 32])
    mhn = din("mlstm_hn", [1, 2048])
    mwo = din("mlstm_w_out", [2048, 2048])
    awi = din("attn_w_in", [2048, 5120])
    aqk = din("attn_qk", [128, 2])
    awo = din("attn_w_out", [2048, 2048])
    CN = {n: din("c_" + n, s, dt) for n, s, dt in CONST_SPECS}
    outT = nc.dram_tensor("outT", [NB, 16, 128, TL], F32, kind="ExternalOutput").ap()

    xr = dsc("xr", [NB, 16, 128, TA], F32)
    hT = dsc("hT", [NB, 16, 128, TA], BF16)
    uT = dsc("uT", [NB, 16, 128, TA], BF16)
    sgT = dsc("sgT", [NB, 16, 128, TA], BF16)
    Pd = dsc("Pd", [NB, 16, TA, 128], BF16)
    Qd = dsc("Qd", [NB, 16, TA, 128], BF16)
    qTd = dsc("qTd", [NB, 16, 128, TA], BF16)
    kTd = dsc("kTd", [NB, 4, 128, TA], BF16)
    vvd = dsc("vvd", [NB, TA, 512], BF16)
    mqT = dsc("mqT", [NB, 8, 128, TA], BF16)
    mkT = dsc("mkT", [NB, 8, 128, TA], BF16)
    mkd = dsc("mkd", [NB, TA, 1024], BF16)
    mvd = dsc("mvd", [NB, TA, 2048], BF16)
    mso = dsc("mso", [NB, TA, 2048], BF16)
    msz = dsc("msz", [NB, TA, 2048], BF16)
    mgd = dsc("mgd", [NB, TA, 32], F32)
    mhd = dsc("mhd", [NB, 2, TA, 2048], F32)

    st = ExitStack()
    with st:
        P = Prog(nc)
        sb = st.enter_context(nc.sbuf_tensor("arena", [128, 51200], F32))
        PA = Arena(sb[:], P)
        banks = []
        for i in range(8):
            pt = st.enter_context(nc.psum_tensor("ps%d" % i, [128, 512], F32))
            banks.append((pt[:], P.tile("psb%d" % i, phase=False)))

        def dtile(name):
            return [P.tile(name + str(b), multi=True, phase=False) for b in range(NB)]
        d_x, d_h, d_u, d_sg, d_P, d_Q = dtile("x"), dtile("h"), dtile("u"), dtile("sg"), dtile("P"), dtile("Q")
        d_q, d_k, d_v = dtile("q"), dtile("k"), dtile("v")
        d_mq, d_mkT, d_mk, d_mv, d_mso, d_msz, d_mg, d_mh = (dtile("mq"), dtile("mkT"), dtile("mk"), dtile("mv"),
                                                            dtile("mso"), dtile("msz"), dtile("mg"), dtile("mh"))
        d_out = P.tile("out", multi=True, phase=False)

        m128, t_m128 = PA.alloc("m128", [4, 128], BF16, phase=False)
        f128, t_f128 = PA.alloc("f128", [3, 128], F32, phase=False)
        onesb, t_onesb = PA.alloc("onesb", [128], BF16, phase=False)
        scT, t_scT = PA.alloc("scT", [16, 3], BF16, phase=False)
        cTs, t_cTs = PA.alloc("cTs", [16, 3], F32, phase=False)
        gTs, t_gTs = PA.alloc("gTs", [5, 16], F32, phase=False)
        mod, t_mod = PA.alloc("mod", [4, 48, 3], F32, phase=False)
        amod, t_amod = PA.alloc("amod", [4, 16, 3], F32, phase=False)
        qks, t_qks = PA.alloc("qks", [2], F32, phase=False)
        scr, t_scr = PA.alloc("scr", [4], F32, phase=False)
        epsc, t_eps = PA.alloc("epsc", [1], F32, phase=False)
        onec, t_onec = PA.alloc("onec", [1], F32, phase=False)
        wbufs = [PA.alloc("wbuf%d" % i, [16, 1024], BF16, phase=False) for i in range(2)]
        wstate = {"i": 0, "pf": {}}
        PA.base = PA.off
        A = PA
        ident = m128[:, 0, :]
        Rm = m128[:, 1, :]
        maskd = [m128[:, 2, :], m128[:, 3, :]]
        trid = [f128[:, 0, :], f128[:, 1, :]]
        onesf = f128[:, 2, :]

        def ld(eng, dst, src, tl, reads=(), n=1):
            def f(e, s):
                e.dma_start(out=dst, in_=src).then_inc(s, 16)
            P.dma(eng, f, 1, tl, reads=reads, writes=[tl])

        def stq(eng, dst, src, tl, dtiles):
            def f(e, s):
                e.dma_start(out=dst, in_=src).then_inc(s, 16)
            P.dma(eng, f, 1, tl, reads=[tl], writes=dtiles)

        def wload(key, src, gc):
            if key in wstate["pf"]:
                return wstate["pf"].pop(key)
            wv, wt = wbufs[wstate["i"] % 2]
            wstate["i"] += 1
            ld("pool", wv[:, :, :gc], src.rearrange("(k p) n -> p k n", p=128), wt)
            return wv, wt

        def prefetch(key, src, gc):
            if key not in wstate["pf"]:
                wstate["pf"][key] = wload(None, src, gc)

        ld("sp", m128, CN["m128"], t_m128)
        ld("sp", f128, CN["f128"], t_f128)
        ld("sp", cTs, cT, t_cTs)
        ld("sp", gTs, gT, t_gTs)
        ld("sp", qks, aqk, t_qks)
        P.op("dve", lambda e: e.memset(onesb, 1.0), writes=[t_onesb])
        P.op("dve", lambda e: e.memset(epsc, EPS), writes=[t_eps])
        P.op("dve", lambda e: e.memset(onec, 1.0), writes=[t_onec])
        P.op("act", lambda e: e.activation(out=scT, in_=cTs, func=AF.Silu), reads=[t_cTs], writes=[t_scT])

        def phase_end():
            P.barrier(lambda e: e.memset(scr, 0.0))
            A.reset()

        def mm16(out, lhs_fn, rhs_fn, nk=16):
            def f(e):
                for k in range(nk):
                    ins = e.matmul(out, lhsT=lhs_fn(k), rhs=rhs_fn(k), start=(k == 0), stop=(k == nk - 1))
                return ins
            return f

        def mod_phase():
            wbs = [A.alloc("mw%d" % i, [16, 512], BF16) for i in range(2)]
            adb, t_adb = A.alloc("adb", [6144], BF16)
            psm, t_psm = banks[0]
            for l in range(nlayers):
                def f(e, s, l=l):
                    e.dma_start(out=adb[0:1, :], in_=ada_b[l:l + 1, :]).then_inc(s, 16)
                P.dma("pool", f, 1, t_adb, writes=[t_adb])
                for nb in range(12):
                    wv, wt = wbs[nb % 2]
                    ld("pool", wv, ada_w[l, :, nb * 512:(nb + 1) * 512].rearrange("(k p) n -> p k n", p=128), wt)

                    def f(e, nb=nb, wv=wv):
                        for j in range(4):
                            nt_ = nb * 4 + j
                            o = psm[:, nt_ * 3:nt_ * 3 + 3]
                            for k in range(16):
                                e.matmul(o, lhsT=wv[:, k, j * 128:(j + 1) * 128], rhs=scT[:, k, :], start=(k == 0), stop=False)
                            ins = e.matmul(o, lhsT=adb[0:1, nt_ * 128:(nt_ + 1) * 128], rhs=onesb[0:1, 0:3], start=False, stop=True)
                        return ins
                    P.op("pe", f, reads=[wt, t_scT, t_adb, t_onesb], writes=[t_psm])
                P.op("act", lambda e, l=l: e.activation(out=mod[:, l, :, :], in_=psm[:, 0:144].rearrange("p (a b) -> p a b", a=48), func=AF.Identity),
                     reads=[t_psm], writes=[t_mod])
                P.op("dve", lambda e, l=l: e.tensor_scalar_add(out=amod[:, l, :, :], in0=mod[:, l, 16:32, :], scalar1=1.0),
                     reads=[t_mod], writes=[t_amod])
                P.op("dve", lambda e, l=l: e.tensor_tensor(out=amod[:, l, :, :], in0=amod[:, l, :, :],
                                                          in1=gTs[:, l, :].unsqueeze(2).to_broadcast([128, 16, 3]), op=ALU.mult),
                     reads=[t_gTs, t_amod], writes=[t_amod])
            phase_end()

        def norm_phase(l, blocks, xsrc, final=False, pf=None):
            if pf is not None:
                prefetch(*pf)
            xs = [A.alloc("nx%d" % i, [16, 512], F32) for i in range(2)]
            sq, t_sq = A.alloc("nsq", [16, 512], BF16)
            hb = [A.alloc("nh%d" % i, [16, 512], F32 if final else BF16) for i in range(1 if final else 2)]
            rs, t_rs = A.alloc("nrs", [512], F32)
            psn, t_psn = banks[1]

            def load(i):
                b, t0, nt = blocks[i]
                xv, xt = xs[i % 2]
                ld("sp", xv[:, :, :nt], xsrc[b, :, :, t0:t0 + nt].rearrange("i p t -> p i t"), xt, reads=[d_x[b]])
            load(0)
            for i, (b, t0, nt) in enumerate(blocks):
                if i + 1 < len(blocks):
                    load(i + 1)
                c = 2 if t0 >= TL else b
                xv, xt = xs[i % 2]
                hv, ht = hb[i % len(hb)]
                P.op("act", lambda e, xv=xv, nt=nt: e.activation(out=sq[:, :, :nt], in_=xv[:, :, :nt], func=AF.Square),
                     reads=[xt], writes=[t_sq])
                P.op("pe", mm16(psn[:, :nt], lambda k: onesb, lambda k, nt=nt: sq[:, k, :nt]), reads=[t_sq, t_onesb], writes=[t_psn])
                P.op("act", lambda e, nt=nt: e.activation(out=rs[:, :nt], in_=psn[:, :nt], func=AF.Ln, scale=1.0 / D, bias=epsc),
                     reads=[t_psn, t_eps], writes=[t_rs])
                P.op("act", lambda e, nt=nt: e.activation(out=rs[:, :nt], in_=rs[:, :nt], func=AF.Exp, scale=-0.5),
                     reads=[t_rs], writes=[t_rs])
                tt, t_tt = xv, xt
                P.op("dve", lambda e, xv=xv, nt=nt: e.tensor_tensor(out=xv[:, :, :nt], in0=xv[:, :, :nt],
                                                                  in1=rs[:, :nt].unsqueeze(1).to_broadcast([128, 16, nt]), op=ALU.mult),
                     reads=[xt, t_rs, t_sq], writes=[xt])

                def f(e, hv=hv, nt=nt, c=c, tt=tt):
                    for i_ in range(16):
                        if final:
                            ins = e.activation(out=hv[:, i_, :nt], in_=tt[:, i_, :nt], func=AF.Identity, scale=gTs[:, 4, i_:i_ + 1])
                        else:
                            ins = e.activation(out=hv[:, i_, :nt], in_=tt[:, i_, :nt], func=AF.Identity,
                                               bias=mod[:, l, i_, c:c + 1], scale=amod[:, l, i_, c:c + 1])
                    return ins
                P.op("act", f, reads=[t_tt, t_mod, t_amod, t_gTs], writes=[ht])
                if final:
                    stq("sp", outT[b, :, :, t0:t0 + nt].rearrange("i p t -> p i t"), hv[:, :, :nt], ht, [d_out])
                else:
                    stq("sp", hT[b, :, :, t0:t0 + nt].rearrange("i p t -> p i t"), hv[:, :, :nt], ht, [d_h[b]])
            phase_end()

        def linear(src, d_src, wsrc, ncols, mode, blocks, evac, psb, pre=None, gsz=1024, wkey=None):
            hb = [A.alloc("lh%d" % i, [16, 512], BF16) for i in range(2)]
            ng = (ncols + gsz - 1) // gsz
            items = [(g, b, t0, nt) for g in range(ng) for (b, t0, nt) in blocks]

            def load(i):
                g, b, t0, nt = items[i]
                hv, ht = hb[i % 2]
                ld("sp", hv[:, :, :nt], src[b, :, :, t0:t0 + nt].rearrange("i p t -> p i t"), ht, reads=[d_src[b]])
                if pre is not None:
                    pre(i, g, b, t0, nt)
            load(0)
            lastg = -1
            for i, (g, b, t0, nt) in enumerate(items):
                c0 = g * gsz
                gc = min(gsz, ncols - c0)
                if g != lastg:
                    wcur = wload((wkey, g), wsrc[:, c0:c0 + gc], gc)
                    lastg = g
                wv, wt = wcur
                if i + 1 < len(items):
                    load(i + 1)
                hv, ht = hb[i % 2]
                if mode == "ws":
                    for j in range(gc // 128):
                        ps, pst = psb.next()
                        P.op("pe", mm16(ps[:, :nt], lambda k, j=j, wv=wv: wv[:, k, j * 128:(j + 1) * 128],
                                        lambda k, hv=hv, nt=nt: hv[:, k, :nt]), reads=[wt, ht], writes=[pst])
                        evac(i, b, t0, nt, c0 // 128 + j, j, gc // 128, ps[:, :nt], pst)
                else:
                    for tq in range(nt // 128):
                        for n0 in range(0, gc, 512):
                            n1 = min(gc, n0 + 512)
                            ps, pst = psb.next()
                            P.op("pe", mm16(ps[:, :n1 - n0], lambda k, hv=hv, tq=tq: hv[:, k, tq * 128:(tq + 1) * 128],
                                            lambda k, wv=wv, n0=n0, n1=n1: wv[:, k, n0:n1]), reads=[wt, ht], writes=[pst])
                            evac(i, b, t0 + tq * 128, c0 + n0, n1 - n0, n0, gc, ps[:, :n1 - n0], pst)

        def ws_simple(dst, d_dst, func, scale=1.0, hbase=0):
            stg = Rot([A.alloc("wss%d" % i, [8, 512], BF16) for i in range(2)])
            cur = {}

            def evac(i, b, t0, nt, jn, j, nj, ps, pst):
                if j == 0:
                    cur["s"] = stg.next()
                sv, stl = cur["s"]
                P.op("act", lambda e: e.activation(out=sv[:, j, :nt], in_=ps, func=func, scale=scale), reads=[pst], writes=[stl])
                if j == nj - 1:
                    h0 = jn - j - hbase
                    stq("sp", dst[b, h0:h0 + nj, :, t0:t0 + nt].rearrange("j p t -> p j t"), sv[:, :nj, :nt], stl, [d_dst[b]])
            return evac

        def as_simple(dst, d_dst, func, cbase, width=1024, dt=BF16, addt=None):
            stg = Rot([A.alloc("ass%d" % i, [width], dt) for i in range(2)])
            cur = {}

            def evac(i, b, tok0, col0, ncol, n0, gc, ps, pst):
                if n0 == 0:
                    cur["s"] = stg.next()
                sv, stl = cur["s"]
                if addt is not None:
                    av, at = addt
                    P.op("dve", lambda e: e.tensor_tensor(out=sv[:, n0:n0 + ncol], in0=ps, in1=av[:, n0:n0 + ncol], op=ALU.add),
                         reads=[pst, at], writes=[stl])
                else:
                    P.op("act", lambda e: e.activation(out=sv[:, n0:n0 + ncol], in_=ps, func=func), reads=[pst], writes=[stl])
                if n0 + ncol >= gc:
                    cs = col0 - n0
                    stq("sp", dst[b, tok0:tok0 + 128, cs:cs + gc], sv[:, :gc], stl, [d_dst[b]])
            return evac

        def out_phase(l, wout, blocks, xsrc, wkey=None):
            xb = [A.alloc("ox%d" % i, [8, 512], F32) for i in range(2)]

            def pre(i, g, b, t0, nt):
                xv, xt = xb[i % 2]
                ld("sp", xv[:, :, :nt], xsrc[b, g * 8:(g + 1) * 8, :, t0:t0 + nt].rearrange("i p t -> p i t"), xt, reads=[d_x[b]])

            def evac(i, b, t0, nt, jn, j, nj, ps, pst):
                xv, xt = xb[i % 2]
                c = 2 if t0 >= TL else b
                P.op("dve", lambda e: e.scalar_tensor_tensor(out=xv[:, j, :nt], in0=ps, scalar=mod[:, l, 32 + jn, c:c + 1],
                                                            in1=xv[:, j, :nt], op0=ALU.mult, op1=ALU.add),
                     reads=[pst, xt, t_mod], writes=[xt])
                if j == nj - 1:
                    g0 = jn - j
                    stq("sp", xr[b, g0:g0 + 8, :, t0:t0 + nt].rearrange("i p t -> p i t"), xv[:, :, :nt], xt, [d_x[b]])
            linear(uT, d_u, wout, 2048, "ws", blocks, evac, Rot(banks[0:4]), pre=pre, wkey=wkey)
            phase_end()

        def fnet_layer(l, j, blocks, with_ctx, xsrc):
            norm_phase(l, blocks, xsrc, pf=(("fwg%d" % j, 0), fwg[j][:, 0:1024], 1024))
            linear(hT, d_h, fwg[j], 2048, "ws", blocks, ws_simple(sgT, d_sg, AF.Silu), Rot(banks[0:4]), wkey="fwg%d" % j)
            phase_end()
            CDs, t_CD = A.alloc("CD", [4, 512], BF16)
            SDs, t_SD = A.alloc("SD", [4, 512], BF16)
            ld("sp", CDs, CN["CD"], t_CD)
            ld("sp", SDs, CN["SD"], t_SD)
            hb = [A.alloc("fh%d" % i, [16, 512], BF16) for i in range(2)]
            stP = Rot([A.alloc("fsp%d" % i, [2048], BF16) for i in range(2)])
            stQ = Rot([A.alloc("fsq%d" % i, [2048], BF16) for i in range(2)])
            psb = Rot(banks[0:6])

            def load(i):
                b, t0, nt = blocks[i]
                hv, ht = hb[i % 2]
                ld("sp", hv[:, :, :nt], hT[b, :, :, t0:t0 + nt].rearrange("i p t -> p i t"), ht, reads=[d_h[b]])
            load(0)
            for i, (b, t0, nt) in enumerate(blocks):
                if i + 1 < len(blocks):
                    load(i + 1)
                hv, ht = hb[i % 2]
                nrm = (1.0 / 1024.0) if t0 < TL else 1.0 / math.sqrt(256.0 * 512.0)
                for tq in range(nt // 128):
                    pv, pt = stP.next()
                    qv, qt = stQ.next()
                    for g in range(4):
                        psP, tP = psb.next()
                        P.op("pe", mm16(psP, lambda k, g=g, tq=tq, hv=hv: hv[:, 4 * g + k, tq * 128:(tq + 1) * 128],
                                        lambda k: CDs[:, k, :], nk=4), reads=[ht, t_CD], writes=[tP])
                        P.op("act", lambda e, g=g, psP=psP, pv=pv, nrm=nrm: e.activation(out=pv[:, g * 512:(g + 1) * 512], in_=psP, func=AF.Identity, scale=nrm),
                             reads=[tP], writes=[pt])
                        psQ, tQ = psb.next()
                        P.op("pe", mm16(psQ, lambda k, g=g, tq=tq, hv=hv: hv[:, 4 * g + k, tq * 128:(tq + 1) * 128],
                                        lambda k: SDs[:, k, :], nk=4), reads=[ht, t_SD], writes=[tQ])
                        P.op("dve", lambda e, g=g, psQ=psQ, qv=qv, nrm=nrm: e.tensor_scalar(out=qv[:, g * 512:(g + 1) * 512], in0=psQ, scalar1=nrm,
                                                                                 scalar2=None, op0=ALU.mult), reads=[tQ], writes=[qt])
                    tok = t0 + tq * 128
                    stq("sp", Pd[b, :, tok:tok + 128, :].rearrange("j t e -> t j e"), pv.rearrange("p (j e) -> p j e", j=16), pt, [d_P[b]])
                    stq("sp", Qd[b, :, tok:tok + 128, :].rearrange("j t e -> t j e"), qv.rearrange("p (j e) -> p j e", j=16), qt, [d_Q[b]])
            phase_end()
            prefetch(("fwo%d" % j, 0), fwo[j][:, 0:1024], 1024)
            CTs, t_CT = A.alloc("CT", [2, 8, 1024], BF16)
            STs, t_ST = A.alloc("ST", [2, 8, 1024], BF16)
            ld("sp", CTs, CN["CT"], t_CT)
            ld("sp", STs, CN["STn"], t_ST)
            Pe = [A.alloc("Pe%d" % i, [2, 8, 128], BF16) for i in range(2)]
            Qe = [A.alloc("Qe%d" % i, [2, 8, 128], BF16) for i in range(2)]
            sgs = [A.alloc("sgs%d" % i, [2048], BF16) for i in range(2)]
            ust = [A.alloc("ust%d" % i, [2048], BF16) for i in range(2)]
            tB, t_tB = A.alloc("tB", [512], F32)
            y1, t_y1 = A.alloc("y1", [512], F32)
            y2, t_y2 = A.alloc("y2", [512], F32)
            psA = Rot(banks[0:2])
            psBk = Rot(banks[2:4])
            nb_list = sorted(set(b for (b, _, _) in blocks))
            items = [(b, jt) for b in nb_list for jt in range(16)]

            def load2(i):
                b, jt = items[i]
                pv, pt = Pe[i % 2]
                qv, qt = Qe[i % 2]
                sv, stl = sgs[i % 2]
                for r in range(2):
                    ld("sp", pv[:, r, :, :], Pd[b, jt, 0:TL, :].rearrange("(c p r) e -> r p c e", p=128, r=2)[r], pt, reads=[d_P[b]])
                    ld("sp", qv[:, r, :, :], Qd[b, jt, 0:TL, :].rearrange("(c p r) e -> r p c e", p=128, r=2)[r], qt, reads=[d_Q[b]])
                ld("sp", sv, sgT[b, jt, :, 0:TL], stl, reads=[d_sg[b]])
            load2(0)
            for i, (b, jt) in enumerate(items):
                if i + 1 < len(items):
                    load2(i + 1)
                pv, pt = Pe[i % 2]
                qv, qt = Qe[i % 2]
                sv, stl = sgs[i % 2]
                uv, ut = ust[i % 2]
                for jb in range(2):
                    cs = slice(jb * 512, (jb + 1) * 512)
                    pa, ta = psA.next()
                    pb, tb = psBk.next()
                    for r, (pp, tp) in enumerate(((pa, ta), (pb, tb))):
                        def f(e, r=r, pp=pp, pv=pv, qv=qv, cs=cs):
                            for c in range(8):
                                e.matmul(pp, lhsT=pv[:, r, c, :], rhs=CTs[:, r, c, cs], start=(c == 0), stop=False)
                                ins = e.matmul(pp, lhsT=qv[:, r, c, :], rhs=STs[:, r, c, cs], start=False, stop=(c == 7))
                            return ins
                        P.op("pe", f, reads=[pt, qt, t_CT, t_ST], writes=[tp])
                    P.op("act", lambda e, pb=pb: e.activation(out=tB, in_=pb, func=AF.Identity), reads=[tb], writes=[t_tB])
                    P.op("dve", lambda e, pa=pa: e.tensor_tensor(out=y1, in0=pa, in1=tB, op=ALU.add), reads=[ta, t_tB], writes=[t_y1])
                    P.op("dve", lambda e, pa=pa: e.tensor_tensor(out=y2, in0=pa, in1=tB, op=ALU.subtract), reads=[ta, t_tB], writes=[t_y2])
                    P.op("pool", lambda e, uv=uv, sv=sv, cs=cs: e.tensor_tensor(out=uv[:, cs], in0=y1, in1=sv[:, cs], op=ALU.mult),
                         reads=[t_y1, stl], writes=[ut])
                    c2 = slice(1024 + jb * 512, 1024 + (jb + 1) * 512)
                    P.op("dve", lambda e, uv=uv, sv=sv, c2=c2: e.tensor_tensor(out=uv[:, c2], in0=y2, in1=sv[:, c2], op=ALU.mult),
                         reads=[t_y2, stl], writes=[ut])
                stq("sp", uT[b, jt, :, 0:TL], uv, ut, [d_u[b]])
            if with_ctx:
                C2, t_C2 = A.alloc("C2", [2, 256], BF16)
                S2, t_S2 = A.alloc("S2", [2, 256], BF16)
                ld("sp", C2, CN["C256"], t_C2)
                ld("sp", S2, CN["S256n"], t_S2)
                Pc, t_Pc = A.alloc("Pc", [8, 2, 128], BF16)
                Qc, t_Qc = A.alloc("Qc", [8, 2, 128], BF16)
                sgc, t_sgc = A.alloc("sgc", [8, 256], BF16)
                uc, t_uc = A.alloc("uc", [8, 256], BF16)
                for b in nb_list:
                  for hf_ in range(2):
                    js = slice(hf_ * 8, hf_ * 8 + 8)
                    for c in range(2):
                        ld("sp", Pc[:, :, c, :], Pd[b, js, TL + c * 128:TL + (c + 1) * 128, :].rearrange("j p e -> p j e"), t_Pc, reads=[d_P[b]])
                        ld("sp", Qc[:, :, c, :], Qd[b, js, TL + c * 128:TL + (c + 1) * 128, :].rearrange("j p e -> p j e"), t_Qc, reads=[d_Q[b]])
                    ld("sp", sgc, sgT[b, js, :, TL:TA].rearrange("j p t -> p j t"), t_sgc, reads=[d_sg[b]])
                    for jt in range(8):
                        pa, ta = psA.next()

                        def f(e, pa=pa, jt=jt):
                            for c in range(2):
                                e.matmul(pa[:, 0:256], lhsT=Pc[:, jt, c, :], rhs=C2[:, c, :], start=(c == 0), stop=False)
                                ins = e.matmul(pa[:, 0:256], lhsT=Qc[:, jt, c, :], rhs=S2[:, c, :], start=False, stop=(c == 1))
                            return ins
                        P.op("pe", f, reads=[t_Pc, t_Qc, t_C2, t_S2], writes=[ta])
                        P.op("dve", lambda e, pa=pa, jt=jt: e.tensor_tensor(out=uc[:, jt, :], in0=pa[:, 0:256], in1=sgc[:, jt, :], op=ALU.mult),
                             reads=[ta, t_sgc], writes=[t_uc])
                    stq("sp", uT[b, js, :, TL:TA].rearrange("j p t -> p j t"), uc, t_uc, [d_u[b]])
            phase_end()
            out_phase(l, fwo[j], blocks, xsrc, wkey="fwo%d" % j)

        def attn_layer(l, blocks, xsrc):
            norm_phase(l, blocks, xsrc, pf=(("awq", 0), awi[:, 0:1024], 1024))
            rc, t_rc = A.alloc("rc", [2048], F32)
            rsn, t_rsn = A.alloc("rsn", [2048], F32)
            ld("sp", rc, CN["ropec"], t_rc)
            ld("sp", rsn, CN["ropes"], t_rsn)
            stg = Rot([A.alloc("aqs%d" % i, [8, 512], BF16) for i in range(2)])
            sqh = Rot([A.alloc("asq%d" % i, [512], BF16) for i in range(2)])
            rsv = Rot([A.alloc("ars%d" % i, [512], F32) for i in range(2)])
            qnb = Rot([A.alloc("aqn%d" % i, [512], BF16) for i in range(2)])
            t1v = Rot([A.alloc("at1%d" % i, [512], F32) for i in range(2)])
            t2v = Rot([A.alloc("at2%d" % i, [512], F32) for i in range(2)])
            ps2 = Rot(banks[4:6])
            ps3 = Rot(banks[6:8])
            cur = {}

            def evac(i, b, t0, nt, jn, j, nj, ps, pst):
                if j == 0:
                    cur["s"] = stg.next()
                sv, stl = cur["s"]
                gcol = qks[:, 0:1] if jn < 16 else qks[:, 1:2]
                sqv, sqt = sqh.next()
                P.op("act", lambda e: e.activation(out=sqv[:, :nt], in_=ps, func=AF.Square), reads=[pst], writes=[sqt])
                p2, tp2 = ps2.next()
                P.op("pe", lambda e: e.matmul(p2[:, :nt], lhsT=onesb, rhs=sqv[:, :nt], start=True, stop=True), reads=[sqt, t_onesb], writes=[tp2])
                rv, rt = rsv.next()
                P.op("act", lambda e: e.activation(out=rv[:, :nt], in_=p2[:, :nt], func=AF.Ln, scale=1.0 / 128, bias=epsc),
                     reads=[tp2, t_eps], writes=[rt])
                P.op("act", lambda e: e.activation(out=rv[:, :nt], in_=rv[:, :nt], func=AF.Exp, scale=-0.5), reads=[rt], writes=[rt])
                if t0 >= TL:
                    P.op("dve", lambda e: e.scalar_tensor_tensor(out=sv[:, j, :nt], in0=ps, scalar=gcol, in1=rv[:, :nt], op0=ALU.mult, op1=ALU.mult),
                         reads=[pst, rt, t_qks], writes=[stl])
                else:
                    qv, qt = qnb.next()
                    P.op("dve", lambda e: e.scalar_tensor_tensor(out=qv[:, :nt], in0=ps, scalar=gcol, in1=rv[:, :nt], op0=ALU.mult, op1=ALU.mult),
                         reads=[pst, rt, t_qks], writes=[qt])
                    p3, tp3 = ps3.next()
                    P.op("pe", lambda e: e.matmul(p3[:, :nt], lhsT=Rm, rhs=qv[:, :nt], start=True, stop=True), reads=[qt, t_m128], writes=[tp3])
                    a1, ta1 = t1v.next()
                    a2, ta2 = t2v.next()
                    P.op("pool", lambda e: e.tensor_tensor(out=a1[:, :nt], in0=qv[:, :nt], in1=rc[:, t0:t0 + nt], op=ALU.mult),
                         reads=[qt, t_rc], writes=[ta1])
                    P.op("dve", lambda e: e.tensor_tensor(out=a2[:, :nt], in0=p3[:, :nt], in1=rsn[:, t0:t0 + nt], op=ALU.mult),
                         reads=[tp3, t_rsn], writes=[ta2])
                    P.op("pool", lambda e: e.tensor_tensor(out=sv[:, j, :nt], in0=a1[:, :nt], in1=a2[:, :nt], op=ALU.add),
                         reads=[ta1, ta2], writes=[stl])
                if j == nj - 1:
                    h0 = jn - j
                    if h0 < 16:
                        stq("sp", qTd[b, h0:h0 + nj, :, t0:t0 + nt].rearrange("j p t -> p j t"), sv[:, :nj, :nt], stl, [d_q[b]])
                    else:
                        stq("sp", kTd[b, h0 - 16:h0 - 16 + nj, :, t0:t0 + nt].rearrange("j p t -> p j t"), sv[:, :nj, :nt], stl, [d_k[b]])
            linear(hT, d_h, awi[:, 0:2560], 2560, "ws", blocks, evac, Rot(banks[0:4]), wkey="awq")
            phase_end()
            linear(hT, d_h, awi[:, 3072:5120], 2048, "ws", blocks, ws_simple(sgT, d_sg, AF.Silu), Rot(banks[0:4]))
            phase_end()
            linear(hT, d_h, awi[:, 2560:3072], 512, "as", blocks, as_simple(vvd, d_v, AF.Identity, 0, width=512), Rot(banks[0:4]))
            phase_end()
            prefetch(("awo", 0), awo[:, 0:1024], 1024)
            kTs = [A.alloc("kTs%d" % i, [TA], BF16) for i in range(2)]
            vs = [A.alloc("vs%d" % i, [18, 128], BF16) for i in range(2)]
            qs = Rot([A.alloc("qs%d" % i, [512], BF16) for i in range(2)])
            szs = Rot([A.alloc("szs%d" % i, [512], BF16) for i in range(2)])
            pTs = Rot([A.alloc("pT%d" % i, [512], BF16) for i in range(4)])
            rden, t_rden = A.alloc("rden", [512], F32)
            ot, t_ot = A.alloc("ot", [512], F32)
            usts = Rot([A.alloc("aus%d" % i, [512], BF16) for i in range(2)])
            psS = Rot(banks[0:3])
            psO = Rot(banks[3:5])
            psD = Rot(banks[5:7])
            sc = 128.0 ** -0.5

            def attn_block(b, h, t0, nt, kv_, kt_, vv_, vt_):
                chunks = list(range(18)) if t0 < TL else [16, 17]
                qv, qt = qs.next()
                zv, zt = szs.next()
                ld("sp", qv[:, :nt], qTd[b, h, :, t0:t0 + nt], qt, reads=[d_q[b]])
                ld("sp", zv[:, :nt], sgT[b, h, :, t0:t0 + nt], zt, reads=[d_sg[b]])
                po, tpo = psO.next()
                pd_, tpd = psD.next()
                n = len(chunks)
                pend = []

                def s_step(c):
                    pss, tps = psS.next()
                    pv, ptl = pTs.next()
                    P.op("pe", lambda e: e.matmul(pss[:, :nt], lhsT=kv_[:, c * 128:(c + 1) * 128], rhs=qv[:, :nt],
                                                  start=True, stop=True), reads=[kt_, qt], writes=[tps])
                    P.op("act", lambda e: e.activation(out=pv[:, :nt], in_=pss[:, :nt], func=AF.Exp, scale=sc),
                         reads=[tps], writes=[ptl])
                    pend.append((c, pv, ptl))

                def pv_step(c, pv, ptl, first, last):
                    def f(e):
                        e.matmul(po[:, :nt], lhsT=vv_[:, c, :], rhs=pv[:, :nt], start=first, stop=last)
                        return e.matmul(pd_[:, :nt], lhsT=onesb, rhs=pv[:, :nt], start=first, stop=last)
                    P.op("pe", f, reads=[vt_, ptl, t_onesb], writes=[tpo, tpd])
                for idx in range(n + 2):
                    if idx < n:
                        s_step(chunks[idx])
                    if idx >= 2:
                        c, pv, ptl = pend[idx - 2]
                        pv_step(c, pv, ptl, idx == 2, idx == n + 1)
                uv, ut = usts.next()
                P.op("dve", lambda e: e.reciprocal(out=rden[:, :nt], in_=pd_[:, :nt]), reads=[tpd], writes=[t_rden])
                P.op("dve", lambda e: e.tensor_tensor(out=ot[:, :nt], in0=po[:, :nt], in1=rden[:, :nt], op=ALU.mult),
                     reads=[tpo, t_rden], writes=[t_ot])
                P.op("dve", lambda e: e.tensor_tensor(out=uv[:, :nt], in0=ot[:, :nt], in1=zv[:, :nt], op=ALU.mult),
                     reads=[t_ot, zt], writes=[ut])
                stq("pool", uT[b, h, :, t0:t0 + nt], uv[:, :nt], ut, [d_u[b]])

            kvi = 0
            for b in sorted(set(b for (b, _, _) in blocks)):
                for kv in range(4):
                    kv_, kt_ = kTs[kvi % 2]
                    vv_, vt_ = vs[kvi % 2]
                    kvi += 1
                    ld("sp", kv_, kTd[b, kv, :, :], kt_, reads=[d_k[b]])
                    ld("sp", vv_, vvd[b, :, kv * 128:(kv + 1) * 128].rearrange("(c p) e -> p c e", p=128), vt_, reads=[d_v[b]])
                    for hh in range(4):
                        h = kv * 4 + hh
                        for (bb, t0, nt) in blocks:
                            if bb != b:
                                continue
                            attn_block(b, h, t0, nt, kv_, kt_, vv_, vt_)
            phase_end()
            out_phase(l, awo, blocks, xsrc, wkey="awo")

        def mlstm_layer(l, blocks, xsrc):
            norm_phase(l, blocks, xsrc, pf=(("mq", 0), mwi[:, 0:1024], 1024))
            evq = ws_simple(mqT, d_mq, AF.Identity, scale=128.0 ** -0.5)
            linear(hT, d_h, mwi[:, 0:1024], 1024, "ws", blocks, evq, Rot(banks[0:4]), wkey="mq")
            phase_end()
            linear(hT, d_h, mwi[:, 1024:2048], 1024, "ws", blocks, ws_simple(mkT, d_mkT, AF.Identity), Rot(banks[0:4]))
            phase_end()
            linear(hT, d_h, mwi[:, 1024:2048], 1024, "as", blocks, as_simple(mkd, d_mk, AF.Identity, 1024), Rot(banks[0:4]))
            phase_end()
            linear(hT, d_h, mwi[:, 2048:4096], 2048, "as", blocks, as_simple(mvd, d_mv, AF.Identity, 2048), Rot(banks[0:4]))
            phase_end()
            linear(hT, d_h, mwi[:, 4096:6144], 2048, "as", blocks, as_simple(mso, d_mso, AF.Sigmoid, 4096), Rot(banks[0:4]))
            phase_end()
            bgb, t_bgb = A.alloc("bgb", [32], F32)
            ld("sp", bgb, mbg[0:1, :].to_broadcast([128, 32]), t_bgb)
            linear(hT, d_h, mwi[:, 6144:6176], 32, "as", blocks, as_simple(mgd, d_mg, None, 6144, width=32, dt=F32, addt=(bgb, t_bgb)),
                   Rot(banks[0:4]), gsz=32)
            phase_end()
            linear(hT, d_h, mwi[:, 6176:8224], 2048, "as", blocks, as_simple(msz, d_msz, AF.Silu, 6176), Rot(banks[0:4]))
            phase_end()
            for b in sorted(set(b for (b, _, _) in blocks)):
                prefetch(("mwo", 0), mwo[:, 0:1024], 1024)
                mlstm_scan(b)
                mlstm_finish(b)
            out_phase(l, mwo, blocks, xsrc, wkey="mwo")

        def mlstm_scan(b):
            G, t_G = A.alloc("G", [18, 32], F32)
            ld("sp", G, mgd[b].rearrange("(c p) n -> p c n", p=128), t_G, reads=[d_mg[b]])
            E, t_E = A.alloc("E", [2, 144], F32)
            L, t_L = A.alloc("L", [2, 144], F32)
            Wt, t_W = A.alloc("Wt", [2, 18, 8], F32)
            RI, t_RI = A.alloc("RI", [2, 18, 8], F32)
            DC, t_DC = A.alloc("DC", [2, 18, 8], F32)
            pS, t_pS = banks[0]
            pT_, t_pT = banks[1]
            for d in range(2):
                P.op("act", lambda e, d=d: e.activation(out=E[:, d, :].rearrange("p (c h) -> p c h", c=18), in_=G[:, :, 16 * d + 8:16 * d + 16], func=AF.Exp, scale=-1.0),
                     reads=[t_G], writes=[t_E])
            P.op("act", lambda e: e.activation(out=L, in_=E, func=AF.Ln, bias=onec), reads=[t_E, t_onec], writes=[t_L])
            for d in range(2):
                P.op("pe", lambda e, d=d: e.matmul(pS[:, d * 144:(d + 1) * 144], lhsT=trid[d], rhs=L[:, d, :], start=True, stop=True),
                     reads=[t_L, t_f128], writes=[t_pS])
                P.op("pe", lambda e, d=d: e.matmul(pT_[:, d * 144:(d + 1) * 144], lhsT=onesf, rhs=L[:, d, :], start=True, stop=True),
                     reads=[t_L, t_f128], writes=[t_pT])
            for d in range(2):
                P.op("dve", lambda e, d=d: e.tensor_tensor(out=Wt[:, d, :, :], in0=G[:, :, 16 * d:16 * d + 8],
                                                          in1=pS[:, d * 144:(d + 1) * 144].rearrange("p (c h) -> p c h", c=18), op=ALU.subtract),
                     reads=[t_G, t_pS], writes=[t_W])
            P.op("act", lambda e: e.activation(out=Wt, in_=Wt, func=AF.Exp), reads=[t_W], writes=[t_W])
            P.op("act", lambda e: e.activation(out=RI, in_=pS[:, 0:288].rearrange("p (d c h) -> p d c h", d=2, c=18), func=AF.Exp, scale=-1.0),
                 reads=[t_pS], writes=[t_RI])
            P.op("act", lambda e: e.activation(out=DC, in_=pT_[:, 0:288].rearrange("p (d c h) -> p d c h", d=2, c=18), func=AF.Exp, scale=-1.0),
                 reads=[t_pT], writes=[t_DC])
            Cst = [[A.alloc("Cs%d%d" % (d, h), [257], F32) for h in range(8)] for d in range(2)]
            Cb = [[A.alloc("Cb%d%d" % (d, h), [258], BF16) for h in range(8)] for d in range(2)]
            qc = [[A.alloc("qc%d%d" % (d, i), [8, 128], BF16) for i in range(2)] for d in range(2)]
            kc = [[A.alloc("kc%d%d" % (d, i), [8, 128], BF16) for i in range(2)] for d in range(2)]
            kt = [[A.alloc("kt%d%d" % (d, i), [8, 128], BF16) for i in range(2)] for d in range(2)]
            ve = [[A.alloc("ve%d%d" % (d, i), [8, 258], BF16) for i in range(2)] for d in range(2)]
            kw = [A.alloc("kw%d" % d, [8, 128], BF16) for d in range(2)]
            stt = Rot([A.alloc("stt%d" % i, [128], BF16) for i in range(4)])
            fc = Rot([A.alloc("fc%d" % i, [8], F32) for i in range(4)])
            hst = [[A.alloc("hst%d%d" % (d, i), [8, 257], F32) for i in range(1)] for d in range(2)]
            hout = [A.alloc("hout%d" % d, [8, 256], F32) for d in range(2)]
            for d in range(2):
                for i in range(2):
                    vv_, vt_ = ve[d][i]
                    P.op("pool", lambda e, vv_=vv_: e.memset(vv_[:, :, 256:257], 1.0), writes=[vt_])
            order = [[16, 17] + list(range(16)), [17, 16] + list(range(15, -1, -1))]
            psSb = Rot([(banks[2][0][:, i * 128:(i + 1) * 128], None) for i in range(4)])
            psSt = banks[2][1]
            psS2 = Rot([(banks[3][0][:, i * 128:(i + 1) * 128], None) for i in range(4)])
            psS2t = banks[3][1]
            psAr = Rot(banks[4:6])
            psCr = Rot(banks[6:8])

            def loads(d, si):
                c = order[d][si]
                tk = slice(c * 128, (c + 1) * 128)
                i = si % 2
                ld("sp", qc[d][i][0], mqT[b, :, :, tk].rearrange("h p t -> p h t"), qc[d][i][1], reads=[d_mq[b]])
                ld("sp", kc[d][i][0], mkT[b, :, :, tk].rearrange("h p t -> p h t"), kc[d][i][1], reads=[d_mkT[b]])
                ld("sp", kt[d][i][0], mkd[b, tk, :].rearrange("t (h e) -> t h e", h=8), kt[d][i][1], reads=[d_mk[b]])
                ld("sp", ve[d][i][0][:, :, 0:256], mvd[b, tk, :].rearrange("t (h e) -> t h e", h=8), ve[d][i][1], reads=[d_mv[b]])
            for d in range(2):
                loads(d, 0)
            for si in range(18):
                for d in range(2):
                    if si + 1 < 18:
                        loads(d, si + 1)
                    c = order[d][si]
                    i = si % 2
                    first = si == 0
                    last = si == 17
                    cn = order[d][si + 1] if not last else None
                    qv, qt = qc[d][i]
                    kv_, kt_ = kc[d][i]
                    ktv, ktt = kt[d][i]
                    vv_, vt_ = ve[d][i]
                    kwv, kwt = kw[d]
                    hv, ht = hst[d][0]
                    P.op("dve", lambda e, d=d, c=c, kwv=kwv, ktv=ktv: e.tensor_tensor(
                        out=kwv, in0=ktv, in1=Wt[:, d, c, :].unsqueeze(2).to_broadcast([128, 8, 128]), op=ALU.mult),
                        reads=[ktt, t_W], writes=[kwt])
                    for h in range(8):
                        if h < 4:
                            pss, tps = psSb.next()[0], psSt
                        else:
                            pss, tps = psS2.next()[0], psS2t
                        P.op("pe", lambda e, pss=pss, kv_=kv_, qv=qv, h=h: e.matmul(pss, lhsT=kv_[:, h, :], rhs=qv[:, h, :], start=True, stop=True),
                             reads=[kt_, qt], writes=[tps])
                        sv, stl = stt.next()
                        P.op("dve", lambda e, sv=sv, pss=pss, d=d, c=c, h=h: e.scalar_tensor_tensor(
                            out=sv, in0=pss, scalar=Wt[:, d, c, h:h + 1], in1=maskd[d], op0=ALU.mult, op1=ALU.mult),
                            reads=[tps, t_W, t_m128], writes=[stl])
                        pa, tpa = psAr.next()
                        cbv, cbt = Cb[d][h]

                        def f(e, pa=pa, qv=qv, h=h, cbv=cbv, sv=sv, vv_=vv_, first=first):
                            if not first:
                                e.matmul(pa[:, 0:257], lhsT=qv[:, h, :], rhs=cbv[:, 0:257], start=True, stop=False)
                            return e.matmul(pa[:, 0:257], lhsT=sv, rhs=vv_[:, h, 0:257], start=first, stop=True)
                        P.op("pe", f, reads=[qt, cbt, stl, vt_], writes=[tpa])
                        pc, tpc = psCr.next()
                        P.op("pe", lambda e, pc=pc, kwv=kwv, vv_=vv_, h=h: e.matmul(pc[:, 0:257], lhsT=kwv[:, h, :], rhs=vv_[:, h, 0:257], start=True, stop=True),
                             reads=[kwt, vt_], writes=[tpc])
                        csv, cst = Cst[d][h]
                        if first:
                            P.op("dve", lambda e, csv=csv, pc=pc: e.tensor_copy(out=csv, in_=pc[:, 0:257]), reads=[tpc], writes=[cst])
                        else:
                            P.op("dve", lambda e, csv=csv, pc=pc, d=d, c=c, h=h: e.scalar_tensor_tensor(
                                out=csv, in0=csv, scalar=DC[:, d, c, h:h + 1], in1=pc[:, 0:257], op0=ALU.mult, op1=ALU.add),
                                reads=[tpc, cst, t_DC], writes=[cst])
                        if not last:
                            P.op("act", lambda e, cbv=cbv, csv=csv, d=d, cn=cn, h=h: e.activation(
                                out=cbv[:, 0:257], in_=csv, func=AF.Identity, scale=DC[:, d, cn, h:h + 1]), reads=[cst, t_DC], writes=[cbt])
                        P.op("act", lambda e, hv=hv, pa=pa, h=h: e.activation(out=hv[:, h, :], in_=pa[:, 0:257], func=AF.Identity),
                             reads=[tpa], writes=[ht])
                    fv, ft = fc.next()
                    f2, ft2 = fc.next()
                    P.op("dve", lambda e, fv=fv, hv=hv: e.tensor_scalar(out=fv, in0=hv[:, :, 256], scalar1=-1.0, scalar2=None, op0=ALU.mult),
                         reads=[ht], writes=[ft])
                    P.op("dve", lambda e, fv=fv, hv=hv: e.tensor_tensor(out=fv, in0=fv, in1=hv[:, :, 256], op=ALU.max), reads=[ht, ft], writes=[ft])
                    P.op("dve", lambda e, fv=fv, d=d, c=c: e.tensor_tensor(out=fv, in0=fv, in1=RI[:, d, c, :], op=ALU.max), reads=[ft, t_RI], writes=[ft])
                    P.op("dve", lambda e, fv=fv, f2=f2: e.reciprocal(out=f2, in_=fv), reads=[ft], writes=[ft2])
                    ho, hot = hout[d]
                    P.op("dve", lambda e, ho=ho, hv=hv, f2=f2: e.tensor_tensor(out=ho, in0=hv[:, :, 0:256],
                                                                              in1=f2.unsqueeze(2).to_broadcast([128, 8, 256]), op=ALU.mult),
                         reads=[ht, ft2], writes=[hot])
                    stq("sp", mhd[b, d, c * 128:(c + 1) * 128, :], ho.rearrange("p h e -> p (h e)"), hot, [d_mh[b]])
            phase_end()

        def mlstm_finish(b):
            hnb, t_hnb = A.alloc("hnb", [2048], F32)
            ld("sp", hnb, mhn[0:1, :].to_broadcast([128, 2048]), t_hnb)
            hf = [A.alloc("hf%d" % i, [2048], F32) for i in range(2)]
            hbk = [A.alloc("hbk%d" % i, [2048], F32) for i in range(2)]
            so = [A.alloc("so%d" % i, [2048], BF16) for i in range(2)]
            sz = [A.alloc("sz%d" % i, [2048], BF16) for i in range(2)]
            y, t_y = A.alloc("fy", [8, 256], F32)
            sq, t_sq = A.alloc("fsq", [8, 256], F32)
            ss, t_ss = A.alloc("fss", [8], F32)
            ub, t_ub = A.alloc("fub", [2048], BF16)
            uts = [A.alloc("uts%d" % i, [16, 128], BF16) for i in range(2)]
            psT = [(banks[0][0].bitcast(BF16), banks[0][1]), (banks[1][0].bitcast(BF16), banks[1][1])]

            def loads(c):
                tk = slice(c * 128, (c + 1) * 128)
                i = c % 2
                ld("sp", hf[i][0], mhd[b, 0, tk, :], hf[i][1], reads=[d_mh[b]])
                ld("sp", hbk[i][0], mhd[b, 1, tk, :], hbk[i][1], reads=[d_mh[b]])
                ld("sp", so[i][0], mso[b, tk, :], so[i][1], reads=[d_mso[b]])
                ld("sp", sz[i][0], msz[b, tk, :], sz[i][1], reads=[d_msz[b]])
            loads(0)
            for c in range(18):
                if c + 1 < 18:
                    loads(c + 1)
                i = c % 2
                hfv, hft = hf[i]
                hbv, hbt = hbk[i]
                sov, sot = so[i]
                szv, szt = sz[i]
                yf = y.rearrange("p h e -> p (h e)")
                P.op("pool", lambda e, hfv=hfv, hbv=hbv: e.tensor_tensor(out=hfv, in0=hfv, in1=hbv, op=ALU.add), reads=[hbt, hft], writes=[hft])
                P.op("dve", lambda e, hfv=hfv, sov=sov: e.tensor_tensor(out=yf, in0=hfv, in1=sov, op=ALU.mult), reads=[hft, sot], writes=[t_y])
                P.op("pool", lambda e: e.tensor_tensor(out=sq, in0=y, in1=y, op=ALU.mult), reads=[t_y], writes=[t_sq])
                P.op("dve", lambda e: e.tensor_reduce(out=ss, in_=sq, axis=AX.X, op=ALU.add), reads=[t_sq], writes=[t_ss])
                P.op("act", lambda e: e.activation(out=ss, in_=ss, func=AF.Ln, scale=1.0 / 256, bias=epsc), reads=[t_ss, t_eps], writes=[t_ss])
                P.op("act", lambda e: e.activation(out=ss, in_=ss, func=AF.Exp, scale=-0.5), reads=[t_ss], writes=[t_ss])
                P.op("dve", lambda e: e.tensor_tensor(out=y, in0=y, in1=ss.unsqueeze(2).to_broadcast([128, 8, 256]), op=ALU.mult),
                     reads=[t_y, t_ss], writes=[t_y])
                P.op("pool", lambda e: e.tensor_tensor(out=yf, in0=yf, in1=hnb, op=ALU.mult), reads=[t_y, t_hnb], writes=[t_y])
                P.op("dve", lambda e, szv=szv: e.tensor_tensor(out=ub, in0=yf, in1=szv, op=ALU.mult), reads=[t_y, szt], writes=[t_ub])
                uv, ut = uts[i]
                for half in range(2):
                    pt_, tpt = psT[half]

                    def f(e, pt_=pt_, half=half):
                        for jj in range(8):
                            j = half * 8 + jj
                            ins = e.transpose(out=pt_[:, jj * 128:(jj + 1) * 128], in_=ub[:, j * 128:(j + 1) * 128], identity=ident)
                        return ins
                    P.op("pe", f, reads=[t_ub, t_m128], writes=[tpt])
                    P.op("act", lambda e, uv=uv, pt_=pt_, half=half: e.activation(
                        out=uv[:, half * 8:(half + 1) * 8, :], in_=pt_[:, 0:1024].rearrange("p (j t) -> p j t", j=8), func=AF.Identity),
                        reads=[tpt], writes=[ut])
                stq("sp", uT[b, :, :, c * 128:(c + 1) * 128].rearrange("j p t -> p j t"), uv, ut, [d_u[b]])
            phase_end()

        blocks_all = [(b, t0, 512) for b in range(NB) for t0 in range(0, TL, 512)] + [(b, TL, 256) for b in range(NB)]
        blocks_all.sort(key=lambda x: (x[0], x[1]))
        blocks_lat = [(b, t0, 512) for b in range(NB) for t0 in range(0, TL, 512)]
        mod_phase()
        xsrc = xT
        for l in range(nlayers):
            kind = l % 3
            if kind == 0:
                last_no_ctx = (l == 3)
                fnet_layer(l, l // 3, blocks_lat if last_no_ctx else blocks_all, not last_no_ctx, xsrc)
            elif kind == 1:
                mlstm_layer(l, blocks_all, xsrc)
            else:
                attn_layer(l, blocks_all, xsrc)
            xsrc = xr
        norm_phase(0, blocks_lat, xsrc, final=True)
        P.op("sp", lambda e: None, reads=[d_out])
        print("ops:", {e: len(P.ops[e]) for e in ENGS}, "dma slots:", len(P.slots))
        P.emit(st)
    return nc


_CACHE = {}


def prep_inputs(inp, core):
    b0 = 2 * core
    f32 = np.float32
    x = inp["x"]
    ctx = inp["ctx"]
    xt = np.empty((NB, D, TA), f32)
    for i in range(NB):
        xt[i, :, :TL] = x[b0 + i].T
        xt[i, :, TL:] = ctx[b0 + i].T
    m = {"xT": xt.reshape(NB, 16, 128, TA)}
    cc = np.stack([inp["c"][b0], inp["c"][b0 + 1], inp["c_ctx"]], axis=1).astype(f32)
    m["cT"] = np.ascontiguousarray(cc.reshape(16, 128, 3).transpose(1, 0, 2))
    return m


def shared_inputs(inp):
    f32 = np.float32
    m = {}
    m["ada_w"] = np.ascontiguousarray(inp["ada_w"], dtype=f32)
    m["ada_b"] = np.ascontiguousarray(inp["ada_b"], dtype=f32)
    g = np.concatenate([inp["norm_g"], inp["final_g"][None]], axis=0).astype(f32)
    m["gT"] = np.ascontiguousarray(g.reshape(5, 16, 128).transpose(2, 0, 1))
    m["fnet_w_gate"] = np.ascontiguousarray(inp["fnet_w_gate"], dtype=f32)
    m["fnet_w_out"] = np.ascontiguousarray(inp["fnet_w_out"], dtype=f32)
    m["mlstm_w_in"] = np.ascontiguousarray(inp["mlstm_w_in"][0], dtype=f32)
    m["mlstm_b_gate"] = np.ascontiguousarray(inp["mlstm_b_gate"], dtype=f32)
    m["mlstm_hn"] = np.ascontiguousarray(inp["mlstm_hn"], dtype=f32)
    m["mlstm_w_out"] = np.ascontiguousarray(inp["mlstm_w_out"][0], dtype=f32)
    m["attn_w_in"] = np.ascontiguousarray(inp["attn_w_in"][0], dtype=f32)
    m["attn_qk"] = np.ascontiguousarray(np.stack([inp["attn_qn"][0], inp["attn_kn"][0]], axis=1), dtype=f32)
    m["attn_w_out"] = np.ascontiguousarray(inp["attn_w_out"][0], dtype=f32)
    for k, v in make_consts().items():
        m["c_" + k] = v
    return m


def kernel(**inputs):
    inp = {k: np.asarray(v) for k, v in inputs.items()}
    ncores = 8
    if "nc" not in _CACHE:
        _CACHE["nc"] = build()
    nc = _CACHE["nc"]
    sh = shared_inputs(inp)
    in_maps = []
    for c in range(ncores):
        m = dict(sh)
        m.update(prep_inputs(inp, c))
        in_maps.append(m)
    res = run_bass_kernel_spmd(nc, in_maps, core_ids=list(range(ncores)))
    out = np.empty((16, TL, D), np.float32)
    for c in range(ncores):
        o = np.asarray(res.results[c]["outT"]).reshape(NB, D, TL)
        for i in range(NB):
            out[2 * c + i] = o[i].T
    return out
```

```python
import math
import numpy as np
import ml_dtypes
from contextlib import ExitStack
import concourse.bass as bass
import concourse.mybir as mybir
from concourse.bass_utils import run_bass_kernel_spmd

F32 = mybir.dt.float32
BF16 = mybir.dt.bfloat16
AF = mybir.ActivationFunctionType
ALU = mybir.AluOpType
AX = mybir.AxisListType
NPBF = ml_dtypes.bfloat16

D = 2048
TL = 2048
TC = 256
TA = TL + TC
NB = 2
EPS = 1e-6
ENGS = ("sp", "act", "dve", "pool", "pe")


class Slot:
    __slots__ = ("sem", "cnt")

    def __init__(self):
        self.sem = None
        self.cnt = 0


class DT:
    __slots__ = ("name", "writers", "readers", "multi", "slot", "last_dma")

    def __init__(self, name, multi=False, fence=()):
        self.name = name
        self.writers = list(fence)
        self.readers = []
        self.multi = multi
        self.slot = None
        self.last_dma = None


class Op:
    __slots__ = ("eng", "fn", "deps", "is_dma", "sig", "need")

    def __init__(self, eng, fn, is_dma):
        self.eng = eng
        self.fn = fn
        self.deps = []
        self.is_dma = is_dma
        self.sig = None
        self.need = False


class Prog:
    def __init__(self, nc):
        self.nc = nc
        self.ops = {e: [] for e in ENGS}
        self.slots = []
        self.free_slots = []
        self.fence = []
        self.live = []

    def tile(self, name, multi=False, phase=True):
        t = DT(name, multi, self.fence)
        if phase:
            self.live.append(t)
        return t

    def _track(self, op, reads, writes):
        deps = op.deps
        for t in reads:
            deps.extend(t.writers)
            t.readers.append(op)
        for t in writes:
            if t.multi:
                if t.readers:
                    deps.extend(t.readers)
                    t.readers = []
                    t.writers = [op]
                else:
                    t.writers.append(op)
            else:
                deps.extend(t.readers)
                deps.extend(t.writers)
                t.readers = []
                t.writers = [op]

    def op(self, eng, fn, reads=(), writes=()):
        o = Op(eng, fn, False)
        self._track(o, reads, writes)
        o.deps = [d for d in o.deps if d is not o]
        self.ops[eng].append(o)
        return o

    def dma(self, eng, fn, n, st, reads=(), writes=()):
        o = Op(eng, fn, True)
        self._track(o, reads, writes)
        o.deps = [d for d in o.deps if d is not o]
        if st.last_dma is not None:
            o.deps.append(st.last_dma)
        st.last_dma = o
        if st.slot is None:
            if self.free_slots:
                st.slot = self.free_slots.pop()
            else:
                st.slot = Slot()
                self.slots.append(st.slot)
        st.slot.cnt += 16 * n
        o.sig = (st.slot, st.slot.cnt)
        self.ops[eng].append(o)
        return o

    def barrier(self, fn):
        o = self.op("dve", fn, writes=self.live)
        for t in self.live:
            if t.slot is not None:
                self.free_slots.append(t.slot)
        self.live = []
        self.fence = [o]
        return o

    def emit(self, stack):
        nc = self.nc
        for e in ENGS:
            for o in self.ops[e]:
                for d in o.deps:
                    if not d.is_dma:
                        if d.eng == "pe" and o.eng == "pe" and not o.is_dma:
                            continue
                        d.need = True
        engsem = {}
        for e in ENGS:
            if e != "sp":
                engsem[e] = stack.enter_context(nc.semaphore("eng_" + e))
        for i, t in enumerate(self.slots):
            t.sem = stack.enter_context(nc.semaphore("dsl%d" % i))
        for e in ENGS:
            c = 0
            for o in self.ops[e]:
                if o.is_dma:
                    continue
                if o.need:
                    c += 1
                    o.sig = (e, c)
        block = stack.enter_context(nc.Block())
        prog = self

        def run(e, eng):
            waited = {}
            for o in prog.ops[e]:
                req = {}
                for d in o.deps:
                    if d.sig is None:
                        continue
                    if (not d.is_dma) and d.eng == "pe" and e == "pe" and not o.is_dma:
                        continue
                    k, v = d.sig
                    if req.get(k, 0) < v:
                        req[k] = v
                for k, v in req.items():
                    if waited.get(k, 0) >= v:
                        continue
                    waited[k] = v
                    sem = engsem[k] if isinstance(k, str) else k.sem
                    eng.wait_ge(sem, v)
                if o.is_dma:
                    o.fn(eng, o.sig[0].sem)
                else:
                    ins = o.fn(eng)
                    if o.need:
                        ins.then_inc(engsem[e], 1)

        @block.sync
        def _(eng):
            run("sp", eng)

        @block.scalar
        def _(eng):
            run("act", eng)

        @block.vector
        def _(eng):
            run("dve", eng)

        @block.gpsimd
        def _(eng):
            run("pool", eng)

        @block.tensor
        def _(eng):
            run("pe", eng)


class Arena:
    def __init__(self, ap, P, base=0):
        self.ap = ap
        self.P = P
        self.off = base
        self.base = base
        self.cap = ap.shape[1]

    def reset(self):
        self.off = self.base

    def alloc(self, name, shape, dtype, phase=True):
        n = int(np.prod(shape))
        words = n if dtype == F32 else (n + 1) // 2
        assert self.off + words <= self.cap, (name, self.off, words, self.cap)
        v = self.ap[:, self.off:self.off + words]
        if dtype != F32:
            v = v.bitcast(dtype)
            if n % 2:
                v = v[:, 0:n]
        self.off += words
        if len(shape) == 2:
            v = v.rearrange("p (a b) -> p a b", a=shape[0])
        elif len(shape) == 3:
            v = v.rearrange("p (a b c) -> p a b c", a=shape[0], b=shape[1])
        elif len(shape) == 4:
            v = v.rearrange("p (a b c d) -> p a b c d", a=shape[0], b=shape[1], c=shape[2])
        return v, self.P.tile(name, phase=phase)


class Rot:
    def __init__(self, items):
        self.items = items
        self.i = 0

    def next(self):
        r = self.items[self.i % len(self.items)]
        self.i += 1
        return r


def make_consts():
    c = {}
    p = np.arange(128)
    d = (np.arange(4)[None, :] * 128 + p[:, None]).astype(np.float64)
    e = np.arange(512, dtype=np.float64)
    ang = 2 * np.pi * d[:, :, None] * e[None, None, :] / 512.0
    c["CD"] = np.cos(ang).astype(NPBF)
    c["SD"] = np.sin(ang).astype(NPBF)
    t = (256 * np.arange(8)[None, None, :] + 2 * p[:, None, None] + np.arange(2)[None, :, None]).astype(np.float64)
    tt = np.arange(1024, dtype=np.float64)
    tm = np.mod(t[:, :, :, None] * tt[None, None, None, :], 2048.0)
    ang = 2 * np.pi * tm / 2048.0
    c["CT"] = np.cos(ang).astype(NPBF)
    c["STn"] = (-np.sin(ang)).astype(NPBF)
    t = (128 * np.arange(2)[None, :] + p[:, None]).astype(np.float64)
    tt = np.arange(256, dtype=np.float64)
    ang = 2 * np.pi * np.mod(t[:, :, None] * tt[None, None, :], 256.0) / 256.0
    c["C256"] = np.cos(ang).astype(NPBF)
    c["S256n"] = (-np.sin(ang)).astype(NPBF)
    rows = TL // 64
    r = np.repeat(np.arange(rows), 64).astype(np.float32)
    col = np.tile(np.arange(64), rows).astype(np.float32)
    freqs = (np.float32(10000.0) ** (-np.arange(0, 64, 2, dtype=np.float32) / np.float32(64))).astype(np.float32)
    angt = np.concatenate([r[:, None] * freqs, col[:, None] * freqs], axis=-1).astype(np.float32)
    cosT = np.repeat(np.cos(angt).T, 2, axis=0)
    sinT = np.repeat(np.sin(angt).T, 2, axis=0)
    c["ropec"] = np.ascontiguousarray(cosT).astype(np.float32)
    c["ropes"] = np.ascontiguousarray(sinT).astype(np.float32)
    ident = np.eye(128, dtype=np.float32)
    Rm = np.zeros((128, 128), np.float32)
    for i in range(64):
        Rm[2 * i + 1, 2 * i] = -1.0
        Rm[2 * i, 2 * i + 1] = 1.0
    s = np.arange(128)[:, None]
    tq = np.arange(128)[None, :]
    mf = (s <= tq).astype(np.float32)
    mb = (s >= tq).astype(np.float32)
    c["m128"] = np.stack([ident, Rm, mf, mb], axis=1).astype(NPBF)
    c["f128"] = np.stack([(s > tq).astype(np.float32), (s < tq).astype(np.float32), np.ones((128, 128), np.float32)], axis=1)
    return c


CONST_SPECS = [("CD", [128, 4, 512], BF16), ("SD", [128, 4, 512], BF16), ("CT", [128, 2, 8, 1024], BF16),
               ("STn", [128, 2, 8, 1024], BF16), ("C256", [128, 2, 256], BF16), ("S256n", [128, 2, 256], BF16),
               ("ropec", [128, 2048], F32), ("ropes", [128, 2048], F32), ("m128", [128, 4, 128], BF16),
               ("f128", [128, 3, 128], F32)]


def build(nlayers=4, dump=()):
    nc = bass.Bass("TRN2", target_bir_lowering=False)

    def din(name, shape, dt=F32):
        return nc.dram_tensor(name, shape, dt, kind="ExternalInput").ap()

    kinds = set(l % 3 for l in range(nlayers))
    need = {"xr": nlayers > 0, "hT": nlayers > 0, "uT": nlayers > 0, "sgT": bool(kinds & {0, 2}),
            "Pd": 0 in kinds, "Qd": 0 in kinds, "qTd": 2 in kinds, "kTd": 2 in kinds, "vvd": 2 in kinds}

    def dsc(name, shape, dt):
        if not need.get(name, 1 in kinds):
            shape = [1] * (len(shape) - 1) + [16]
        if name in dump:
            return nc.dram_tensor(name, shape, dt, kind="ExternalOutput").ap()
        return nc.dram_tensor(name, shape, dt).ap()

    xT = din("xT", [NB, 16, 128, TA])
    cT = din("cT", [128, 16, 3])
    ada_w = din("ada_w", [4, 2048, 6144])
    ada_b = din("ada_b", [4, 6144])
    gT = din("gT", [128, 5, 16])
    fwg = din("fnet_w_gate", [2, 2048, 2048])
    fwo = din("fnet_w_out", [2, 2048, 2048])
    mwi = din("mlstm_w_in", [2048, 8224])
    mbg = din("mlstm_b_gate", [1, 32])
    mhn = din("mlstm_hn", [1, 2048])
    mwo = din("mlstm_w_out", [2048, 2048])
    awi = din("attn_w_in", [2048, 5120])
    aqk = din("attn_qk", [128, 2])
    awo = din("attn_w_out", [2048, 2048])
    CN = {n: din("c_" + n, s, dt) for n, s, dt in CONST_SPECS}
    outT = nc.dram_tensor("outT", [NB, 16, 128, TL], F32, kind="ExternalOutput").ap()

    xr = dsc("xr", [NB, 16, 128, TA], F32)
    hT = dsc("hT", [NB, 16, 128, TA], BF16)
    uT = dsc("uT", [NB, 16, 128, TA], BF16)
    sgT = dsc("sgT", [NB, 16, 128, TA], BF16)
    Pd = dsc("Pd", [NB, 16, TA, 128], BF16)
    Qd = dsc("Qd", [NB, 16, TA, 128], BF16)
    qTd = dsc("qTd", [NB, 16, 128, TA], BF16)
    kTd = dsc("kTd", [NB, 4, 128, TA], BF16)
    vvd = dsc("vvd", [NB, TA, 512], BF16)
    mqT = dsc("mqT", [NB, 8, 128, TA], BF16)
    mkT = dsc("mkT", [NB, 8, 128, TA], BF16)
    mkd = dsc("mkd", [NB, TA, 1024], BF16)
    mvd = dsc("mvd", [NB, TA, 2048], BF16)
    mso = dsc("mso", [NB, TA, 2048], BF16)
    msz = dsc("msz", [NB, TA, 2048], BF16)
    mgd = dsc("mgd", [NB, TA, 32], F32)
    mhd = dsc("mhd", [NB, 2, TA, 2048], F32)

    st = ExitStack()
    with st:
        P = Prog(nc)
        sb = st.enter_context(nc.sbuf_tensor("arena", [128, 51200], F32))
        PA = Arena(sb[:], P)
        banks = []
        for i in range(8):
            pt = st.enter_context(nc.psum_tensor("ps%d" % i, [128, 512], F32))
            banks.append((pt[:], P.tile("psb%d" % i, phase=False)))

        def dtile(name):
            return [P.tile(name + str(b), multi=True, phase=False) for b in range(NB)]
        d_x, d_h, d_u, d_sg, d_P, d_Q = dtile("x"), dtile("h"), dtile("u"), dtile("sg"), dtile("P"), dtile("Q")
        d_q, d_k, d_v = dtile("q"), dtile("k"), dtile("v")
        d_mq, d_mkT, d_mk, d_mv, d_mso, d_msz, d_mg, d_mh = (dtile("mq"), dtile("mkT"), dtile("mk"), dtile("mv"),
                                                            dtile("mso"), dtile("msz"), dtile("mg"), dtile("mh"))
        d_out = P.tile("out", multi=True, phase=False)

        m128, t_m128 = PA.alloc("m128", [4, 128], BF16, phase=False)
        f128, t_f128 = PA.alloc("f128", [3, 128], F32, phase=False)
        onesb, t_onesb = PA.alloc("onesb", [128], BF16, phase=False)
        scT, t_scT = PA.alloc("scT", [16, 3], BF16, phase=False)
        cTs, t_cTs = PA.alloc("cTs", [16, 3], F32, phase=False)
        gTs, t_gTs = PA.alloc("gTs", [5, 16], F32, phase=False)
        mod, t_mod = PA.alloc("mod", [4, 48, 3], F32, phase=False)
        amod, t_amod = PA.alloc("amod", [4, 16, 3], F32, phase=False)
        qks, t_qks = PA.alloc("qks", [2], F32, phase=False)
        scr, t_scr = PA.alloc("scr", [4], F32, phase=False)
        epsc, t_eps = PA.alloc("epsc", [1], F32, phase=False)
        onec, t_onec = PA.alloc("onec", [1], F32, phase=False)
        wbufs = [PA.alloc("wbuf%d" % i, [16, 1024], BF16, phase=False) for i in range(2)]
        wstate = {"i": 0, "pf": {}}
        PA.base = PA.off
        A = PA
        ident = m128[:, 0, :]
        Rm = m128[:, 1, :]
        maskd = [m128[:, 2, :], m128[:, 3, :]]
        trid = [f128[:, 0, :], f128[:, 1, :]]
        onesf = f128[:, 2, :]

        def ld(eng, dst, src, tl, reads=(), n=1):
            def f(e, s):
                e.dma_start(out=dst, in_=src).then_inc(s, 16)
            P.dma(eng, f, 1, tl, reads=reads, writes=[tl])

        def stq(eng, dst, src, tl, dtiles):
            def f(e, s):
                e.dma_start(out=dst, in_=src).then_inc(s, 16)
            P.dma(eng, f, 1, tl, reads=[tl], writes=dtiles)

        def wload(key, src, gc):
            if key in wstate["pf"]:
                return wstate["pf"].pop(key)
            wv, wt = wbufs[wstate["i"] % 2]
            wstate["i"] += 1
            ld("pool", wv[:, :, :gc], src.rearrange("(k p) n -> p k n", p=128), wt)
            return wv, wt

        def prefetch(key, src, gc):
            if key not in wstate["pf"]:
                wstate["pf"][key] = wload(None, src, gc)

        ld("sp", m128, CN["m128"], t_m128)
        ld("sp", f128, CN["f128"], t_f128)
        ld("sp", cTs, cT, t_cTs)
        ld("sp", gTs, gT, t_gTs)
        ld("sp", qks, aqk, t_qks)
        P.op("dve", lambda e: e.memset(onesb, 1.0), writes=[t_onesb])
        P.op("dve", lambda e: e.memset(epsc, EPS), writes=[t_eps])
        P.op("dve", lambda e: e.memset(onec, 1.0), writes=[t_onec])
        P.op("act", lambda e: e.activation(out=scT, in_=cTs, func=AF.Silu), reads=[t_cTs], writes=[t_scT])

        def phase_end():
            P.barrier(lambda e: e.memset(scr, 0.0))
            A.reset()

        def mm16(out, lhs_fn, rhs_fn, nk=16):
            def f(e):
                for k in range(nk):
                    ins = e.matmul(out, lhsT=lhs_fn(k), rhs=rhs_fn(k), start=(k == 0), stop=(k == nk - 1))
                return ins
            return f

        def mod_phase():
            wbs = [A.alloc("mw%d" % i, [16, 512], BF16) for i in range(2)]
            adb, t_adb = A.alloc("adb", [6144], BF16)
            psm, t_psm = banks[0]
            for l in range(nlayers):
                def f(e, s, l=l):
                    e.dma_start(out=adb[0:1, :], in_=ada_b[l:l + 1, :]).then_inc(s, 16)
                P.dma("pool", f, 1, t_adb, writes=[t_adb])
                for nb in range(12):
                    wv, wt = wbs[nb % 2]
                    ld("pool", wv, ada_w[l, :, nb * 512:(nb + 1) * 512].rearrange("(k p) n -> p k n", p=128), wt)

                    def f(e, nb=nb, wv=wv):
                        for j in range(4):
                            nt_ = nb * 4 + j
                            o = psm[:, nt_ * 3:nt_ * 3 + 3]
                            for k in range(16):
                                e.matmul(o, lhsT=wv[:, k, j * 128:(j + 1) * 128], rhs=scT[:, k, :], start=(k == 0), stop=False)
                            ins = e.matmul(o, lhsT=adb[0:1, nt_ * 128:(nt_ + 1) * 128], rhs=onesb[0:1, 0:3], start=False, stop=True)
                        return ins
                    P.op("pe", f, reads=[wt, t_scT, t_adb, t_onesb], writes=[t_psm])
                P.op("act", lambda e, l=l: e.activation(out=mod[:, l, :, :], in_=psm[:, 0:144].rearrange("p (a b) -> p a b", a=48), func=AF.Identity),
                     reads=[t_psm], writes=[t_mod])
                P.op("dve", lambda e, l=l: e.tensor_scalar_add(out=amod[:, l, :, :], in0=mod[:, l, 16:32, :], scalar1=1.0),
                     reads=[t_mod], writes=[t_amod])
                P.op("dve", lambda e, l=l: e.tensor_tensor(out=amod[:, l, :, :], in0=amod[:, l, :, :],
                                                          in1=gTs[:, l, :].unsqueeze(2).to_broadcast([128, 16, 3]), op=ALU.mult),
                     reads=[t_gTs, t_amod], writes=[t_amod])
            phase_end()

        def norm_phase(l, blocks, xsrc, final=False, pf=None):
            if pf is not None:
                prefetch(*pf)
            xs = [A.alloc("nx%d" % i, [16, 512], F32) for i in range(2)]
            sq, t_sq = A.alloc("nsq", [16, 512], BF16)
            hb = [A.alloc("nh%d" % i, [16, 512], F32 if final else BF16) for i in range(1 if final else 2)]
            rs, t_rs = A.alloc("nrs", [512], F32)
            psn, t_psn = banks[1]

            def load(i):
                b, t0, nt = blocks[i]
                xv, xt = xs[i % 2]
                ld("sp", xv[:, :, :nt], xsrc[b, :, :, t0:t0 + nt].rearrange("i p t -> p i t"), xt, reads=[d_x[b]])
            load(0)
            for i, (b, t0, nt) in enumerate(blocks):
                if i + 1 < len(blocks):
                    load(i + 1)
                c = 2 if t0 >= TL else b
                xv, xt = xs[i % 2]
                hv, ht = hb[i % len(hb)]
                P.op("act", lambda e, xv=xv, nt=nt: e.activation(out=sq[:, :, :nt], in_=xv[:, :, :nt], func=AF.Square),
                     reads=[xt], writes=[t_sq])
                P.op("pe", mm16(psn[:, :nt], lambda k: onesb, lambda k, nt=nt: sq[:, k, :nt]), reads=[t_sq, t_onesb], writes=[t_psn])
                P.op("act", lambda e, nt=nt: e.activation(out=rs[:, :nt], in_=psn[:, :nt], func=AF.Ln, scale=1.0 / D, bias=epsc),
                     reads=[t_psn, t_eps], writes=[t_rs])
                P.op("act", lambda e, nt=nt: e.activation(out=rs[:, :nt], in_=rs[:, :nt], func=AF.Exp, scale=-0.5),
                     reads=[t_rs], writes=[t_rs])
                tt, t_tt = xv, xt
                P.op("dve", lambda e, xv=xv, nt=nt: e.tensor_tensor(out=xv[:, :, :nt], in0=xv[:, :, :nt],
                                                                  in1=rs[:, :nt].unsqueeze(1).to_broadcast([128, 16, nt]), op=ALU.mult),
                     reads=[xt, t_rs, t_sq], writes=[xt])

                def f(e, hv=hv, nt=nt, c=c, tt=tt):
                    for i_ in range(16):
                        if final:
                            ins = e.activation(out=hv[:, i_, :nt], in_=tt[:, i_, :nt], func=AF.Identity, scale=gTs[:, 4, i_:i_ + 1])
                        else:
                            ins = e.activation(out=hv[:, i_, :nt], in_=tt[:, i_, :nt], func=AF.Identity,
                                               bias=mod[:, l, i_, c:c + 1], scale=amod[:, l, i_, c:c + 1])
                    return ins
                P.op("act", f, reads=[t_tt, t_mod, t_amod, t_gTs], writes=[ht])
                if final:
                    stq("sp", outT[b, :, :, t0:t0 + nt].rearrange("i p t -> p i t"), hv[:, :, :nt], ht, [d_out])
                else:
                    stq("sp", hT[b, :, :, t0:t0 + nt].rearrange("i p t -> p i t"), hv[:, :, :nt], ht, [d_h[b]])
            phase_end()

        def linear(src, d_src, wsrc, ncols, mode, blocks, evac, psb, pre=None, gsz=1024, wkey=None, norm=None):
            hb = [A.alloc("lh%d" % i, [16, 512], BF16) for i in range(2)]
            ng = (ncols + gsz - 1) // gsz
            items = [(g, b, t0, nt) for g in range(ng) for (b, t0, nt) in blocks]
            if norm is not None:
                nl_, nxsrc = norm
                nxs = [A.alloc("fx%d" % i, [16, 512], F32) for i in range(2)]
                nrs, t_nrs = A.alloc("frs", [512], F32)
                psn, t_psn = banks[7]

            def load(i):
                g, b, t0, nt = items[i]
                hv, ht = hb[i % 2]
                if norm is not None and g == 0:
                    xv, xt = nxs[i % 2]
                    ld("sp", xv[:, :, :nt], nxsrc[b, :, :, t0:t0 + nt].rearrange("i p t -> p i t"), xt, reads=[d_x[b]])
                    P.op("act", lambda e: e.activation(out=hv[:, :, :nt], in_=xv[:, :, :nt], func=AF.Square), reads=[xt], writes=[ht])
                else:
                    ld("sp", hv[:, :, :nt], src[b, :, :, t0:t0 + nt].rearrange("i p t -> p i t"), ht, reads=[d_src[b]])
                if pre is not None:
                    pre(i, g, b, t0, nt)

            def load_b(i):
                g, b, t0, nt = items[i]
                if norm is None or g != 0:
                    return
                hv, ht = hb[i % 2]
                xv, xt = nxs[i % 2]
                c = 2 if t0 >= TL else b
                P.op("pe", mm16(psn[:, :nt], lambda k: onesb, lambda k: hv[:, k, :nt]), reads=[ht, t_onesb], writes=[t_psn])
                P.op("act", lambda e: e.activation(out=nrs[:, :nt], in_=psn[:, :nt], func=AF.Ln, scale=1.0 / D, bias=epsc),
                     reads=[t_psn, t_eps], writes=[t_nrs])
                P.op("act", lambda e: e.activation(out=nrs[:, :nt], in_=nrs[:, :nt], func=AF.Exp, scale=-0.5), reads=[t_nrs], writes=[t_nrs])
                P.op("dve", lambda e: e.tensor_tensor(out=xv[:, :, :nt], in0=xv[:, :, :nt],
                                                      in1=nrs[:, :nt].unsqueeze(1).to_broadcast([128, 16, nt]), op=ALU.mult),
                     reads=[xt, t_nrs], writes=[xt])

                def f(e):
                    for i_ in range(16):
                        ins = e.activation(out=hv[:, i_, :nt], in_=xv[:, i_, :nt], func=AF.Identity,
                                           bias=mod[:, nl_, i_, c:c + 1], scale=amod[:, nl_, i_, c:c + 1])
                    return ins
                P.op("act", f, reads=[xt, t_mod, t_amod, t_psn], writes=[ht])
                stq("sp", src[b, :, :, t0:t0 + nt].rearrange("i p t -> p i t"), hv[:, :, :nt], ht, [d_src[b]])
            load(0)
            load_b(0)
            lastg = -1
            for i, (g, b, t0, nt) in enumerate(items):
                c0 = g * gsz
                gc = min(gsz, ncols - c0)
                if g != lastg:
                    wcur = wload((wkey, g), wsrc[:, c0:c0 + gc], gc)
                    lastg = g
                wv, wt = wcur
                if i + 1 < len(items):
                    load(i + 1)
                hv, ht = hb[i % 2]
                if mode == "ws":
                    nj = gc // 128
                    for j in range(nj):
                        if j == nj // 2 and i + 1 < len(items):
                            load_b(i + 1)
                        ps, pst = psb.next()
                        P.op("pe", mm16(ps[:, :nt], lambda k, j=j, wv=wv: wv[:, k, j * 128:(j + 1) * 128],
                                        lambda k, hv=hv, nt=nt: hv[:, k, :nt]), reads=[wt, ht], writes=[pst])
                        evac(i, b, t0, nt, c0 // 128 + j, j, gc // 128, ps[:, :nt], pst)
                else:
                    assert norm is None
                    for tq in range(nt // 128):
                        for n0 in range(0, gc, 512):
                            n1 = min(gc, n0 + 512)
                            ps, pst = psb.next()
                            P.op("pe", mm16(ps[:, :n1 - n0], lambda k, hv=hv, tq=tq: hv[:, k, tq * 128:(tq + 1) * 128],
                                            lambda k, wv=wv, n0=n0, n1=n1: wv[:, k, n0:n1]), reads=[wt, ht], writes=[pst])
                            evac(i, b, t0 + tq * 128, c0 + n0, n1 - n0, n0, gc, ps[:, :n1 - n0], pst)

        def ws_simple(dst, d_dst, func, scale=1.0, hbase=0):
            stg = Rot([A.alloc("wss%d" % i, [8, 512], BF16) for i in range(2)])
            cur = {}

            def evac(i, b, t0, nt, jn, j, nj, ps, pst):
                if j == 0:
                    cur["s"] = stg.next()
                sv, stl = cur["s"]
                P.op("act", lambda e: e.activation(out=sv[:, j, :nt], in_=ps, func=func, scale=scale), reads=[pst], writes=[stl])
                if j == nj - 1:
                    h0 = jn - j - hbase
                    stq("sp", dst[b, h0:h0 + nj, :, t0:t0 + nt].rearrange("j p t -> p j t"), sv[:, :nj, :nt], stl, [d_dst[b]])
            return evac

        def as_simple(dst, d_dst, func, cbase, width=1024, dt=BF16, addt=None):
            stg = Rot([A.alloc("ass%d" % i, [width], dt) for i in range(2)])
            cur = {}

            def evac(i, b, tok0, col0, ncol, n0, gc, ps, pst):
                if n0 == 0:
                    cur["s"] = stg.next()
                sv, stl = cur["s"]
                if addt is not None:
                    av, at = addt
                    P.op("dve", lambda e: e.tensor_tensor(out=sv[:, n0:n0 + ncol], in0=ps, in1=av[:, n0:n0 + ncol], op=ALU.add),
                         reads=[pst, at], writes=[stl])
                else:
                    P.op("act", lambda e: e.activation(out=sv[:, n0:n0 + ncol], in_=ps, func=func), reads=[pst], writes=[stl])
                if n0 + ncol >= gc:
                    cs = col0 - n0
                    stq("sp", dst[b, tok0:tok0 + 128, cs:cs + gc], sv[:, :gc], stl, [d_dst[b]])
            return evac

        def out_phase(l, wout, blocks, xsrc, wkey=None):
            xb = [A.alloc("ox%d" % i, [8, 512], F32) for i in range(2)]

            def pre(i, g, b, t0, nt):
                xv, xt = xb[i % 2]
                ld("sp", xv[:, :, :nt], xsrc[b, g * 8:(g + 1) * 8, :, t0:t0 + nt].rearrange("i p t -> p i t"), xt, reads=[d_x[b]])

            def evac(i, b, t0, nt, jn, j, nj, ps, pst):
                xv, xt = xb[i % 2]
                c = 2 if t0 >= TL else b
                P.op("dve", lambda e: e.scalar_tensor_tensor(out=xv[:, j, :nt], in0=ps, scalar=mod[:, l, 32 + jn, c:c + 1],
                                                            in1=xv[:, j, :nt], op0=ALU.mult, op1=ALU.add),
                     reads=[pst, xt, t_mod], writes=[xt])
                if j == nj - 1:
                    g0 = jn - j
                    stq("sp", xr[b, g0:g0 + 8, :, t0:t0 + nt].rearrange("i p t -> p i t"), xv[:, :, :nt], xt, [d_x[b]])
            linear(uT, d_u, wout, 2048, "ws", blocks, evac, Rot(banks[0:4]), pre=pre, wkey=wkey)
            phase_end()

        def fnet_layer(l, j, blocks, with_ctx, xsrc):
            linear(hT, d_h, fwg[j], 2048, "ws", blocks, ws_simple(sgT, d_sg, AF.Silu), Rot(banks[0:4]), wkey="fwg%d" % j, norm=(l, xsrc))
            phase_end()
            CDs, t_CD = A.alloc("CD", [4, 512], BF16)
            SDs, t_SD = A.alloc("SD", [4, 512], BF16)
            ld("sp", CDs, CN["CD"], t_CD)
            ld("sp", SDs, CN["SD"], t_SD)
            hb = [A.alloc("fh%d" % i, [16, 512], BF16) for i in range(2)]
            stP = Rot([A.alloc("fsp%d" % i, [2048], BF16) for i in range(2)])
            stQ = Rot([A.alloc("fsq%d" % i, [2048], BF16) for i in range(2)])
            psb = Rot(banks[0:6])

            def load(i):
                b, t0, nt = blocks[i]
                hv, ht = hb[i % 2]
                ld("sp", hv[:, :, :nt], hT[b, :, :, t0:t0 + nt].rearrange("i p t -> p i t"), ht, reads=[d_h[b]])
            load(0)
            for i, (b, t0, nt) in enumerate(blocks):
                if i + 1 < len(blocks):
                    load(i + 1)
                hv, ht = hb[i % 2]
                nrm = (1.0 / 1024.0) if t0 < TL else 1.0 / math.sqrt(256.0 * 512.0)
                for tq in range(nt // 128):
                    pv, pt = stP.next()
                    qv, qt = stQ.next()
                    for g in range(4):
                        psP, tP = psb.next()
                        P.op("pe", mm16(psP, lambda k, g=g, tq=tq, hv=hv: hv[:, 4 * g + k, tq * 128:(tq + 1) * 128],
                                        lambda k: CDs[:, k, :], nk=4), reads=[ht, t_CD], writes=[tP])
                        P.op("act", lambda e, g=g, psP=psP, pv=pv, nrm=nrm: e.activation(out=pv[:, g * 512:(g + 1) * 512], in_=psP, func=AF.Identity, scale=nrm),
                             reads=[tP], writes=[pt])
                        psQ, tQ = psb.next()
                        P.op("pe", mm16(psQ, lambda k, g=g, tq=tq, hv=hv: hv[:, 4 * g + k, tq * 128:(tq + 1) * 128],
                                        lambda k: SDs[:, k, :], nk=4), reads=[ht, t_SD], writes=[tQ])
                        P.op("dve", lambda e, g=g, psQ=psQ, qv=qv, nrm=nrm: e.tensor_scalar(out=qv[:, g * 512:(g + 1) * 512], in0=psQ, scalar1=nrm,
                                                                                 scalar2=None, op0=ALU.mult), reads=[tQ], writes=[qt])
                    tok = t0 + tq * 128
                    stq("sp", Pd[b, :, tok:tok + 128, :].rearrange("j t e -> t j e"), pv.rearrange("p (j e) -> p j e", j=16), pt, [d_P[b]])
                    stq("sp", Qd[b, :, tok:tok + 128, :].rearrange("j t e -> t j e"), qv.rearrange("p (j e) -> p j e", j=16), qt, [d_Q[b]])
            phase_end()
            prefetch(("fwo%d" % j, 0), fwo[j][:, 0:1024], 1024)
            CTs, t_CT = A.alloc("CT", [2, 8, 1024], BF16)
            STs, t_ST = A.alloc("ST", [2, 8, 1024], BF16)
            ld("sp", CTs, CN["CT"], t_CT)
            ld("sp", STs, CN["STn"], t_ST)
            Pe = [A.alloc("Pe%d" % i, [2, 8, 128], BF16) for i in range(2)]
            Qe = [A.alloc("Qe%d" % i, [2, 8, 128], BF16) for i in range(2)]
            sgs = [A.alloc("sgs%d" % i, [2048], BF16) for i in range(2)]
            ust = [A.alloc("ust%d" % i, [2048], BF16) for i in range(2)]
            tB, t_tB = A.alloc("tB", [512], F32)
            y1, t_y1 = A.alloc("y1", [512], F32)
            y2, t_y2 = A.alloc("y2", [512], F32)
            psA = Rot(banks[0:2])
            psBk = Rot(banks[2:4])
            nb_list = sorted(set(b for (b, _, _) in blocks))
            items = [(b, jt) for b in nb_list for jt in range(16)]

            def load2(i):
                b, jt = items[i]
                pv, pt = Pe[i % 2]
                qv, qt = Qe[i % 2]
                sv, stl = sgs[i % 2]
                for r in range(2):
                    ld("sp", pv[:, r, :, :], Pd[b, jt, 0:TL, :].rearrange("(c p r) e -> r p c e", p=128, r=2)[r], pt, reads=[d_P[b]])
                    ld("sp", qv[:, r, :, :], Qd[b, jt, 0:TL, :].rearrange("(c p r) e -> r p c e", p=128, r=2)[r], qt, reads=[d_Q[b]])
                ld("sp", sv, sgT[b, jt, :, 0:TL], stl, reads=[d_sg[b]])
            load2(0)
            for i, (b, jt) in enumerate(items):
                if i + 1 < len(items):
                    load2(i + 1)
                pv, pt = Pe[i % 2]
                qv, qt = Qe[i % 2]
                sv, stl = sgs[i % 2]
                uv, ut = ust[i % 2]
                for jb in range(2):
                    cs = slice(jb * 512, (jb + 1) * 512)
                    pa, ta = psA.next()
                    pb, tb = psBk.next()
                    for r, (pp, tp) in enumerate(((pa, ta), (pb, tb))):
                        def f(e, r=r, pp=pp, pv=pv, qv=qv, cs=cs):
                            for c in range(8):
                                e.matmul(pp, lhsT=pv[:, r, c, :], rhs=CTs[:, r, c, cs], start=(c == 0), stop=False)
                                ins = e.matmul(pp, lhsT=qv[:, r, c, :], rhs=STs[:, r, c, cs], start=False, stop=(c == 7))
                            return ins
                        P.op("pe", f, reads=[pt, qt, t_CT, t_ST], writes=[tp])
                    P.op("act", lambda e, pb=pb: e.activation(out=tB, in_=pb, func=AF.Identity), reads=[tb], writes=[t_tB])
                    P.op("dve", lambda e, pa=pa: e.tensor_tensor(out=y1, in0=pa, in1=tB, op=ALU.add), reads=[ta, t_tB], writes=[t_y1])
                    P.op("dve", lambda e, pa=pa: e.tensor_tensor(out=y2, in0=pa, in1=tB, op=ALU.subtract), reads=[ta, t_tB], writes=[t_y2])
                    P.op("pool", lambda e, uv=uv, sv=sv, cs=cs: e.tensor_tensor(out=uv[:, cs], in0=y1, in1=sv[:, cs], op=ALU.mult),
                         reads=[t_y1, stl], writes=[ut])
                    c2 = slice(1024 + jb * 512, 1024 + (jb + 1) * 512)
                    P.op("dve", lambda e, uv=uv, sv=sv, c2=c2: e.tensor_tensor(out=uv[:, c2], in0=y2, in1=sv[:, c2], op=ALU.mult),
                         reads=[t_y2, stl], writes=[ut])
                stq("sp", uT[b, jt, :, 0:TL], uv, ut, [d_u[b]])
            if with_ctx:
                C2, t_C2 = A.alloc("C2", [2, 256], BF16)
                S2, t_S2 = A.alloc("S2", [2, 256], BF16)
                ld("sp", C2, CN["C256"], t_C2)
                ld("sp", S2, CN["S256n"], t_S2)
                Pc, t_Pc = A.alloc("Pc", [8, 2, 128], BF16)
                Qc, t_Qc = A.alloc("Qc", [8, 2, 128], BF16)
                sgc, t_sgc = A.alloc("sgc", [8, 256], BF16)
                uc, t_uc = A.alloc("uc", [8, 256], BF16)
                for b in nb_list:
                  for hf_ in range(2):
                    js = slice(hf_ * 8, hf_ * 8 + 8)
                    for c in range(2):
                        ld("sp", Pc[:, :, c, :], Pd[b, js, TL + c * 128:TL + (c + 1) * 128, :].rearrange("j p e -> p j e"), t_Pc, reads=[d_P[b]])
                        ld("sp", Qc[:, :, c, :], Qd[b, js, TL + c * 128:TL + (c + 1) * 128, :].rearrange("j p e -> p j e"), t_Qc, reads=[d_Q[b]])
                    ld("sp", sgc, sgT[b, js, :, TL:TA].rearrange("j p t -> p j t"), t_sgc, reads=[d_sg[b]])
                    for jt in range(8):
                        pa, ta = psA.next()

                        def f(e, pa=pa, jt=jt):
                            for c in range(2):
                                e.matmul(pa[:, 0:256], lhsT=Pc[:, jt, c, :], rhs=C2[:, c, :], start=(c == 0), stop=False)
                                ins = e.matmul(pa[:, 0:256], lhsT=Qc[:, jt, c, :], rhs=S2[:, c, :], start=False, stop=(c == 1))
                            return ins
                        P.op("pe", f, reads=[t_Pc, t_Qc, t_C2, t_S2], writes=[ta])
                        P.op("dve", lambda e, pa=pa, jt=jt: e.tensor_tensor(out=uc[:, jt, :], in0=pa[:, 0:256], in1=sgc[:, jt, :], op=ALU.mult),
                             reads=[ta, t_sgc], writes=[t_uc])
                    stq("sp", uT[b, js, :, TL:TA].rearrange("j p t -> p j t"), uc, t_uc, [d_u[b]])
            phase_end()
            out_phase(l, fwo[j], blocks, xsrc, wkey="fwo%d" % j)

        def attn_layer(l, blocks, xsrc):
            norm_phase(l, blocks, xsrc, pf=(("awq", 0), awi[:, 0:1024], 1024))
            rc, t_rc = A.alloc("rc", [2048], F32)
            rsn, t_rsn = A.alloc("rsn", [2048], F32)
            ld("sp", rc, CN["ropec"], t_rc)
            ld("sp", rsn, CN["ropes"], t_rsn)
            stg = Rot([A.alloc("aqs%d" % i, [8, 512], BF16) for i in range(2)])
            sqh = Rot([A.alloc("asq%d" % i, [512], BF16) for i in range(2)])
            rsv = Rot([A.alloc("ars%d" % i, [512], F32) for i in range(2)])
            qnb = Rot([A.alloc("aqn%d" % i, [512], BF16) for i in range(2)])
            t1v = Rot([A.alloc("at1%d" % i, [512], F32) for i in range(2)])
            t2v = Rot([A.alloc("at2%d" % i, [512], F32) for i in range(2)])
            ps2 = Rot(banks[4:6])
            ps3 = Rot(banks[6:8])
            cur = {}

            def evac(i, b, t0, nt, jn, j, nj, ps, pst):
                if j == 0:
                    cur["s"] = stg.next()
                sv, stl = cur["s"]
                gcol = qks[:, 0:1] if jn < 16 else qks[:, 1:2]
                sqv, sqt = sqh.next()
                P.op("act", lambda e: e.activation(out=sqv[:, :nt], in_=ps, func=AF.Square), reads=[pst], writes=[sqt])
                p2, tp2 = ps2.next()
                P.op("pe", lambda e: e.matmul(p2[:, :nt], lhsT=onesb, rhs=sqv[:, :nt], start=True, stop=True), reads=[sqt, t_onesb], writes=[tp2])
                rv, rt = rsv.next()
                P.op("act", lambda e: e.activation(out=rv[:, :nt], in_=p2[:, :nt], func=AF.Ln, scale=1.0 / 128, bias=epsc),
                     reads=[tp2, t_eps], writes=[rt])
                P.op("act", lambda e: e.activation(out=rv[:, :nt], in_=rv[:, :nt], func=AF.Exp, scale=-0.5), reads=[rt], writes=[rt])
                if t0 >= TL:
                    P.op("dve", lambda e: e.scalar_tensor_tensor(out=sv[:, j, :nt], in0=ps, scalar=gcol, in1=rv[:, :nt], op0=ALU.mult, op1=ALU.mult),
                         reads=[pst, rt, t_qks], writes=[stl])
                else:
                    qv, qt = qnb.next()
                    P.op("dve", lambda e: e.scalar_tensor_tensor(out=qv[:, :nt], in0=ps, scalar=gcol, in1=rv[:, :nt], op0=ALU.mult, op1=ALU.mult),
                         reads=[pst, rt, t_qks], writes=[qt])
                    p3, tp3 = ps3.next()
                    P.op("pe", lambda e: e.matmul(p3[:, :nt], lhsT=Rm, rhs=qv[:, :nt], start=True, stop=True), reads=[qt, t_m128], writes=[tp3])
                    a1, ta1 = t1v.next()
                    a2, ta2 = t2v.next()
                    P.op("pool", lambda e: e.tensor_tensor(out=a1[:, :nt], in0=qv[:, :nt], in1=rc[:, t0:t0 + nt], op=ALU.mult),
                         reads=[qt, t_rc], writes=[ta1])
                    P.op("dve", lambda e: e.tensor_tensor(out=a2[:, :nt], in0=p3[:, :nt], in1=rsn[:, t0:t0 + nt], op=ALU.mult),
                         reads=[tp3, t_rsn], writes=[ta2])
                    P.op("pool", lambda e: e.tensor_tensor(out=sv[:, j, :nt], in0=a1[:, :nt], in1=a2[:, :nt], op=ALU.add),
                         reads=[ta1, ta2], writes=[stl])
                if j == nj - 1:
                    h0 = jn - j
                    if h0 < 16:
                        stq("sp", qTd[b, h0:h0 + nj, :, t0:t0 + nt].rearrange("j p t -> p j t"), sv[:, :nj, :nt], stl, [d_q[b]])
                    else:
                        stq("sp", kTd[b, h0 - 16:h0 - 16 + nj, :, t0:t0 + nt].rearrange("j p t -> p j t"), sv[:, :nj, :nt], stl, [d_k[b]])
            linear(hT, d_h, awi[:, 0:2560], 2560, "ws", blocks, evac, Rot(banks[0:4]), wkey="awq")
            phase_end()
            linear(hT, d_h, awi[:, 3072:5120], 2048, "ws", blocks, ws_simple(sgT, d_sg, AF.Silu), Rot(banks[0:4]))
            phase_end()
            linear(hT, d_h, awi[:, 2560:3072], 512, "as", blocks, as_simple(vvd, d_v, AF.Identity, 0, width=512), Rot(banks[0:4]))
            phase_end()
            prefetch(("awo", 0), awo[:, 0:1024], 1024)
            kTs = [A.alloc("kTs%d" % i, [TA], BF16) for i in range(2)]
            vs = [A.alloc("vs%d" % i, [18, 128], BF16) for i in range(2)]
            qs = Rot([A.alloc("qs%d" % i, [512], BF16) for i in range(2)])
            szs = Rot([A.alloc("szs%d" % i, [512], BF16) for i in range(2)])
            pTs = Rot([A.alloc("pT%d" % i, [512], BF16) for i in range(4)])
            rden, t_rden = A.alloc("rden", [512], F32)
            ot, t_ot = A.alloc("ot", [512], F32)
            usts = Rot([A.alloc("aus%d" % i, [512], BF16) for i in range(2)])
            psS = Rot(banks[0:3])
            psO = Rot(banks[3:5])
            psD = Rot(banks[5:7])
            sc = 128.0 ** -0.5

            def attn_block(b, h, t0, nt, kv_, kt_, vv_, vt_):
                chunks = list(range(18)) if t0 < TL else [16, 17]
                qv, qt = qs.next()
                zv, zt = szs.next()
                ld("sp", qv[:, :nt], qTd[b, h, :, t0:t0 + nt], qt, reads=[d_q[b]])
                ld("sp", zv[:, :nt], sgT[b, h, :, t0:t0 + nt], zt, reads=[d_sg[b]])
                po, tpo = psO.next()
                pd_, tpd = psD.next()
                n = len(chunks)
                pend = []

                def s_step(c):
                    pss, tps = psS.next()
                    pv, ptl = pTs.next()
                    P.op("pe", lambda e: e.matmul(pss[:, :nt], lhsT=kv_[:, c * 128:(c + 1) * 128], rhs=qv[:, :nt],
                                                  start=True, stop=True), reads=[kt_, qt], writes=[tps])
                    P.op("act", lambda e: e.activation(out=pv[:, :nt], in_=pss[:, :nt], func=AF.Exp, scale=sc),
                         reads=[tps], writes=[ptl])
                    pend.append((c, pv, ptl))

                def pv_step(c, pv, ptl, first, last):
                    def f(e):
                        e.matmul(po[:, :nt], lhsT=vv_[:, c, :], rhs=pv[:, :nt], start=first, stop=last)
                        return e.matmul(pd_[:, :nt], lhsT=onesb, rhs=pv[:, :nt], start=first, stop=last)
                    P.op("pe", f, reads=[vt_, ptl, t_onesb], writes=[tpo, tpd])
                for idx in range(n + 2):
                    if idx < n:
                        s_step(chunks[idx])
                    if idx >= 2:
                        c, pv, ptl = pend[idx - 2]
                        pv_step(c, pv, ptl, idx == 2, idx == n + 1)
                uv, ut = usts.next()
                P.op("dve", lambda e: e.reciprocal(out=rden[:, :nt], in_=pd_[:, :nt]), reads=[tpd], writes=[t_rden])
                P.op("dve", lambda e: e.tensor_tensor(out=ot[:, :nt], in0=po[:, :nt], in1=rden[:, :nt], op=ALU.mult),
                     reads=[tpo, t_rden], writes=[t_ot])
                P.op("dve", lambda e: e.tensor_tensor(out=uv[:, :nt], in0=ot[:, :nt], in1=zv[:, :nt], op=ALU.mult),
                     reads=[t_ot, zt], writes=[ut])
                stq("pool", uT[b, h, :, t0:t0 + nt], uv[:, :nt], ut, [d_u[b]])

            kvi = 0
            for b in sorted(set(b for (b, _, _) in blocks)):
                for kv in range(4):
                    kv_, kt_ = kTs[kvi % 2]
                    vv_, vt_ = vs[kvi % 2]
                    kvi += 1
                    ld("sp", kv_, kTd[b, kv, :, :], kt_, reads=[d_k[b]])
                    ld("sp", vv_, vvd[b, :, kv * 128:(kv + 1) * 128].rearrange("(c p) e -> p c e", p=128), vt_, reads=[d_v[b]])
                    for hh in range(4):
                        h = kv * 4 + hh
                        for (bb, t0, nt) in blocks:
                            if bb != b:
                                continue
                            attn_block(b, h, t0, nt, kv_, kt_, vv_, vt_)
            phase_end()
            out_phase(l, awo, blocks, xsrc, wkey="awo")

        def mlstm_layer(l, blocks, xsrc):
            evq = ws_simple(mqT, d_mq, AF.Identity, scale=128.0 ** -0.5)
            linear(hT, d_h, mwi[:, 0:1024], 1024, "ws", blocks, evq, Rot(banks[0:4]), wkey="mq", norm=(l, xsrc))
            phase_end()
            linear(hT, d_h, mwi[:, 1024:2048], 1024, "ws", blocks, ws_simple(mkT, d_mkT, AF.Identity), Rot(banks[0:4]))
            phase_end()
            linear(hT, d_h, mwi[:, 1024:2048], 1024, "as", blocks, as_simple(mkd, d_mk, AF.Identity, 1024), Rot(banks[0:4]))
            phase_end()
            linear(hT, d_h, mwi[:, 2048:4096], 2048, "as", blocks, as_simple(mvd, d_mv, AF.Identity, 2048), Rot(banks[0:4]))
            phase_end()
            linear(hT, d_h, mwi[:, 4096:6144], 2048, "as", blocks, as_simple(mso, d_mso, AF.Sigmoid, 4096), Rot(banks[0:4]))
            phase_end()
            bgb, t_bgb = A.alloc("bgb", [32], F32)
            ld("sp", bgb, mbg[0:1, :].to_broadcast([128, 32]), t_bgb)
            linear(hT, d_h, mwi[:, 6144:6176], 32, "as", blocks, as_simple(mgd, d_mg, None, 6144, width=32, dt=F32, addt=(bgb, t_bgb)),
                   Rot(banks[0:4]), gsz=32)
            phase_end()
            linear(hT, d_h, mwi[:, 6176:8224], 2048, "as", blocks, as_simple(msz, d_msz, AF.Silu, 6176), Rot(banks[0:4]))
            phase_end()
            for b in sorted(set(b for (b, _, _) in blocks)):
                prefetch(("mwo", 0), mwo[:, 0:1024], 1024)
                mlstm_scan(b)
                mlstm_finish(b)
            out_phase(l, mwo, blocks, xsrc, wkey="mwo")

        def mlstm_scan(b):
            G, t_G = A.alloc("G", [18, 32], F32)
            ld("sp", G, mgd[b].rearrange("(c p) n -> p c n", p=128), t_G, reads=[d_mg[b]])
            E, t_E = A.alloc("E", [2, 144], F32)
            L, t_L = A.alloc("L", [2, 144], F32)
            Wt, t_W = A.alloc("Wt", [2, 18, 8], F32)
            RI, t_RI = A.alloc("RI", [2, 18, 8], F32)
            DC, t_DC = A.alloc("DC", [2, 18, 8], F32)
            pS, t_pS = banks[0]
            pT_, t_pT = banks[1]
            for d in range(2):
                P.op("act", lambda e, d=d: e.activation(out=E[:, d, :].rearrange("p (c h) -> p c h", c=18), in_=G[:, :, 16 * d + 8:16 * d + 16], func=AF.Exp, scale=-1.0),
                     reads=[t_G], writes=[t_E])
            P.op("act", lambda e: e.activation(out=L, in_=E, func=AF.Ln, bias=onec), reads=[t_E, t_onec], writes=[t_L])
            for d in range(2):
                P.op("pe", lambda e, d=d: e.matmul(pS[:, d * 144:(d + 1) * 144], lhsT=trid[d], rhs=L[:, d, :], start=True, stop=True),
                     reads=[t_L, t_f128], writes=[t_pS])
                P.op("pe", lambda e, d=d: e.matmul(pT_[:, d * 144:(d + 1) * 144], lhsT=onesf, rhs=L[:, d, :], start=True, stop=True),
                     reads=[t_L, t_f128], writes=[t_pT])
            for d in range(2):
                P.op("dve", lambda e, d=d: e.tensor_tensor(out=Wt[:, d, :, :], in0=G[:, :, 16 * d:16 * d + 8],
                                                          in1=pS[:, d * 144:(d + 1) * 144].rearrange("p (c h) -> p c h", c=18), op=ALU.subtract),
                     reads=[t_G, t_pS], writes=[t_W])
            P.op("act", lambda e: e.activation(out=Wt, in_=Wt, func=AF.Exp), reads=[t_W], writes=[t_W])
            P.op("act", lambda e: e.activation(out=RI, in_=pS[:, 0:288].rearrange("p (d c h) -> p d c h", d=2, c=18), func=AF.Exp, scale=-1.0),
                 reads=[t_pS], writes=[t_RI])
            P.op("act", lambda e: e.activation(out=DC, in_=pT_[:, 0:288].rearrange("p (d c h) -> p d c h", d=2, c=18), func=AF.Exp, scale=-1.0),
                 reads=[t_pT], writes=[t_DC])
            Cst = [[A.alloc("Cs%d%d" % (d, h), [257], F32) for h in range(8)] for d in range(2)]
            Cb = [[A.alloc("Cb%d%d" % (d, h), [258], BF16) for h in range(8)] for d in range(2)]
            qc = [[A.alloc("qc%d%d" % (d, i), [8, 128], BF16) for i in range(2)] for d in range(2)]
            kc = [[A.alloc("kc%d%d" % (d, i), [8, 128], BF16) for i in range(2)] for d in range(2)]
            kt = [[A.alloc("kt%d%d" % (d, i), [8, 128], BF16) for i in range(2)] for d in range(2)]
            ve = [[A.alloc("ve%d%d" % (d, i), [8, 258], BF16) for i in range(2)] for d in range(2)]
            kw = [A.alloc("kw%d" % d, [8, 128], BF16) for d in range(2)]
            stt = Rot([A.alloc("stt%d" % i, [128], BF16) for i in range(16)])
            fc = Rot([A.alloc("fc%d" % i, [8], F32) for i in range(4)])
            hst = [[A.alloc("hst%d%d" % (d, i), [8, 257], F32) for i in range(1)] for d in range(2)]
            hout = [A.alloc("hout%d" % d, [8, 256], F32) for d in range(2)]
            for d in range(2):
                for i in range(2):
                    vv_, vt_ = ve[d][i]
                    P.op("pool", lambda e, vv_=vv_: e.memset(vv_[:, :, 256:257], 1.0), writes=[vt_])
            order = [[16, 17] + list(range(16)), [17, 16] + list(range(15, -1, -1))]
            psSb = Rot([(banks[2][0][:, i * 128:(i + 1) * 128], None) for i in range(4)])
            psSt = banks[2][1]
            psS2 = Rot([(banks[3][0][:, i * 128:(i + 1) * 128], None) for i in range(4)])
            psS2t = banks[3][1]
            psAr = Rot(banks[4:6])
            psCr = Rot(banks[6:8])
            psbank = [[banks[2], banks[3]], [banks[0], banks[1]]]

            def loads(d, si):
                c = order[d][si]
                tk = slice(c * 128, (c + 1) * 128)
                i = si % 2
                ld("sp", qc[d][i][0], mqT[b, :, :, tk].rearrange("h p t -> p h t"), qc[d][i][1], reads=[d_mq[b]])
                ld("sp", kc[d][i][0], mkT[b, :, :, tk].rearrange("h p t -> p h t"), kc[d][i][1], reads=[d_mkT[b]])
                ld("sp", kt[d][i][0], mkd[b, tk, :].rearrange("t (h e) -> t h e", h=8), kt[d][i][1], reads=[d_mk[b]])
                ld("sp", ve[d][i][0][:, :, 0:256], mvd[b, tk, :].rearrange("t (h e) -> t h e", h=8), ve[d][i][1], reads=[d_mv[b]])
            for d in range(2):
                loads(d, 0)
            for si in range(18):
                for d in range(2):
                    if si + 1 < 18:
                        loads(d, si + 1)
                    c = order[d][si]
                    i = si % 2
                    first = si == 0
                    last = si == 17
                    cn = order[d][si + 1] if not last else None
                    qv, qt = qc[d][i]
                    kv_, kt_ = kc[d][i]
                    ktv, ktt = kt[d][i]
                    vv_, vt_ = ve[d][i]
                    kwv, kwt = kw[d]
                    hv, ht = hst[d][0]
                    P.op("dve", lambda e, d=d, c=c, kwv=kwv, ktv=ktv: e.tensor_tensor(
                        out=kwv, in0=ktv, in1=Wt[:, d, c, :].unsqueeze(2).to_broadcast([128, 8, 128]), op=ALU.mult),
                        reads=[ktt, t_W], writes=[kwt])
                    slots = []
                    for h in range(8):
                        pss = psbank[d][h // 4][0][:, (h % 4) * 128:(h % 4 + 1) * 128]
                        tps = psbank[d][h // 4][1]
                        slots.append((pss, tps))
                        P.op("pe", lambda e, pss=pss, kv_=kv_, qv=qv, h=h: e.matmul(pss, lhsT=kv_[:, h, :], rhs=qv[:, h, :], start=True, stop=True),
                             reads=[kt_, qt], writes=[tps])
                    svs = []
                    for h in range(8):
                        pss, tps = slots[h]
                        pc, tpc = psCr.next()
                        P.op("pe", lambda e, pc=pc, kwv=kwv, vv_=vv_, h=h: e.matmul(pc[:, 0:257], lhsT=kwv[:, h, :], rhs=vv_[:, h, 0:257], start=True, stop=True),
                             reads=[kwt, vt_], writes=[tpc])
                        sv, stl = stt.next()
                        svs.append((sv, stl))
                        P.op("dve", lambda e, sv=sv, pss=pss, d=d, c=c, h=h: e.scalar_tensor_tensor(
                            out=sv, in0=pss, scalar=Wt[:, d, c, h:h + 1], in1=maskd[d], op0=ALU.mult, op1=ALU.mult),
                            reads=[tps, t_W, t_m128], writes=[stl])
                        csv, cst = Cst[d][h]
                        if first:
                            P.op("dve", lambda e, csv=csv, pc=pc: e.tensor_copy(out=csv, in_=pc[:, 0:257]), reads=[tpc], writes=[cst])
                        else:
                            P.op("dve", lambda e, csv=csv, pc=pc, d=d, c=c, h=h: e.scalar_tensor_tensor(
                                out=csv, in0=csv, scalar=DC[:, d, c, h:h + 1], in1=pc[:, 0:257], op0=ALU.mult, op1=ALU.add),
                                reads=[tpc, cst, t_DC], writes=[cst])
                    for h in range(8):
                        sv, stl = svs[h]
                        pa, tpa = psAr.next()
                        cbv, cbt = Cb[d][h]
                        csv, cst = Cst[d][h]

                        def f(e, pa=pa, qv=qv, h=h, cbv=cbv, sv=sv, vv_=vv_, first=first):
                            if not first:
                                e.matmul(pa[:, 0:257], lhsT=qv[:, h, :], rhs=cbv[:, 0:257], start=True, stop=False)
                            return e.matmul(pa[:, 0:257], lhsT=sv, rhs=vv_[:, h, 0:257], start=first, stop=True)
                        P.op("pe", f, reads=[qt, cbt, stl, vt_], writes=[tpa])
                        P.op("act", lambda e, hv=hv, pa=pa, h=h: e.activation(out=hv[:, h, :], in_=pa[:, 0:257], func=AF.Identity),
                             reads=[tpa], writes=[ht])
                        if not last:
                            P.op("act", lambda e, cbv=cbv, csv=csv, d=d, cn=cn, h=h: e.activation(
                                out=cbv[:, 0:257], in_=csv, func=AF.Identity, scale=DC[:, d, cn, h:h + 1]), reads=[cst, t_DC], writes=[cbt])
                    fv, ft = fc.next()
                    f2, ft2 = fc.next()
                    P.op("dve", lambda e, fv=fv, hv=hv: e.tensor_scalar(out=fv, in0=hv[:, :, 256], scalar1=-1.0, scalar2=None, op0=ALU.mult),
                         reads=[ht], writes=[ft])
                    P.op("dve", lambda e, fv=fv, hv=hv: e.tensor_tensor(out=fv, in0=fv, in1=hv[:, :, 256], op=ALU.max), reads=[ht, ft], writes=[ft])
                    P.op("dve", lambda e, fv=fv, d=d, c=c: e.tensor_tensor(out=fv, in0=fv, in1=RI[:, d, c, :], op=ALU.max), reads=[ft, t_RI], writes=[ft])
                    P.op("dve", lambda e, fv=fv, f2=f2: e.reciprocal(out=f2, in_=fv), reads=[ft], writes=[ft2])
                    ho, hot = hout[d]
                    P.op("dve", lambda e, ho=ho, hv=hv, f2=f2: e.tensor_tensor(out=ho, in0=hv[:, :, 0:256],
                                                                              in1=f2.unsqueeze(2).to_broadcast([128, 8, 256]), op=ALU.mult),
                         reads=[ht, ft2], writes=[hot])
                    stq("sp", mhd[b, d, c * 128:(c + 1) * 128, :], ho.rearrange("p h e -> p (h e)"), hot, [d_mh[b]])
            phase_end()

        def mlstm_finish(b):
            hnb, t_hnb = A.alloc("hnb", [2048], F32)
            ld("sp", hnb, mhn[0:1, :].to_broadcast([128, 2048]), t_hnb)
            hf = [A.alloc("hf%d" % i, [2048], F32) for i in range(2)]
            hbk = [A.alloc("hbk%d" % i, [2048], F32) for i in range(2)]
            so = [A.alloc("so%d" % i, [2048], BF16) for i in range(2)]
            sz = [A.alloc("sz%d" % i, [2048], BF16) for i in range(2)]
            y, t_y = A.alloc("fy", [8, 256], F32)
            sq, t_sq = A.alloc("fsq", [8, 256], F32)
            ss, t_ss = A.alloc("fss", [8], F32)
            ub, t_ub = A.alloc("fub", [2048], BF16)
            uts = [A.alloc("uts%d" % i, [16, 128], BF16) for i in range(2)]
            psT = [(banks[0][0].bitcast(BF16), banks[0][1]), (banks[1][0].bitcast(BF16), banks[1][1])]

            def loads(c):
                tk = slice(c * 128, (c + 1) * 128)
                i = c % 2
                ld("sp", hf[i][0], mhd[b, 0, tk, :], hf[i][1], reads=[d_mh[b]])
                ld("sp", hbk[i][0], mhd[b, 1, tk, :], hbk[i][1], reads=[d_mh[b]])
                ld("sp", so[i][0], mso[b, tk, :], so[i][1], reads=[d_mso[b]])
                ld("sp", sz[i][0], msz[b, tk, :], sz[i][1], reads=[d_msz[b]])
            loads(0)
            for c in range(18):
                if c + 1 < 18:
                    loads(c + 1)
                i = c % 2
                hfv, hft = hf[i]
                hbv, hbt = hbk[i]
                sov, sot = so[i]
                szv, szt = sz[i]
                yf = y.rearrange("p h e -> p (h e)")
                P.op("pool", lambda e, hfv=hfv, hbv=hbv: e.tensor_tensor(out=hfv, in0=hfv, in1=hbv, op=ALU.add), reads=[hbt, hft], writes=[hft])
                P.op("dve", lambda e, hfv=hfv, sov=sov: e.tensor_tensor(out=yf, in0=hfv, in1=sov, op=ALU.mult), reads=[hft, sot], writes=[t_y])
                P.op("pool", lambda e: e.tensor_tensor(out=sq, in0=y, in1=y, op=ALU.mult), reads=[t_y], writes=[t_sq])
                P.op("dve", lambda e: e.tensor_reduce(out=ss, in_=sq, axis=AX.X, op=ALU.add), reads=[t_sq], writes=[t_ss])
                P.op("act", lambda e: e.activation(out=ss, in_=ss, func=AF.Ln, scale=1.0 / 256, bias=epsc), reads=[t_ss, t_eps], writes=[t_ss])
                P.op("act", lambda e: e.activation(out=ss, in_=ss, func=AF.Exp, scale=-0.5), reads=[t_ss], writes=[t_ss])
                P.op("dve", lambda e: e.tensor_tensor(out=y, in0=y, in1=ss.unsqueeze(2).to_broadcast([128, 8, 256]), op=ALU.mult),
                     reads=[t_y, t_ss], writes=[t_y])
                P.op("pool", lambda e: e.tensor_tensor(out=yf, in0=yf, in1=hnb, op=ALU.mult), reads=[t_y, t_hnb], writes=[t_y])
                P.op("dve", lambda e, szv=szv: e.tensor_tensor(out=ub, in0=yf, in1=szv, op=ALU.mult), reads=[t_y, szt], writes=[t_ub])
                uv, ut = uts[i]
                for half in range(2):
                    pt_, tpt = psT[half]

                    def f(e, pt_=pt_, half=half):
                        for jj in range(8):
                            j = half * 8 + jj
                            ins = e.transpose(out=pt_[:, jj * 128:(jj + 1) * 128], in_=ub[:, j * 128:(j + 1) * 128], identity=ident)
                        return ins
                    P.op("pe", f, reads=[t_ub, t_m128], writes=[tpt])
                    P.op("act", lambda e, uv=uv, pt_=pt_, half=half: e.activation(
                        out=uv[:, half * 8:(half + 1) * 8, :], in_=pt_[:, 0:1024].rearrange("p (j t) -> p j t", j=8), func=AF.Identity),
                        reads=[tpt], writes=[ut])
                stq("sp", uT[b, :, :, c * 128:(c + 1) * 128].rearrange("j p t -> p j t"), uv, ut, [d_u[b]])
            phase_end()

        blocks_all = [(b, t0, 512) for b in range(NB) for t0 in range(0, TL, 512)] + [(b, TL, 256) for b in range(NB)]
        blocks_all.sort(key=lambda x: (x[0], x[1]))
        blocks_lat = [(b, t0, 512) for b in range(NB) for t0 in range(0, TL, 512)]
        mod_phase()
        xsrc = xT
        for l in range(nlayers):
            kind = l % 3
            if kind == 0:
                last_no_ctx = (l == 3)
                fnet_layer(l, l // 3, blocks_lat if last_no_ctx else blocks_all, not last_no_ctx, xsrc)
            elif kind == 1:
                mlstm_layer(l, blocks_all, xsrc)
            else:
                attn_layer(l, blocks_all, xsrc)
            xsrc = xr
        norm_phase(0, blocks_lat, xsrc, final=True)
        P.op("sp", lambda e: None, reads=[d_out])
        print("ops:", {e: len(P.ops[e]) for e in ENGS}, "dma slots:", len(P.slots))
        P.emit(st)
    return nc


_CACHE = {}


def prep_inputs(inp, core):
    b0 = 2 * core
    f32 = np.float32
    x = inp["x"]
    ctx = inp["ctx"]
    xt = np.empty((NB, D, TA), f32)
    for i in range(NB):
        xt[i, :, :TL] = x[b0 + i].T
        xt[i, :, TL:] = ctx[b0 + i].T
    m = {"xT": xt.reshape(NB, 16, 128, TA)}
    cc = np.stack([inp["c"][b0], inp["c"][b0 + 1], inp["c_ctx"]], axis=1).astype(f32)
    m["cT"] = np.ascontiguousarray(cc.reshape(16, 128, 3).transpose(1, 0, 2))
    return m


def shared_inputs(inp):
    f32 = np.float32
    m = {}
    m["ada_w"] = np.ascontiguousarray(inp["ada_w"], dtype=f32)
    m["ada_b"] = np.ascontiguousarray(inp["ada_b"], dtype=f32)
    g = np.concatenate([inp["norm_g"], inp["final_g"][None]], axis=0).astype(f32)
    m["gT"] = np.ascontiguousarray(g.reshape(5, 16, 128).transpose(2, 0, 1))
    m["fnet_w_gate"] = np.ascontiguousarray(inp["fnet_w_gate"], dtype=f32)
    m["fnet_w_out"] = np.ascontiguousarray(inp["fnet_w_out"], dtype=f32)
    m["mlstm_w_in"] = np.ascontiguousarray(inp["mlstm_w_in"][0], dtype=f32)
    m["mlstm_b_gate"] = np.ascontiguousarray(inp["mlstm_b_gate"], dtype=f32)
    m["mlstm_hn"] = np.ascontiguousarray(inp["mlstm_hn"], dtype=f32)
    m["mlstm_w_out"] = np.ascontiguousarray(inp["mlstm_w_out"][0], dtype=f32)
    m["attn_w_in"] = np.ascontiguousarray(inp["attn_w_in"][0], dtype=f32)
    m["attn_qk"] = np.ascontiguousarray(np.stack([inp["attn_qn"][0], inp["attn_kn"][0]], axis=1), dtype=f32)
    m["attn_w_out"] = np.ascontiguousarray(inp["attn_w_out"][0], dtype=f32)
    for k, v in make_consts().items():
        m["c_" + k] = v
    return m


def kernel(**inputs):
    inp = {k: np.asarray(v) for k, v in inputs.items()}
    ncores = 8
    if "nc" not in _CACHE:
        _CACHE["nc"] = build()
    nc = _CACHE["nc"]
    sh = shared_inputs(inp)
    in_maps = []
    for c in range(ncores):
        m = dict(sh)
        m.update(prep_inputs(inp, c))
        in_maps.append(m)
    res = run_bass_kernel_spmd(nc, in_maps, core_ids=list(range(ncores)))
    out = np.empty((16, TL, D), np.float32)
    for c in range(ncores):
        o = np.asarray(res.results[c]["outT"]).reshape(NB, D, TL)
        for i in range(NB):
            out[2 * c + i] = o[i].T
    return out
```

```python
import math
import numpy as np
import ml_dtypes
from contextlib import ExitStack
import concourse.bass as bass
import concourse.mybir as mybir
from concourse.bass_utils import run_bass_kernel_spmd

F32 = mybir.dt.float32
BF16 = mybir.dt.bfloat16
AF = mybir.ActivationFunctionType
ALU = mybir.AluOpType
AX = mybir.AxisListType
NPBF = ml_dtypes.bfloat16

D = 2048
TL = 2048
TC = 256
TA = TL + TC
NB = 2
EPS = 1e-6
ENGS = ("sp", "act", "dve", "pool", "pe")


class Slot:
    __slots__ = ("sem", "cnt", "kind")

    def __init__(self):
        self.sem = None
        self.cnt = 0
        self.kind = None


class DT:
    __slots__ = ("name", "writers", "readers", "multi", "slot", "last_dma")

    def __init__(self, name, multi=False, fence=()):
        self.name = name
        self.writers = list(fence)
        self.readers = []
        self.multi = multi
        self.slot = None
        self.last_dma = None


class Op:
    __slots__ = ("eng", "fn", "deps", "is_dma", "sig", "need")

    def __init__(self, eng, fn, is_dma):
        self.eng = eng
        self.fn = fn
        self.deps = []
        self.is_dma = is_dma
        self.sig = None
        self.need = False


class Prog:
    def __init__(self, nc):
        self.nc = nc
        self.ops = {e: [] for e in ENGS}
        self.slots = []
        self.free_slots = {}
        self.fence = []
        self.live = []

    def tile(self, name, multi=False, phase=True):
        t = DT(name, multi, self.fence)
        if phase:
            self.live.append(t)
        return t

    def _track(self, op, reads, writes):
        deps = op.deps
        for t in reads:
            deps.extend(t.writers)
            t.readers.append(op)
        for t in writes:
            if t.multi:
                if t.readers:
                    deps.extend(t.readers)
                    t.readers = []
                    t.writers = [op]
                else:
                    t.writers.append(op)
            else:
                deps.extend(t.readers)
                deps.extend(t.writers)
                t.readers = []
                t.writers = [op]

    def op(self, eng, fn, reads=(), writes=()):
        o = Op(eng, fn, False)
        self._track(o, reads, writes)
        o.deps = [d for d in o.deps if d is not o]
        self.ops[eng].append(o)
        return o

    def dma(self, eng, fn, n, st, reads=(), writes=()):
        o = Op(eng, fn, True)
        self._track(o, reads, writes)
        o.deps = [d for d in o.deps if d is not o]
        if st.last_dma is not None:
            o.deps.append(st.last_dma)
        st.last_dma = o
        if st.slot is None:
            fs = self.free_slots.setdefault(eng, [])
            if fs:
                st.slot = fs.pop()
                if st.slot.cnt:
                    ghost = Op(eng, None, True)
                    ghost.sig = (st.slot, st.slot.cnt)
                    o.deps.append(ghost)
            else:
                st.slot = Slot()
                st.slot.kind = eng
                self.slots.append(st.slot)
        st.slot.cnt += 16 * n
        o.sig = (st.slot, st.slot.cnt)
        self.ops[eng].append(o)
        return o

    def barrier(self, fn):
        o = self.op("dve", fn, writes=self.live)
        for t in self.live:
            if t.slot is not None:
                self.free_slots[t.slot.kind].append(t.slot)
        self.live = []
        self.fence = [o]
        return o

    def emit(self, stack):
        nc = self.nc
        for e in ENGS:
            for o in self.ops[e]:
                for d in o.deps:
                    if not d.is_dma:
                        if d.eng == "pe" and o.eng == "pe" and not o.is_dma:
                            continue
                        d.need = True
        engsem = {}
        for e in ENGS:
            if e != "sp":
                engsem[e] = stack.enter_context(nc.semaphore("eng_" + e))
        for i, t in enumerate(self.slots):
            t.sem = stack.enter_context(nc.semaphore("dsl%d" % i))
        for e in ENGS:
            c = 0
            for o in self.ops[e]:
                if o.is_dma:
                    continue
                if o.need:
                    c += 1
                    o.sig = (e, c)
        block = stack.enter_context(nc.Block())
        prog = self

        def run(e, eng):
            waited = {}
            for o in prog.ops[e]:
                req = {}
                for d in o.deps:
                    if d.sig is None:
                        continue
                    if (not d.is_dma) and d.eng == "pe" and e == "pe" and not o.is_dma:
                        continue
                    k, v = d.sig
                    if req.get(k, 0) < v:
                        req[k] = v
                for k, v in req.items():
                    if waited.get(k, 0) >= v:
                        continue
                    waited[k] = v
                    sem = engsem[k] if isinstance(k, str) else k.sem
                    eng.wait_ge(sem, v)
                if o.is_dma:
                    o.fn(eng, o.sig[0].sem)
                else:
                    ins = o.fn(eng)
                    if o.need:
                        ins.then_inc(engsem[e], 1)

        @block.sync
        def _(eng):
            run("sp", eng)

        @block.scalar
        def _(eng):
            run("act", eng)

        @block.vector
        def _(eng):
            run("dve", eng)

        @block.gpsimd
        def _(eng):
            run("pool", eng)

        @block.tensor
        def _(eng):
            run("pe", eng)


class Arena:
    def __init__(self, ap, P, base=0):
        self.ap = ap
        self.P = P
        self.off = base
        self.base = base
        self.cap = ap.shape[1]

    def reset(self):
        self.off = self.base

    def alloc(self, name, shape, dtype, phase=True):
        n = int(np.prod(shape))
        words = n if dtype == F32 else (n + 1) // 2
        assert self.off + words <= self.cap, (name, self.off, words, self.cap)
        v = self.ap[:, self.off:self.off + words]
        if dtype != F32:
            v = v.bitcast(dtype)
            if n % 2:
                v = v[:, 0:n]
        self.off += words
        if len(shape) == 2:
            v = v.rearrange("p (a b) -> p a b", a=shape[0])
        elif len(shape) == 3:
            v = v.rearrange("p (a b c) -> p a b c", a=shape[0], b=shape[1])
        elif len(shape) == 4:
            v = v.rearrange("p (a b c d) -> p a b c d", a=shape[0], b=shape[1], c=shape[2])
        return v, self.P.tile(name, phase=phase)


class Rot:
    def __init__(self, items):
        self.items = items
        self.i = 0

    def next(self):
        r = self.items[self.i % len(self.items)]
        self.i += 1
        return r


def make_consts():
    c = {}
    p = np.arange(128)
    d = (np.arange(4)[None, :] * 128 + p[:, None]).astype(np.float64)
    e = np.arange(512, dtype=np.float64)
    ang = 2 * np.pi * d[:, :, None] * e[None, None, :] / 512.0
    c["CD"] = np.cos(ang).astype(NPBF)
    c["SD"] = np.sin(ang).astype(NPBF)
    t = (256 * np.arange(8)[None, None, :] + 2 * p[:, None, None] + np.arange(2)[None, :, None]).astype(np.float64)
    tt = np.arange(1024, dtype=np.float64)
    tm = np.mod(t[:, :, :, None] * tt[None, None, None, :], 2048.0)
    ang = 2 * np.pi * tm / 2048.0
    c["CT"] = np.cos(ang).astype(NPBF)
    c["STn"] = (-np.sin(ang)).astype(NPBF)
    t = (128 * np.arange(2)[None, :] + p[:, None]).astype(np.float64)
    tt = np.arange(256, dtype=np.float64)
    ang = 2 * np.pi * np.mod(t[:, :, None] * tt[None, None, :], 256.0) / 256.0
    c["C256"] = np.cos(ang).astype(NPBF)
    c["S256n"] = (-np.sin(ang)).astype(NPBF)
    rows = TL // 64
    r = np.repeat(np.arange(rows), 64).astype(np.float32)
    col = np.tile(np.arange(64), rows).astype(np.float32)
    freqs = (np.float32(10000.0) ** (-np.arange(0, 64, 2, dtype=np.float32) / np.float32(64))).astype(np.float32)
    angt = np.concatenate([r[:, None] * freqs, col[:, None] * freqs], axis=-1).astype(np.float32)
    cosT = np.repeat(np.cos(angt).T, 2, axis=0)
    sinT = np.repeat(np.sin(angt).T, 2, axis=0)
    c["ropec"] = np.ascontiguousarray(cosT).astype(np.float32)
    c["ropes"] = np.ascontiguousarray(sinT).astype(np.float32)
    ident = np.eye(128, dtype=np.float32)
    Rm = np.zeros((128, 128), np.float32)
    for i in range(64):
        Rm[2 * i + 1, 2 * i] = -1.0
        Rm[2 * i, 2 * i + 1] = 1.0
    s = np.arange(128)[:, None]
    tq = np.arange(128)[None, :]
    mf = (s <= tq).astype(np.float32)
    mb = (s >= tq).astype(np.float32)
    c["m128"] = np.stack([ident, Rm, mf, mb], axis=1).astype(NPBF)
    c["f128"] = np.stack([(s > tq).astype(np.float32), (s < tq).astype(np.float32), np.ones((128, 128), np.float32)], axis=1)
    return c


CONST_SPECS = [("CD", [128, 4, 512], BF16), ("SD", [128, 4, 512], BF16), ("CT", [128, 2, 8, 1024], BF16),
               ("STn", [128, 2, 8, 1024], BF16), ("C256", [128, 2, 256], BF16), ("S256n", [128, 2, 256], BF16),
               ("ropec", [128, 2048], F32), ("ropes", [128, 2048], F32), ("m128", [128, 4, 128], BF16),
               ("f128", [128, 3, 128], F32)]


def build(nlayers=4, dump=()):
    nc = bass.Bass("TRN2", target_bir_lowering=False)

    def din(name, shape, dt=F32):
        return nc.dram_tensor(name, shape, dt, kind="ExternalInput").ap()

    kinds = set(l % 3 for l in range(nlayers))
    need = {"xr": nlayers > 0, "hT": nlayers > 0, "uT": nlayers > 0, "sgT": bool(kinds & {0, 2}),
            "Pd": 0 in kinds, "Qd": 0 in kinds, "qTd": 2 in kinds, "kTd": 2 in kinds, "vvd": 2 in kinds}

    def dsc(name, shape, dt):
        if not need.get(name, 1 in kinds):
            shape = [1] * (len(shape) - 1) + [16]
        if name in dump:
            return nc.dram_tensor(name, shape, dt, kind="ExternalOutput").ap()
        return nc.dram_tensor(name, shape, dt).ap()

    xT = din("xT", [NB, 16, 128, TA])
    cT = din("cT", [128, 16, 3])
    ada_w = din("ada_w", [4, 2048, 6144])
    ada_b = din("ada_b", [4, 6144])
    gT = din("gT", [128, 5, 16])
    fwg = din("fnet_w_gate", [2, 2048, 2048])
    fwo = din("fnet_w_out", [2, 2048, 2048])
    mwi = din("mlstm_w_in", [2048, 8224])
    mbg = din("mlstm_b_gate", [1, 32])
    mhn = din("mlstm_hn", [1, 2048])
    mwo = din("mlstm_w_out", [2048, 2048])
    awi = din("attn_w_in", [2048, 5120])
    aqk = din("attn_qk", [128, 2])
    awo = din("attn_w_out", [2048, 2048])
    CN = {n: din("c_" + n, s, dt) for n, s, dt in CONST_SPECS}
    outT = nc.dram_tensor("outT", [NB, 16, 128, TL], F32, kind="ExternalOutput").ap()

    xr = dsc("xr", [NB, 16, 128, TA], F32)
    hT = dsc("hT", [NB, 16, 128, TA], BF16)
    uT = dsc("uT", [NB, 16, 128, TA], BF16)
    sgT = dsc("sgT", [NB, 16, 128, TA], BF16)
    Pd = dsc("Pd", [NB, 16, TA, 128], BF16)
    Qd = dsc("Qd", [NB, 16, TA, 128], BF16)
    qTd = dsc("qTd", [NB, 16, 128, TA], BF16)
    kTd = dsc("kTd", [NB, 4, 128, TA], BF16)
    vvd = dsc("vvd", [NB, TA, 512], BF16)
    mqT = dsc("mqT", [NB, 8, 128, TA], BF16)
    mkT = dsc("mkT", [NB, 8, 128, TA], BF16)
    mkd = dsc("mkd", [NB, TA, 1024], BF16)
    mvd = dsc("mvd", [NB, TA, 2048], BF16)
    mso = dsc("mso", [NB, TA, 2048], BF16)
    msz = dsc("msz", [NB, TA, 2048], BF16)
    mgd = dsc("mgd", [NB, TA, 32], F32)
    mhd = dsc("mhd", [NB, 2, TA, 2048], F32)

    st = ExitStack()
    with st:
        P = Prog(nc)
        sb = st.enter_context(nc.sbuf_tensor("arena", [128, 51200], F32))
        PA = Arena(sb[:], P)
        banks = []
        for i in range(8):
            pt = st.enter_context(nc.psum_tensor("ps%d" % i, [128, 512], F32))
            banks.append((pt[:], P.tile("psb%d" % i, phase=False)))

        def dtile(name):
            return [P.tile(name + str(b), multi=True, phase=False) for b in range(NB)]
        d_x, d_h, d_u, d_sg, d_P, d_Q = dtile("x"), dtile("h"), dtile("u"), dtile("sg"), dtile("P"), dtile("Q")
        d_q, d_k, d_v = dtile("q"), dtile("k"), dtile("v")
        d_mq, d_mkT, d_mk, d_mv, d_mso, d_msz, d_mg, d_mh = (dtile("mq"), dtile("mkT"), dtile("mk"), dtile("mv"),
                                                            dtile("mso"), dtile("msz"), dtile("mg"), dtile("mh"))
        d_out = P.tile("out", multi=True, phase=False)

        m128, t_m128 = PA.alloc("m128", [4, 128], BF16, phase=False)
        f128, t_f128 = PA.alloc("f128", [3, 128], F32, phase=False)
        onesb, t_onesb = PA.alloc("onesb", [128], BF16, phase=False)
        scT, t_scT = PA.alloc("scT", [16, 3], BF16, phase=False)
        cTs, t_cTs = PA.alloc("cTs", [16, 3], F32, phase=False)
        gTs, t_gTs = PA.alloc("gTs", [5, 16], F32, phase=False)
        mod, t_mod = PA.alloc("mod", [4, 48, 3], F32, phase=False)
        amod, t_amod = PA.alloc("amod", [4, 16, 3], F32, phase=False)
        qks, t_qks = PA.alloc("qks", [2], F32, phase=False)
        scr, t_scr = PA.alloc("scr", [4], F32, phase=False)
        epsc, t_eps = PA.alloc("epsc", [1], F32, phase=False)
        onec, t_onec = PA.alloc("onec", [1], F32, phase=False)
        wbufs = [PA.alloc("wbuf%d" % i, [16, 1024], BF16, phase=False) for i in range(2)]
        wstate = {"i": 0, "pf": {}}
        PA.base = PA.off
        A = PA
        ident = m128[:, 0, :]
        Rm = m128[:, 1, :]
        maskd = [m128[:, 2, :], m128[:, 3, :]]
        trid = [f128[:, 0, :], f128[:, 1, :]]
        onesf = f128[:, 2, :]

        def ld(eng, dst, src, tl, reads=(), n=1):
            def f(e, s):
                e.dma_start(out=dst, in_=src).then_inc(s, 16)
            P.dma(eng, f, 1, tl, reads=reads, writes=[tl])

        def stq(eng, dst, src, tl, dtiles):
            def f(e, s):
                e.dma_start(out=dst, in_=src).then_inc(s, 16)
            P.dma(eng, f, 1, tl, reads=[tl], writes=dtiles)

        def wload(key, src, gc):
            if key in wstate["pf"]:
                return wstate["pf"].pop(key)
            wv, wt = wbufs[wstate["i"] % 2]
            wstate["i"] += 1
            ld("pool", wv[:, :, :gc], src.rearrange("(k p) n -> p k n", p=128), wt)
            return wv, wt

        def prefetch(key, src, gc):
            if key not in wstate["pf"]:
                wstate["pf"][key] = wload(None, src, gc)

        ld("sp", m128, CN["m128"], t_m128)
        ld("sp", f128, CN["f128"], t_f128)
        ld("sp", cTs, cT, t_cTs)
        ld("sp", gTs, gT, t_gTs)
        ld("sp", qks, aqk, t_qks)
        P.op("dve", lambda e: e.memset(onesb, 1.0), writes=[t_onesb])
        P.op("dve", lambda e: e.memset(epsc, EPS), writes=[t_eps])
        P.op("dve", lambda e: e.memset(onec, 1.0), writes=[t_onec])
        P.op("act", lambda e: e.activation(out=scT, in_=cTs, func=AF.Silu), reads=[t_cTs], writes=[t_scT])

        def phase_end():
            P.barrier(lambda e: e.memset(scr, 0.0))
            A.reset()

        def mm16(out, lhs_fn, rhs_fn, nk=16):
            def f(e):
                for k in range(nk):
                    ins = e.matmul(out, lhsT=lhs_fn(k), rhs=rhs_fn(k), start=(k == 0), stop=(k == nk - 1))
                return ins
            return f

        def mod_phase():
            wbs = [A.alloc("mw%d" % i, [16, 512], BF16) for i in range(2)]
            adb, t_adb = A.alloc("adb", [6144], BF16)
            psm, t_psm = banks[0]
            for l in range(nlayers):
                def f(e, s, l=l):
                    e.dma_start(out=adb[0:1, :], in_=ada_b[l:l + 1, :]).then_inc(s, 16)
                P.dma("pool", f, 1, t_adb, writes=[t_adb])
                for nb in range(12):
                    wv, wt = wbs[nb % 2]
                    ld("pool", wv, ada_w[l, :, nb * 512:(nb + 1) * 512].rearrange("(k p) n -> p k n", p=128), wt)

                    def f(e, nb=nb, wv=wv):
                        for j in range(4):
                            nt_ = nb * 4 + j
                            o = psm[:, nt_ * 3:nt_ * 3 + 3]
                            for k in range(16):
                                e.matmul(o, lhsT=wv[:, k, j * 128:(j + 1) * 128], rhs=scT[:, k, :], start=(k == 0), stop=False)
                            ins = e.matmul(o, lhsT=adb[0:1, nt_ * 128:(nt_ + 1) * 128], rhs=onesb[0:1, 0:3], start=False, stop=True)
                        return ins
                    P.op("pe", f, reads=[wt, t_scT, t_adb, t_onesb], writes=[t_psm])
                P.op("act", lambda e, l=l: e.activation(out=mod[:, l, :, :], in_=psm[:, 0:144].rearrange("p (a b) -> p a b", a=48), func=AF.Identity),
                     reads=[t_psm], writes=[t_mod])
                P.op("dve", lambda e, l=l: e.tensor_scalar_add(out=amod[:, l, :, :], in0=mod[:, l, 16:32, :], scalar1=1.0),
                     reads=[t_mod], writes=[t_amod])
                P.op("dve", lambda e, l=l: e.tensor_tensor(out=amod[:, l, :, :], in0=amod[:, l, :, :],
                                                          in1=gTs[:, l, :].unsqueeze(2).to_broadcast([128, 16, 3]), op=ALU.mult),
                     reads=[t_gTs, t_amod], writes=[t_amod])
            phase_end()

        def norm_phase(l, blocks, xsrc, final=False, pf=None):
            if pf is not None:
                prefetch(*pf)
            xs = [A.alloc("nx%d" % i, [16, 512], F32) for i in range(2)]
            sq, t_sq = A.alloc("nsq", [16, 512], BF16)
            hb = [A.alloc("nh%d" % i, [16, 512], F32 if final else BF16) for i in range(1 if final else 2)]
            rs, t_rs = A.alloc("nrs", [512], F32)
            psn, t_psn = banks[1]

            def load(i):
                b, t0, nt = blocks[i]
                xv, xt = xs[i % 2]
                ld("sp", xv[:, :, :nt], xsrc[b, :, :, t0:t0 + nt].rearrange("i p t -> p i t"), xt, reads=[d_x[b]])
            load(0)
            for i, (b, t0, nt) in enumerate(blocks):
                if i + 1 < len(blocks):
                    load(i + 1)
                c = 2 if t0 >= TL else b
                xv, xt = xs[i % 2]
                hv, ht = hb[i % len(hb)]
                P.op("act", lambda e, xv=xv, nt=nt: e.activation(out=sq[:, :, :nt], in_=xv[:, :, :nt], func=AF.Square),
                     reads=[xt], writes=[t_sq])
                P.op("pe", mm16(psn[:, :nt], lambda k: onesb, lambda k, nt=nt: sq[:, k, :nt]), reads=[t_sq, t_onesb], writes=[t_psn])
                P.op("act", lambda e, nt=nt: e.activation(out=rs[:, :nt], in_=psn[:, :nt], func=AF.Ln, scale=1.0 / D, bias=epsc),
                     reads=[t_psn, t_eps], writes=[t_rs])
                P.op("act", lambda e, nt=nt: e.activation(out=rs[:, :nt], in_=rs[:, :nt], func=AF.Exp, scale=-0.5),
                     reads=[t_rs], writes=[t_rs])
                tt, t_tt = xv, xt
                P.op("dve", lambda e, xv=xv, nt=nt: e.tensor_tensor(out=xv[:, :, :nt], in0=xv[:, :, :nt],
                                                                  in1=rs[:, :nt].unsqueeze(1).to_broadcast([128, 16, nt]), op=ALU.mult),
                     reads=[xt, t_rs, t_sq], writes=[xt])

                def f(e, hv=hv, nt=nt, c=c, tt=tt):
                    for i_ in range(16):
                        if final:
                            ins = e.activation(out=hv[:, i_, :nt], in_=tt[:, i_, :nt], func=AF.Identity, scale=gTs[:, 4, i_:i_ + 1])
                        else:
                            ins = e.activation(out=hv[:, i_, :nt], in_=tt[:, i_, :nt], func=AF.Identity,
                                               bias=mod[:, l, i_, c:c + 1], scale=amod[:, l, i_, c:c + 1])
                    return ins
                P.op("act", f, reads=[t_tt, t_mod, t_amod, t_gTs], writes=[ht])
                if final:
                    stq("sp", outT[b, :, :, t0:t0 + nt].rearrange("i p t -> p i t"), hv[:, :, :nt], ht, [d_out])
                else:
                    stq("sp", hT[b, :, :, t0:t0 + nt].rearrange("i p t -> p i t"), hv[:, :, :nt], ht, [d_h[b]])
            phase_end()

        def linear(src, d_src, wsrc, ncols, mode, blocks, evac, psb, pre=None, gsz=1024, wkey=None, norm=None):
            hb = [A.alloc("lh%d" % i, [16, 512], BF16) for i in range(2)]
            ng = (ncols + gsz - 1) // gsz
            items = [(g, b, t0, nt) for g in range(ng) for (b, t0, nt) in blocks]
            if norm is not None:
                nl_, nxsrc = norm
                nxs = [A.alloc("fx%d" % i, [16, 512], F32) for i in range(2)]
                nrs, t_nrs = A.alloc("frs", [512], F32)
                psn, t_psn = banks[7]

            def load(i):
                g, b, t0, nt = items[i]
                hv, ht = hb[i % 2]
                if norm is not None and g == 0:
                    xv, xt = nxs[i % 2]
                    ld("sp", xv[:, :, :nt], nxsrc[b, :, :, t0:t0 + nt].rearrange("i p t -> p i t"), xt, reads=[d_x[b]])
                    P.op("act", lambda e: e.activation(out=hv[:, :, :nt], in_=xv[:, :, :nt], func=AF.Square), reads=[xt], writes=[ht])
                else:
                    ld("sp", hv[:, :, :nt], src[b, :, :, t0:t0 + nt].rearrange("i p t -> p i t"), ht, reads=[d_src[b]])
                if pre is not None:
                    pre(i, g, b, t0, nt)

            def load_b(i):
                g, b, t0, nt = items[i]
                if norm is None or g != 0:
                    return
                hv, ht = hb[i % 2]
                xv, xt = nxs[i % 2]
                c = 2 if t0 >= TL else b
                P.op("pe", mm16(psn[:, :nt], lambda k: onesb, lambda k: hv[:, k, :nt]), reads=[ht, t_onesb], writes=[t_psn])
                P.op("act", lambda e: e.activation(out=nrs[:, :nt], in_=psn[:, :nt], func=AF.Ln, scale=1.0 / D, bias=epsc),
                     reads=[t_psn, t_eps], writes=[t_nrs])
                P.op("act", lambda e: e.activation(out=nrs[:, :nt], in_=nrs[:, :nt], func=AF.Exp, scale=-0.5), reads=[t_nrs], writes=[t_nrs])
                P.op("dve", lambda e: e.tensor_tensor(out=xv[:, :, :nt], in0=xv[:, :, :nt],
                                                      in1=nrs[:, :nt].unsqueeze(1).to_broadcast([128, 16, nt]), op=ALU.mult),
                     reads=[xt, t_nrs], writes=[xt])

                def f(e):
                    for i_ in range(16):
                        ins = e.activation(out=hv[:, i_, :nt], in_=xv[:, i_, :nt], func=AF.Identity,
                                           bias=mod[:, nl_, i_, c:c + 1], scale=amod[:, nl_, i_, c:c + 1])
                    return ins
                P.op("act", f, reads=[xt, t_mod, t_amod, t_psn], writes=[ht])
                stq("sp", src[b, :, :, t0:t0 + nt].rearrange("i p t -> p i t"), hv[:, :, :nt], ht, [d_src[b]])
            load(0)
            load_b(0)
            lastg = -1
            for i, (g, b, t0, nt) in enumerate(items):
                c0 = g * gsz
                gc = min(gsz, ncols - c0)
                if g != lastg:
                    wcur = wload((wkey, g), wsrc[:, c0:c0 + gc], gc)
                    lastg = g
                wv, wt = wcur
                if i + 1 < len(items):
                    load(i + 1)
                hv, ht = hb[i % 2]
                if mode == "ws":
                    nj = gc // 128
                    for j in range(nj):
                        if j == nj // 2 and i + 1 < len(items):
                            load_b(i + 1)
                        ps, pst = psb.next()
                        P.op("pe", mm16(ps[:, :nt], lambda k, j=j, wv=wv: wv[:, k, j * 128:(j + 1) * 128],
                                        lambda k, hv=hv, nt=nt: hv[:, k, :nt]), reads=[wt, ht], writes=[pst])
                        evac(i, b, t0, nt, c0 // 128 + j, j, gc // 128, ps[:, :nt], pst)
                else:
                    assert norm is None
                    for tq in range(nt // 128):
                        for n0 in range(0, gc, 512):
                            n1 = min(gc, n0 + 512)
                            ps, pst = psb.next()
                            P.op("pe", mm16(ps[:, :n1 - n0], lambda k, hv=hv, tq=tq: hv[:, k, tq * 128:(tq + 1) * 128],
                                            lambda k, wv=wv, n0=n0, n1=n1: wv[:, k, n0:n1]), reads=[wt, ht], writes=[pst])
                            evac(i, b, t0 + tq * 128, c0 + n0, n1 - n0, n0, gc, ps[:, :n1 - n0], pst)

        def ws_simple(dst, d_dst, func, scale=1.0, hbase=0):
            stg = Rot([A.alloc("wss%d" % i, [8, 512], BF16) for i in range(2)])
            cur = {}

            def evac(i, b, t0, nt, jn, j, nj, ps, pst):
                if j == 0:
                    cur["s"] = stg.next()
                sv, stl = cur["s"]
                P.op("act", lambda e: e.activation(out=sv[:, j, :nt], in_=ps, func=func, scale=scale), reads=[pst], writes=[stl])
                if j == nj - 1:
                    h0 = jn - j - hbase
                    stq("sp", dst[b, h0:h0 + nj, :, t0:t0 + nt].rearrange("j p t -> p j t"), sv[:, :nj, :nt], stl, [d_dst[b]])
            return evac

        def as_simple(dst, d_dst, func, cbase, width=1024, dt=BF16, addt=None):
            stg = Rot([A.alloc("ass%d" % i, [width], dt) for i in range(2)])
            cur = {}

            def evac(i, b, tok0, col0, ncol, n0, gc, ps, pst):
                if n0 == 0:
                    cur["s"] = stg.next()
                sv, stl = cur["s"]
                if addt is not None:
                    av, at = addt
                    P.op("dve", lambda e: e.tensor_tensor(out=sv[:, n0:n0 + ncol], in0=ps, in1=av[:, n0:n0 + ncol], op=ALU.add),
                         reads=[pst, at], writes=[stl])
                else:
                    P.op("act", lambda e: e.activation(out=sv[:, n0:n0 + ncol], in_=ps, func=func), reads=[pst], writes=[stl])
                if n0 + ncol >= gc:
                    cs = col0 - n0
                    stq("sp", dst[b, tok0:tok0 + 128, cs:cs + gc], sv[:, :gc], stl, [d_dst[b]])
            return evac

        def out_phase(l, wout, blocks, xsrc, wkey=None):
            xb = [A.alloc("ox%d" % i, [8, 512], F32) for i in range(2)]

            def pre(i, g, b, t0, nt):
                xv, xt = xb[i % 2]
                ld("sp", xv[:, :, :nt], xsrc[b, g * 8:(g + 1) * 8, :, t0:t0 + nt].rearrange("i p t -> p i t"), xt, reads=[d_x[b]])

            def evac(i, b, t0, nt, jn, j, nj, ps, pst):
                xv, xt = xb[i % 2]
                c = 2 if t0 >= TL else b
                P.op("dve", lambda e: e.scalar_tensor_tensor(out=xv[:, j, :nt], in0=ps, scalar=mod[:, l, 32 + jn, c:c + 1],
                                                            in1=xv[:, j, :nt], op0=ALU.mult, op1=ALU.add),
                     reads=[pst, xt, t_mod], writes=[xt])
                if j == nj - 1:
                    g0 = jn - j
                    stq("sp", xr[b, g0:g0 + 8, :, t0:t0 + nt].rearrange("i p t -> p i t"), xv[:, :, :nt], xt, [d_x[b]])
            linear(uT, d_u, wout, 2048, "ws", blocks, evac, Rot(banks[0:4]), pre=pre, wkey=wkey)
            phase_end()

        def fnet_layer(l, j, blocks, with_ctx, xsrc):
            linear(hT, d_h, fwg[j], 2048, "ws", blocks, ws_simple(sgT, d_sg, AF.Silu), Rot(banks[0:4]), wkey="fwg%d" % j, norm=(l, xsrc))
            phase_end()
            CDs, t_CD = A.alloc("CD", [4, 512], BF16)
            SDs, t_SD = A.alloc("SD", [4, 512], BF16)
            ld("sp", CDs, CN["CD"], t_CD)
            ld("sp", SDs, CN["SD"], t_SD)
            hb = [A.alloc("fh%d" % i, [16, 512], BF16) for i in range(2)]
            stP = Rot([A.alloc("fsp%d" % i, [2048], BF16) for i in range(2)])
            stQ = Rot([A.alloc("fsq%d" % i, [2048], BF16) for i in range(2)])
            psb = Rot(banks[0:6])

            def load(i):
                b, t0, nt = blocks[i]
                hv, ht = hb[i % 2]
                ld("sp", hv[:, :, :nt], hT[b, :, :, t0:t0 + nt].rearrange("i p t -> p i t"), ht, reads=[d_h[b]])
            load(0)
            for i, (b, t0, nt) in enumerate(blocks):
                if i + 1 < len(blocks):
                    load(i + 1)
                hv, ht = hb[i % 2]
                nrm = (1.0 / 1024.0) if t0 < TL else 1.0 / math.sqrt(256.0 * 512.0)
                for tq in range(nt // 128):
                    pv, pt = stP.next()
                    qv, qt = stQ.next()
                    for g in range(4):
                        psP, tP = psb.next()
                        P.op("pe", mm16(psP, lambda k, g=g, tq=tq, hv=hv: hv[:, 4 * g + k, tq * 128:(tq + 1) * 128],
                                        lambda k: CDs[:, k, :], nk=4), reads=[ht, t_CD], writes=[tP])
                        P.op("act", lambda e, g=g, psP=psP, pv=pv, nrm=nrm: e.activation(out=pv[:, g * 512:(g + 1) * 512], in_=psP, func=AF.Identity, scale=nrm),
                             reads=[tP], writes=[pt])
                        psQ, tQ = psb.next()
                        P.op("pe", mm16(psQ, lambda k, g=g, tq=tq, hv=hv: hv[:, 4 * g + k, tq * 128:(tq + 1) * 128],
                                        lambda k: SDs[:, k, :], nk=4), reads=[ht, t_SD], writes=[tQ])
                        P.op("dve", lambda e, g=g, psQ=psQ, qv=qv, nrm=nrm: e.tensor_scalar(out=qv[:, g * 512:(g + 1) * 512], in0=psQ, scalar1=nrm,
                                                                                 scalar2=None, op0=ALU.mult), reads=[tQ], writes=[qt])
                    tok = t0 + tq * 128
                    stq("sp", Pd[b, :, tok:tok + 128, :].rearrange("j t e -> t j e"), pv.rearrange("p (j e) -> p j e", j=16), pt, [d_P[b]])
                    stq("sp", Qd[b, :, tok:tok + 128, :].rearrange("j t e -> t j e"), qv.rearrange("p (j e) -> p j e", j=16), qt, [d_Q[b]])
            phase_end()
            prefetch(("fwo%d" % j, 0), fwo[j][:, 0:1024], 1024)
            CTs, t_CT = A.alloc("CT", [2, 8, 1024], BF16)
            STs, t_ST = A.alloc("ST", [2, 8, 1024], BF16)
            ld("sp", CTs, CN["CT"], t_CT)
            ld("sp", STs, CN["STn"], t_ST)
            Pe = [A.alloc("Pe%d" % i, [2, 8, 128], BF16) for i in range(2)]
            Qe = [A.alloc("Qe%d" % i, [2, 8, 128], BF16) for i in range(2)]
            sgs = [A.alloc("sgs%d" % i, [2048], BF16) for i in range(2)]
            ust = [A.alloc("ust%d" % i, [2048], BF16) for i in range(2)]
            tB, t_tB = A.alloc("tB", [512], F32)
            y1, t_y1 = A.alloc("y1", [512], F32)
            y2, t_y2 = A.alloc("y2", [512], F32)
            psA = Rot(banks[0:2])
            psBk = Rot(banks[2:4])
            nb_list = sorted(set(b for (b, _, _) in blocks))
            items = [(b, jt) for b in nb_list for jt in range(16)]

            def load2(i):
                b, jt = items[i]
                pv, pt = Pe[i % 2]
                qv, qt = Qe[i % 2]
                sv, stl = sgs[i % 2]
                for r in range(2):
                    ld("sp", pv[:, r, :, :], Pd[b, jt, 0:TL, :].rearrange("(c p r) e -> r p c e", p=128, r=2)[r], pt, reads=[d_P[b]])
                    ld("sp", qv[:, r, :, :], Qd[b, jt, 0:TL, :].rearrange("(c p r) e -> r p c e", p=128, r=2)[r], qt, reads=[d_Q[b]])
                ld("sp", sv, sgT[b, jt, :, 0:TL], stl, reads=[d_sg[b]])
            load2(0)
            for i, (b, jt) in enumerate(items):
                if i + 1 < len(items):
                    load2(i + 1)
                pv, pt = Pe[i % 2]
                qv, qt = Qe[i % 2]
                sv, stl = sgs[i % 2]
                uv, ut = ust[i % 2]
                for jb in range(2):
                    cs = slice(jb * 512, (jb + 1) * 512)
                    pa, ta = psA.next()
                    pb, tb = psBk.next()
                    for r, (pp, tp) in enumerate(((pa, ta), (pb, tb))):
                        def f(e, r=r, pp=pp, pv=pv, qv=qv, cs=cs):
                            for c in range(8):
                                e.matmul(pp, lhsT=pv[:, r, c, :], rhs=CTs[:, r, c, cs], start=(c == 0), stop=False)
                                ins = e.matmul(pp, lhsT=qv[:, r, c, :], rhs=STs[:, r, c, cs], start=False, stop=(c == 7))
                            return ins
                        P.op("pe", f, reads=[pt, qt, t_CT, t_ST], writes=[tp])
                    P.op("act", lambda e, pb=pb: e.activation(out=tB, in_=pb, func=AF.Identity), reads=[tb], writes=[t_tB])
                    P.op("dve", lambda e, pa=pa: e.tensor_tensor(out=y1, in0=pa, in1=tB, op=ALU.add), reads=[ta, t_tB], writes=[t_y1])
                    P.op("dve", lambda e, pa=pa: e.tensor_tensor(out=y2, in0=pa, in1=tB, op=ALU.subtract), reads=[ta, t_tB], writes=[t_y2])
                    P.op("pool", lambda e, uv=uv, sv=sv, cs=cs: e.tensor_tensor(out=uv[:, cs], in0=y1, in1=sv[:, cs], op=ALU.mult),
                         reads=[t_y1, stl], writes=[ut])
                    c2 = slice(1024 + jb * 512, 1024 + (jb + 1) * 512)
                    P.op("dve", lambda e, uv=uv, sv=sv, c2=c2: e.tensor_tensor(out=uv[:, c2], in0=y2, in1=sv[:, c2], op=ALU.mult),
                         reads=[t_y2, stl], writes=[ut])
                stq("sp", uT[b, jt, :, 0:TL], uv, ut, [d_u[b]])
            if with_ctx:
                C2, t_C2 = A.alloc("C2", [2, 256], BF16)
                S2, t_S2 = A.alloc("S2", [2, 256], BF16)
                ld("sp", C2, CN["C256"], t_C2)
                ld("sp", S2, CN["S256n"], t_S2)
                Pc, t_Pc = A.alloc("Pc", [8, 2, 128], BF16)
                Qc, t_Qc = A.alloc("Qc", [8, 2, 128], BF16)
                sgc, t_sgc = A.alloc("sgc", [8, 256], BF16)
                uc, t_uc = A.alloc("uc", [8, 256], BF16)
                for b in nb_list:
                  for hf_ in range(2):
                    js = slice(hf_ * 8, hf_ * 8 + 8)
                    for c in range(2):
                        ld("sp", Pc[:, :, c, :], Pd[b, js, TL + c * 128:TL + (c + 1) * 128, :].rearrange("j p e -> p j e"), t_Pc, reads=[d_P[b]])
                        ld("sp", Qc[:, :, c, :], Qd[b, js, TL + c * 128:TL + (c + 1) * 128, :].rearrange("j p e -> p j e"), t_Qc, reads=[d_Q[b]])
                    ld("sp", sgc, sgT[b, js, :, TL:TA].rearrange("j p t -> p j t"), t_sgc, reads=[d_sg[b]])
                    for jt in range(8):
                        pa, ta = psA.next()

                        def f(e, pa=pa, jt=jt):
                            for c in range(2):
                                e.matmul(pa[:, 0:256], lhsT=Pc[:, jt, c, :], rhs=C2[:, c, :], start=(c == 0), stop=False)
                                ins = e.matmul(pa[:, 0:256], lhsT=Qc[:, jt, c, :], rhs=S2[:, c, :], start=False, stop=(c == 1))
                            return ins
                        P.op("pe", f, reads=[t_Pc, t_Qc, t_C2, t_S2], writes=[ta])
                        P.op("dve", lambda e, pa=pa, jt=jt: e.tensor_tensor(out=uc[:, jt, :], in0=pa[:, 0:256], in1=sgc[:, jt, :], op=ALU.mult),
                             reads=[ta, t_sgc], writes=[t_uc])
                    stq("sp", uT[b, js, :, TL:TA].rearrange("j p t -> p j t"), uc, t_uc, [d_u[b]])
            phase_end()
            out_phase(l, fwo[j], blocks, xsrc, wkey="fwo%d" % j)

        def attn_layer(l, blocks, xsrc):
            norm_phase(l, blocks, xsrc, pf=(("awq", 0), awi[:, 0:1024], 1024))
            rc, t_rc = A.alloc("rc", [2048], F32)
            rsn, t_rsn = A.alloc("rsn", [2048], F32)
            ld("sp", rc, CN["ropec"], t_rc)
            ld("sp", rsn, CN["ropes"], t_rsn)
            stg = Rot([A.alloc("aqs%d" % i, [8, 512], BF16) for i in range(2)])
            sqh = Rot([A.alloc("asq%d" % i, [512], BF16) for i in range(2)])
            rsv = Rot([A.alloc("ars%d" % i, [512], F32) for i in range(2)])
            qnb = Rot([A.alloc("aqn%d" % i, [512], BF16) for i in range(2)])
            t1v = Rot([A.alloc("at1%d" % i, [512], F32) for i in range(2)])
            t2v = Rot([A.alloc("at2%d" % i, [512], F32) for i in range(2)])
            ps2 = Rot(banks[4:6])
            ps3 = Rot(banks[6:8])
            cur = {}

            def evac(i, b, t0, nt, jn, j, nj, ps, pst):
                if j == 0:
                    cur["s"] = stg.next()
                sv, stl = cur["s"]
                gcol = qks[:, 0:1] if jn < 16 else qks[:, 1:2]
                sqv, sqt = sqh.next()
                P.op("act", lambda e: e.activation(out=sqv[:, :nt], in_=ps, func=AF.Square), reads=[pst], writes=[sqt])
                p2, tp2 = ps2.next()
                P.op("pe", lambda e: e.matmul(p2[:, :nt], lhsT=onesb, rhs=sqv[:, :nt], start=True, stop=True), reads=[sqt, t_onesb], writes=[tp2])
                rv, rt = rsv.next()
                P.op("act", lambda e: e.activation(out=rv[:, :nt], in_=p2[:, :nt], func=AF.Ln, scale=1.0 / 128, bias=epsc),
                     reads=[tp2, t_eps], writes=[rt])
                P.op("act", lambda e: e.activation(out=rv[:, :nt], in_=rv[:, :nt], func=AF.Exp, scale=-0.5), reads=[rt], writes=[rt])
                if t0 >= TL:
                    P.op("dve", lambda e: e.scalar_tensor_tensor(out=sv[:, j, :nt], in0=ps, scalar=gcol, in1=rv[:, :nt], op0=ALU.mult, op1=ALU.mult),
                         reads=[pst, rt, t_qks], writes=[stl])
                else:
                    qv, qt = qnb.next()
                    P.op("dve", lambda e: e.scalar_tensor_tensor(out=qv[:, :nt], in0=ps, scalar=gcol, in1=rv[:, :nt], op0=ALU.mult, op1=ALU.mult),
                         reads=[pst, rt, t_qks], writes=[qt])
                    p3, tp3 = ps3.next()
                    P.op("pe", lambda e: e.matmul(p3[:, :nt], lhsT=Rm, rhs=qv[:, :nt], start=True, stop=True), reads=[qt, t_m128], writes=[tp3])
                    a1, ta1 = t1v.next()
                    a2, ta2 = t2v.next()
                    P.op("pool", lambda e: e.tensor_tensor(out=a1[:, :nt], in0=qv[:, :nt], in1=rc[:, t0:t0 + nt], op=ALU.mult),
                         reads=[qt, t_rc], writes=[ta1])
                    P.op("dve", lambda e: e.tensor_tensor(out=a2[:, :nt], in0=p3[:, :nt], in1=rsn[:, t0:t0 + nt], op=ALU.mult),
                         reads=[tp3, t_rsn], writes=[ta2])
                    P.op("pool", lambda e: e.tensor_tensor(out=sv[:, j, :nt], in0=a1[:, :nt], in1=a2[:, :nt], op=ALU.add),
                         reads=[ta1, ta2], writes=[stl])
                if j == nj - 1:
                    h0 = jn - j
                    if h0 < 16:
                        stq("sp", qTd[b, h0:h0 + nj, :, t0:t0 + nt].rearrange("j p t -> p j t"), sv[:, :nj, :nt], stl, [d_q[b]])
                    else:
                        stq("sp", kTd[b, h0 - 16:h0 - 16 + nj, :, t0:t0 + nt].rearrange("j p t -> p j t"), sv[:, :nj, :nt], stl, [d_k[b]])
            linear(hT, d_h, awi[:, 0:2560], 2560, "ws", blocks, evac, Rot(banks[0:4]), wkey="awq")
            phase_end()
            linear(hT, d_h, awi[:, 3072:5120], 2048, "ws", blocks, ws_simple(sgT, d_sg, AF.Silu), Rot(banks[0:4]))
            phase_end()
            linear(hT, d_h, awi[:, 2560:3072], 512, "as", blocks, as_simple(vvd, d_v, AF.Identity, 0, width=512), Rot(banks[0:4]))
            phase_end()
            prefetch(("awo", 0), awo[:, 0:1024], 1024)
            kTs = [A.alloc("kTs%d" % i, [TA], BF16) for i in range(2)]
            vs = [A.alloc("vs%d" % i, [18, 128], BF16) for i in range(2)]
            qs = Rot([A.alloc("qs%d" % i, [512], BF16) for i in range(2)])
            szs = Rot([A.alloc("szs%d" % i, [512], BF16) for i in range(2)])
            pTs = Rot([A.alloc("pT%d" % i, [512], BF16) for i in range(4)])
            rden, t_rden = A.alloc("rden", [512], F32)
            ot, t_ot = A.alloc("ot", [512], F32)
            usts = Rot([A.alloc("aus%d" % i, [512], BF16) for i in range(2)])
            psS = Rot(banks[0:3])
            psO = Rot(banks[3:5])
            psD = Rot(banks[5:7])
            sc = 128.0 ** -0.5

            def attn_block(b, h, t0, nt, kv_, kt_, vv_, vt_):
                chunks = list(range(18)) if t0 < TL else [16, 17]
                qv, qt = qs.next()
                zv, zt = szs.next()
                ld("sp", qv[:, :nt], qTd[b, h, :, t0:t0 + nt], qt, reads=[d_q[b]])
                ld("sp", zv[:, :nt], sgT[b, h, :, t0:t0 + nt], zt, reads=[d_sg[b]])
                po, tpo = psO.next()
                pd_, tpd = psD.next()
                n = len(chunks)
                pend = []

                def s_step(c):
                    pss, tps = psS.next()
                    pv, ptl = pTs.next()
                    P.op("pe", lambda e: e.matmul(pss[:, :nt], lhsT=kv_[:, c * 128:(c + 1) * 128], rhs=qv[:, :nt],
                                                  start=True, stop=True), reads=[kt_, qt], writes=[tps])
                    P.op("act", lambda e: e.activation(out=pv[:, :nt], in_=pss[:, :nt], func=AF.Exp, scale=sc),
                         reads=[tps], writes=[ptl])
                    pend.append((c, pv, ptl))

                def pv_step(c, pv, ptl, first, last):
                    def f(e):
                        e.matmul(po[:, :nt], lhsT=vv_[:, c, :], rhs=pv[:, :nt], start=first, stop=last)
                        return e.matmul(pd_[:, :nt], lhsT=onesb, rhs=pv[:, :nt], start=first, stop=last)
                    P.op("pe", f, reads=[vt_, ptl, t_onesb], writes=[tpo, tpd])
                for idx in range(n + 2):
                    if idx < n:
                        s_step(chunks[idx])
                    if idx >= 2:
                        c, pv, ptl = pend[idx - 2]
                        pv_step(c, pv, ptl, idx == 2, idx == n + 1)
                uv, ut = usts.next()
                P.op("dve", lambda e: e.reciprocal(out=rden[:, :nt], in_=pd_[:, :nt]), reads=[tpd], writes=[t_rden])
                P.op("dve", lambda e: e.tensor_tensor(out=ot[:, :nt], in0=po[:, :nt], in1=rden[:, :nt], op=ALU.mult),
                     reads=[tpo, t_rden], writes=[t_ot])
                P.op("dve", lambda e: e.tensor_tensor(out=uv[:, :nt], in0=ot[:, :nt], in1=zv[:, :nt], op=ALU.mult),
                     reads=[t_ot, zt], writes=[ut])
                stq("pool", uT[b, h, :, t0:t0 + nt], uv[:, :nt], ut, [d_u[b]])

            kvi = 0
            for b in sorted(set(b for (b, _, _) in blocks)):
                for kv in range(4):
                    kv_, kt_ = kTs[kvi % 2]
                    vv_, vt_ = vs[kvi % 2]
                    kvi += 1
                    ld("sp", kv_, kTd[b, kv, :, :], kt_, reads=[d_k[b]])
                    ld("sp", vv_, vvd[b, :, kv * 128:(kv + 1) * 128].rearrange("(c p) e -> p c e", p=128), vt_, reads=[d_v[b]])
                    for hh in range(4):
                        h = kv * 4 + hh
                        for (bb, t0, nt) in blocks:
                            if bb != b:
                                continue
                            attn_block(b, h, t0, nt, kv_, kt_, vv_, vt_)
            phase_end()
            out_phase(l, awo, blocks, xsrc, wkey="awo")

        def mlstm_layer(l, blocks, xsrc):
            evq = ws_simple(mqT, d_mq, AF.Identity, scale=128.0 ** -0.5)
            linear(hT, d_h, mwi[:, 0:1024], 1024, "ws", blocks, evq, Rot(banks[0:4]), wkey="mq", norm=(l, xsrc))
            phase_end()
            linear(hT, d_h, mwi[:, 1024:2048], 1024, "ws", blocks, ws_simple(mkT, d_mkT, AF.Identity), Rot(banks[0:4]))
            phase_end()
            linear(hT, d_h, mwi[:, 1024:2048], 1024, "as", blocks, as_simple(mkd, d_mk, AF.Identity, 1024), Rot(banks[0:4]))
            phase_end()
            linear(hT, d_h, mwi[:, 2048:4096], 2048, "as", blocks, as_simple(mvd, d_mv, AF.Identity, 2048), Rot(banks[0:4]))
            phase_end()
            linear(hT, d_h, mwi[:, 4096:6144], 2048, "as", blocks, as_simple(mso, d_mso, AF.Sigmoid, 4096), Rot(banks[0:4]))
            phase_end()
            bgb, t_bgb = A.alloc("bgb", [32], F32)
            ld("sp", bgb, mbg[0:1, :].to_broadcast([128, 32]), t_bgb)
            linear(hT, d_h, mwi[:, 6144:6176], 32, "as", blocks, as_simple(mgd, d_mg, None, 6144, width=32, dt=F32, addt=(bgb, t_bgb)),
                   Rot(banks[0:4]), gsz=32)
            phase_end()
            linear(hT, d_h, mwi[:, 6176:8224], 2048, "as", blocks, as_simple(msz, d_msz, AF.Silu, 6176), Rot(banks[0:4]))
            phase_end()
            for b in sorted(set(b for (b, _, _) in blocks)):
                prefetch(("mwo", 0), mwo[:, 0:1024], 1024)
                mlstm_scan(b)
                mlstm_finish(b)
            out_phase(l, mwo, blocks, xsrc, wkey="mwo")

        def mlstm_scan(b):
            G, t_G = A.alloc("G", [18, 32], F32)
            ld("sp", G, mgd[b].rearrange("(c p) n -> p c n", p=128), t_G, reads=[d_mg[b]])
            E, t_E = A.alloc("E", [2, 144], F32)
            L, t_L = A.alloc("L", [2, 144], F32)
            Wt, t_W = A.alloc("Wt", [2, 18, 8], F32)
            RI, t_RI = A.alloc("RI", [2, 18, 8], F32)
            DC, t_DC = A.alloc("DC", [2, 18, 8], F32)
            pS, t_pS = banks[0]
            pT_, t_pT = banks[1]
            for d in range(2):
                P.op("act", lambda e, d=d: e.activation(out=E[:, d, :].rearrange("p (c h) -> p c h", c=18), in_=G[:, :, 16 * d + 8:16 * d + 16], func=AF.Exp, scale=-1.0),
                     reads=[t_G], writes=[t_E])
            P.op("act", lambda e: e.activation(out=L, in_=E, func=AF.Ln, bias=onec), reads=[t_E, t_onec], writes=[t_L])
            for d in range(2):
                P.op("pe", lambda e, d=d: e.matmul(pS[:, d * 144:(d + 1) * 144], lhsT=trid[d], rhs=L[:, d, :], start=True, stop=True),
                     reads=[t_L, t_f128], writes=[t_pS])
                P.op("pe", lambda e, d=d: e.matmul(pT_[:, d * 144:(d + 1) * 144], lhsT=onesf, rhs=L[:, d, :], start=True, stop=True),
                     reads=[t_L, t_f128], writes=[t_pT])
            for d in range(2):
                P.op("dve", lambda e, d=d: e.tensor_tensor(out=Wt[:, d, :, :], in0=G[:, :, 16 * d:16 * d + 8],
                                                          in1=pS[:, d * 144:(d + 1) * 144].rearrange("p (c h) -> p c h", c=18), op=ALU.subtract),
                     reads=[t_G, t_pS], writes=[t_W])
            P.op("act", lambda e: e.activation(out=Wt, in_=Wt, func=AF.Exp), reads=[t_W], writes=[t_W])
            P.op("act", lambda e: e.activation(out=RI, in_=pS[:, 0:288].rearrange("p (d c h) -> p d c h", d=2, c=18), func=AF.Exp, scale=-1.0),
                 reads=[t_pS], writes=[t_RI])
            P.op("act", lambda e: e.activation(out=DC, in_=pT_[:, 0:288].rearrange("p (d c h) -> p d c h", d=2, c=18), func=AF.Exp, scale=-1.0),
                 reads=[t_pT], writes=[t_DC])
            Cst = [[A.alloc("Cs%d%d" % (d, h), [257], F32) for h in range(8)] for d in range(2)]
            Cb = [[A.alloc("Cb%d%d" % (d, h), [258], BF16) for h in range(8)] for d in range(2)]
            qc = [[A.alloc("qc%d%d" % (d, i), [8, 128], BF16) for i in range(2)] for d in range(2)]
            kc = [[A.alloc("kc%d%d" % (d, i), [8, 128], BF16) for i in range(2)] for d in range(2)]
            kt = [[A.alloc("kt%d%d" % (d, i), [8, 128], BF16) for i in range(2)] for d in range(2)]
            ve = [[A.alloc("ve%d%d" % (d, i), [8, 258], BF16) for i in range(2)] for d in range(2)]
            kw = [A.alloc("kw%d" % d, [8, 128], BF16) for d in range(2)]
            stt = Rot([A.alloc("stt%d" % i, [128], BF16) for i in range(16)])
            fc = Rot([A.alloc("fc%d" % i, [8], F32) for i in range(4)])
            hst = [[A.alloc("hst%d%d" % (d, i), [8, 257], F32) for i in range(1)] for d in range(2)]
            hout = [A.alloc("hout%d" % d, [8, 256], F32) for d in range(2)]
            for d in range(2):
                for i in range(2):
                    vv_, vt_ = ve[d][i]
                    P.op("pool", lambda e, vv_=vv_: e.memset(vv_[:, :, 256:257], 1.0), writes=[vt_])
            order = [[16, 17] + list(range(16)), [17, 16] + list(range(15, -1, -1))]
            psSb = Rot([(banks[2][0][:, i * 128:(i + 1) * 128], None) for i in range(4)])
            psSt = banks[2][1]
            psS2 = Rot([(banks[3][0][:, i * 128:(i + 1) * 128], None) for i in range(4)])
            psS2t = banks[3][1]
            psAr = Rot(banks[4:6])
            psCr = Rot(banks[6:8])
            psbank = [[banks[2], banks[3]], [banks[0], banks[1]]]

            def loads(d, si):
                c = order[d][si]
                tk = slice(c * 128, (c + 1) * 128)
                i = si % 2
                ld("sp", qc[d][i][0], mqT[b, :, :, tk].rearrange("h p t -> p h t"), qc[d][i][1], reads=[d_mq[b]])
                ld("sp", kc[d][i][0], mkT[b, :, :, tk].rearrange("h p t -> p h t"), kc[d][i][1], reads=[d_mkT[b]])
                ld("sp", kt[d][i][0], mkd[b, tk, :].rearrange("t (h e) -> t h e", h=8), kt[d][i][1], reads=[d_mk[b]])
                ld("sp", ve[d][i][0][:, :, 0:256], mvd[b, tk, :].rearrange("t (h e) -> t h e", h=8), ve[d][i][1], reads=[d_mv[b]])
            for d in range(2):
                loads(d, 0)
            for si in range(18):
                for d in range(2):
                    if si + 1 < 18:
                        loads(d, si + 1)
                    c = order[d][si]
                    i = si % 2
                    first = si == 0
                    last = si == 17
                    cn = order[d][si + 1] if not last else None
                    qv, qt = qc[d][i]
                    kv_, kt_ = kc[d][i]
                    ktv, ktt = kt[d][i]
                    vv_, vt_ = ve[d][i]
                    kwv, kwt = kw[d]
                    hv, ht = hst[d][0]
                    P.op("dve", lambda e, d=d, c=c, kwv=kwv, ktv=ktv: e.tensor_tensor(
                        out=kwv, in0=ktv, in1=Wt[:, d, c, :].unsqueeze(2).to_broadcast([128, 8, 128]), op=ALU.mult),
                        reads=[ktt, t_W], writes=[kwt])
                    slots = []
                    for h in range(8):
                        pss = psbank[d][h // 4][0][:, (h % 4) * 128:(h % 4 + 1) * 128]
                        tps = psbank[d][h // 4][1]
                        slots.append((pss, tps))
                        P.op("pe", lambda e, pss=pss, kv_=kv_, qv=qv, h=h: e.matmul(pss, lhsT=kv_[:, h, :], rhs=qv[:, h, :], start=True, stop=True),
                             reads=[kt_, qt], writes=[tps])
                    svs = []
                    for h in range(8):
                        pss, tps = slots[h]
                        pc, tpc = psCr.next()
                        P.op("pe", lambda e, pc=pc, kwv=kwv, vv_=vv_, h=h: e.matmul(pc[:, 0:257], lhsT=kwv[:, h, :], rhs=vv_[:, h, 0:257], start=True, stop=True),
                             reads=[kwt, vt_], writes=[tpc])
                        sv, stl = stt.next()
                        svs.append((sv, stl))
                        P.op("dve", lambda e, sv=sv, pss=pss, d=d, c=c, h=h: e.scalar_tensor_tensor(
                            out=sv, in0=pss, scalar=Wt[:, d, c, h:h + 1], in1=maskd[d], op0=ALU.mult, op1=ALU.mult),
                            reads=[tps, t_W, t_m128], writes=[stl])
                        csv, cst = Cst[d][h]
                        if first:
                            P.op("dve", lambda e, csv=csv, pc=pc: e.tensor_copy(out=csv, in_=pc[:, 0:257]), reads=[tpc], writes=[cst])
                        else:
                            P.op("dve", lambda e, csv=csv, pc=pc, d=d, c=c, h=h: e.scalar_tensor_tensor(
                                out=csv, in0=csv, scalar=DC[:, d, c, h:h + 1], in1=pc[:, 0:257], op0=ALU.mult, op1=ALU.add),
                                reads=[tpc, cst, t_DC], writes=[cst])
                    for h in range(8):
                        sv, stl = svs[h]
                        pa, tpa = psAr.next()
                        cbv, cbt = Cb[d][h]
                        csv, cst = Cst[d][h]

                        def f(e, pa=pa, qv=qv, h=h, cbv=cbv, sv=sv, vv_=vv_, first=first):
                            if not first:
                                e.matmul(pa[:, 0:257], lhsT=qv[:, h, :], rhs=cbv[:, 0:257], start=True, stop=False)
                            return e.matmul(pa[:, 0:257], lhsT=sv, rhs=vv_[:, h, 0:257], start=first, stop=True)
                        P.op("pe", f, reads=[qt, cbt, stl, vt_], writes=[tpa])
                        P.op("act", lambda e, hv=hv, pa=pa, h=h: e.activation(out=hv[:, h, :], in_=pa[:, 0:257], func=AF.Identity),
                             reads=[tpa], writes=[ht])
                        if not last:
                            P.op("act", lambda e, cbv=cbv, csv=csv, d=d, cn=cn, h=h: e.activation(
                                out=cbv[:, 0:257], in_=csv, func=AF.Identity, scale=DC[:, d, cn, h:h + 1]), reads=[cst, t_DC], writes=[cbt])
                    fv, ft = fc.next()
                    f2, ft2 = fc.next()
                    P.op("dve", lambda e, fv=fv, hv=hv: e.tensor_scalar(out=fv, in0=hv[:, :, 256], scalar1=-1.0, scalar2=None, op0=ALU.mult),
                         reads=[ht], writes=[ft])
                    P.op("dve", lambda e, fv=fv, hv=hv: e.tensor_tensor(out=fv, in0=fv, in1=hv[:, :, 256], op=ALU.max), reads=[ht, ft], writes=[ft])
                    P.op("dve", lambda e, fv=fv, d=d, c=c: e.tensor_tensor(out=fv, in0=fv, in1=RI[:, d, c, :], op=ALU.max), reads=[ft, t_RI], writes=[ft])
                    P.op("dve", lambda e, fv=fv, f2=f2: e.reciprocal(out=f2, in_=fv), reads=[ft], writes=[ft2])
                    ho, hot = hout[d]
                    P.op("dve", lambda e, ho=ho, hv=hv, f2=f2: e.tensor_tensor(out=ho, in0=hv[:, :, 0:256],
                                                                              in1=f2.unsqueeze(2).to_broadcast([128, 8, 256]), op=ALU.mult),
                         reads=[ht, ft2], writes=[hot])
                    stq("sp", mhd[b, d, c * 128:(c + 1) * 128, :], ho.rearrange("p h e -> p (h e)"), hot, [d_mh[b]])
            phase_end()

        def mlstm_finish(b):
            hnb, t_hnb = A.alloc("hnb", [2048], F32)
            ld("sp", hnb, mhn[0:1, :].to_broadcast([128, 2048]), t_hnb)
            hf = [A.alloc("hf%d" % i, [2048], F32) for i in range(2)]
            hbk = [A.alloc("hbk%d" % i, [2048], F32) for i in range(2)]
            so = [A.alloc("so%d" % i, [2048], BF16) for i in range(2)]
            sz = [A.alloc("sz%d" % i, [2048], BF16) for i in range(2)]
            y, t_y = A.alloc("fy", [8, 256], F32)
            sq, t_sq = A.alloc("fsq", [8, 256], F32)
            ss, t_ss = A.alloc("fss", [8], F32)
            ub, t_ub = A.alloc("fub", [2048], BF16)
            uts = [A.alloc("uts%d" % i, [16, 128], BF16) for i in range(2)]
            psT = [(banks[0][0].bitcast(BF16), banks[0][1]), (banks[1][0].bitcast(BF16), banks[1][1])]

            def loads(c):
                tk = slice(c * 128, (c + 1) * 128)
                i = c % 2
                ld("sp", hf[i][0], mhd[b, 0, tk, :], hf[i][1], reads=[d_mh[b]])
                ld("sp", hbk[i][0], mhd[b, 1, tk, :], hbk[i][1], reads=[d_mh[b]])
                ld("sp", so[i][0], mso[b, tk, :], so[i][1], reads=[d_mso[b]])
                ld("sp", sz[i][0], msz[b, tk, :], sz[i][1], reads=[d_msz[b]])
            loads(0)
            for c in range(18):
                if c + 1 < 18:
                    loads(c + 1)
                i = c % 2
                hfv, hft = hf[i]
                hbv, hbt = hbk[i]
                sov, sot = so[i]
                szv, szt = sz[i]
                yf = y.rearrange("p h e -> p (h e)")
                P.op("pool", lambda e, hfv=hfv, hbv=hbv: e.tensor_tensor(out=hfv, in0=hfv, in1=hbv, op=ALU.add), reads=[hbt, hft], writes=[hft])
                P.op("dve", lambda e, hfv=hfv, sov=sov: e.tensor_tensor(out=yf, in0=hfv, in1=sov, op=ALU.mult), reads=[hft, sot], writes=[t_y])
                P.op("pool", lambda e: e.tensor_tensor(out=sq, in0=y, in1=y, op=ALU.mult), reads=[t_y], writes=[t_sq])
                P.op("dve", lambda e: e.tensor_reduce(out=ss, in_=sq, axis=AX.X, op=ALU.add), reads=[t_sq], writes=[t_ss])
                P.op("act", lambda e: e.activation(out=ss, in_=ss, func=AF.Ln, scale=1.0 / 256, bias=epsc), reads=[t_ss, t_eps], writes=[t_ss])
                P.op("act", lambda e: e.activation(out=ss, in_=ss, func=AF.Exp, scale=-0.5), reads=[t_ss], writes=[t_ss])
                P.op("dve", lambda e: e.tensor_tensor(out=y, in0=y, in1=ss.unsqueeze(2).to_broadcast([128, 8, 256]), op=ALU.mult),
                     reads=[t_y, t_ss], writes=[t_y])
                P.op("pool", lambda e: e.tensor_tensor(out=yf, in0=yf, in1=hnb, op=ALU.mult), reads=[t_y, t_hnb], writes=[t_y])
                P.op("dve", lambda e, szv=szv: e.tensor_tensor(out=ub, in0=yf, in1=szv, op=ALU.mult), reads=[t_y, szt], writes=[t_ub])
                uv, ut = uts[i]
                for half in range(2):
                    pt_, tpt = psT[half]

                    def f(e, pt_=pt_, half=half):
                        for jj in range(8):
                            j = half * 8 + jj
                            ins = e.transpose(out=pt_[:, jj * 128:(jj + 1) * 128], in_=ub[:, j * 128:(j + 1) * 128], identity=ident)
                        return ins
                    P.op("pe", f, reads=[t_ub, t_m128], writes=[tpt])
                    P.op("act", lambda e, uv=uv, pt_=pt_, half=half: e.activation(
                        out=uv[:, half * 8:(half + 1) * 8, :], in_=pt_[:, 0:1024].rearrange("p (j t) -> p j t", j=8), func=AF.Identity),
                        reads=[tpt], writes=[ut])
                stq("sp", uT[b, :, :, c * 128:(c + 1) * 128].rearrange("j p t -> p j t"), uv, ut, [d_u[b]])
            phase_end()

        blocks_all = [(b, t0, 512) for b in range(NB) for t0 in range(0, TL, 512)] + [(b, TL, 256) for b in range(NB)]
        blocks_all.sort(key=lambda x: (x[0], x[1]))
        blocks_lat = [(b, t0, 512) for b in range(NB) for t0 in range(0, TL, 512)]
        mod_phase()
        xsrc = xT
        for l in range(nlayers):
            kind = l % 3
            if kind == 0:
                last_no_ctx = (l == 3)
                fnet_layer(l, l // 3, blocks_lat if last_no_ctx else blocks_all, not last_no_ctx, xsrc)
            elif kind == 1:
                mlstm_layer(l, blocks_all, xsrc)
            else:
                attn_layer(l, blocks_all, xsrc)
            xsrc = xr
        norm_phase(0, blocks_lat, xsrc, final=True)
        P.op("sp", lambda e: None, reads=[d_out])
        print("ops:", {e: len(P.ops[e]) for e in ENGS}, "dma slots:", len(P.slots))
        P.emit(st)
    return nc


_CACHE = {}


def prep_inputs(inp, core):
    b0 = 2 * core
    f32 = np.float32
    x = inp["x"]
    ctx = inp["ctx"]
    xt = np.empty((NB, D, TA), f32)
    for i in range(NB):
        xt[i, :, :TL] = x[b0 + i].T
        xt[i, :, TL:] = ctx[b0 + i].T
    m = {"xT": xt.reshape(NB, 16, 128, TA)}
    cc = np.stack([inp["c"][b0], inp["c"][b0 + 1], inp["c_ctx"]], axis=1).astype(f32)
    m["cT"] = np.ascontiguousarray(cc.reshape(16, 128, 3).transpose(1, 0, 2))
    return m


def shared_inputs(inp):
    f32 = np.float32
    m = {}
    m["ada_w"] = np.ascontiguousarray(inp["ada_w"], dtype=f32)
    m["ada_b"] = np.ascontiguousarray(inp["ada_b"], dtype=f32)
    g = np.concatenate([inp["norm_g"], inp["final_g"][None]], axis=0).astype(f32)
    m["gT"] = np.ascontiguousarray(g.reshape(5, 16, 128).transpose(2, 0, 1))
    m["fnet_w_gate"] = np.ascontiguousarray(inp["fnet_w_gate"], dtype=f32)
    m["fnet_w_out"] = np.ascontiguousarray(inp["fnet_w_out"], dtype=f32)
    m["mlstm_w_in"] = np.ascontiguousarray(inp["mlstm_w_in"][0], dtype=f32)
    m["mlstm_b_gate"] = np.ascontiguousarray(inp["mlstm_b_gate"], dtype=f32)
    m["mlstm_hn"] = np.ascontiguousarray(inp["mlstm_hn"], dtype=f32)
    m["mlstm_w_out"] = np.ascontiguousarray(inp["mlstm_w_out"][0], dtype=f32)
    m["attn_w_in"] = np.ascontiguousarray(inp["attn_w_in"][0], dtype=f32)
    m["attn_qk"] = np.ascontiguousarray(np.stack([inp["attn_qn"][0], inp["attn_kn"][0]], axis=1), dtype=f32)
    m["attn_w_out"] = np.ascontiguousarray(inp["attn_w_out"][0], dtype=f32)
    for k, v in make_consts().items():
        m["c_" + k] = v
    return m


def kernel(**inputs):
    inp = {k: np.asarray(v) for k, v in inputs.items()}
    ncores = 8
    if "nc" not in _CACHE:
        _CACHE["nc"] = build()
    nc = _CACHE["nc"]
    sh = shared_inputs(inp)
    in_maps = []
    for c in range(ncores):
        m = dict(sh)
        m.update(prep_inputs(inp, c))
        in_maps.append(m)
    res = run_bass_kernel_spmd(nc, in_maps, core_ids=list(range(ncores)))
    out = np.empty((16, TL, D), np.float32)
    for c in range(ncores):
        o = np.asarray(res.results[c]["outT"]).reshape(NB, D, TL)
        for i in range(NB):
            out[2 * c + i] = o[i].T
    return out
```
